# Optimizing a Trainium2 kernel written in Bass

```python
import math
import jax, jax.numpy as jnp
from jax import lax
import numpy as np

D_MODEL = 1024
BATCH = 4
SEQ = 4096
DEPTH = 1
DEC_BATCH = 16
DEC_SEQ = 32
PAST_LEN = 1024

CHUNK = 64
N_META = 16
D_MIX = D_MODEL
RET_HEADS = 4
RET_DK = 128
RET_DV = 128
MLA_HEADS = 4
MLA_NOPE = 128
MLA_ROPE = 64
MLA_V = 128
MLA_Q_LORA = 256
MLA_KV_LORA = 128
D_FF = 2816
ROPE_BASE = 10000.0
LN_EPS = 1e-5
RMS_EPS = 1e-6
Q_BLOCK = 128
DN_ALPHA = (2 * DEPTH) ** 0.25
DN_BETA = (8 * DEPTH) ** -0.25

OFF_KR = RET_HEADS * RET_DK
OFF_VR = 2 * RET_HEADS * RET_DK
OFF_GR = OFF_VR + RET_HEADS * RET_DV
OFF_CQ = OFF_GR + RET_HEADS * RET_DV
OFF_CKV = OFF_CQ + MLA_Q_LORA
OFF_KPE = OFF_CKV + MLA_KV_LORA
D_IN = OFF_KPE + MLA_ROPE

kernel_name = "hymba_retnet_mla_macaron_deepnorm_stream_step"


def layer_norm(x, g, b):
    xf = x.astype(jnp.float32)
    mu = xf.mean(-1, keepdims=True)
    var = jnp.square(xf - mu).mean(-1, keepdims=True)
    return ((xf - mu) * lax.rsqrt(var + LN_EPS) * g.astype(jnp.float32) + b.astype(jnp.float32)).astype(x.dtype)


def rms_norm(x, g):
    xf = x.astype(jnp.float32)
    return (xf * lax.rsqrt(jnp.square(xf).mean(-1, keepdims=True) + RMS_EPS) * g.astype(jnp.float32)).astype(x.dtype)


def rope(x, pos):
    d = x.shape[-1]
    inv = ROPE_BASE ** (-jnp.arange(0, d, 2, dtype=jnp.float32) / d)
    ang = pos.astype(jnp.float32)[:, None] * inv[None, :]
    cos = jnp.cos(ang)[:, None, :]
    sin = jnp.sin(ang)[:, None, :]
    xf = x.astype(jnp.float32)
    x1, x2 = xf[..., : d // 2], xf[..., d // 2:]
    return jnp.concatenate([x1 * cos - x2 * sin, x1 * sin + x2 * cos], -1).astype(x.dtype)


def swiglu(x, w_in, w_out):
    g, u = jnp.split(x @ w_in, 2, axis=-1)
    return (jax.nn.silu(g) * u) @ w_out


def ffn_block(x, w_in, w_out, g, b):
    return layer_norm(DN_ALPHA * x + 0.5 * swiglu(x, w_in, w_out), g, b)


def ret_log_gamma():
    return jnp.log(1.0 - jnp.exp2(-5.0 - jnp.arange(RET_HEADS, dtype=jnp.float32)))


def retention(q, k, v, s0, chunk):
    B, L, H, dk = q.shape
    dv = v.shape[-1]
    n = L // chunk
    f32 = jnp.float32
    lg = ret_log_gamma()
    qc = q.astype(f32).reshape(B, n, chunk, H, dk)
    kc = k.astype(f32).reshape(B, n, chunk, H, dk)
    vc = v.astype(f32).reshape(B, n, chunk, H, dv)
    idx = jnp.arange(chunk, dtype=f32)
    rel = idx[:, None] - idx[None, :]
    dmask = jnp.where(rel >= 0, jnp.exp(lg[:, None, None] * jnp.maximum(rel, 0.0)), 0.0)
    scores = jnp.einsum("bnqhd,bnkhd->bnhqk", qc, kc) * dmask[None, None]
    inner = jnp.einsum("bnhqk,bnkhe->bnqhe", scores, vc)
    kdec = jnp.exp(lg[None, :] * (chunk - 1.0 - idx)[:, None])
    kv = jnp.einsum("bnkhd,kh,bnkhe->bnhde", kc, kdec, vc)
    chunk_decay = jnp.exp(lg * chunk)[:, None, None]

    def step(s, kv_i):
        return chunk_decay * s + kv_i, s

    s_last, s_prev = lax.scan(step, s0.astype(f32), jnp.moveaxis(kv, 1, 0))
    qdec = jnp.exp(lg[None, :] * (idx + 1.0)[:, None])
    cross = jnp.einsum("bnqhd,nbhde,qh->bnqhe", qc, s_prev, qdec)
    return (inner + cross).reshape(B, L, H, dv), s_last


def retention_out(o, g_r, gn_g):
    B, L = o.shape[:2]
    mu = o.mean(-1, keepdims=True)
    var = jnp.square(o - mu).mean(-1, keepdims=True)
    on = ((o - mu) * lax.rsqrt(var + LN_EPS)).reshape(B, L, RET_HEADS * RET_DV) * gn_g.astype(jnp.float32)
    return (on * jax.nn.silu(g_r.astype(jnp.float32))).astype(g_r.dtype)


def mla_latent(ckv_raw, kpe_raw, pos, kv_norm_g):
    ckv = rms_norm(ckv_raw, kv_norm_g)
    kpe = rope(kpe_raw[:, :, None, :], pos)[:, :, 0, :]
    return ckv, kpe


def mla_query(cq, pos, q_norm_g, w_uq):
    B, L, _ = cq.shape
    q = (rms_norm(cq, q_norm_g) @ w_uq).reshape(B, L, MLA_HEADS, MLA_NOPE + MLA_ROPE)
    return jnp.concatenate([q[..., :MLA_NOPE], rope(q[..., MLA_NOPE:], pos)], -1)


def mla_keys(ckv, kpe, w_ukv):
    B, L, _ = ckv.shape
    kv = (ckv @ w_ukv).reshape(B, L, MLA_HEADS, MLA_NOPE + MLA_V)
    k = jnp.concatenate([kv[..., :MLA_NOPE], jnp.broadcast_to(kpe[:, :, None, :], (B, L, MLA_HEADS, MLA_ROPE))], -1)
    return k, kv[..., MLA_NOPE:]


def attend(q, k, v, mask):
    s = jnp.einsum("bqhd,bkhd->bhqk", q, k).astype(jnp.float32) * (MLA_NOPE + MLA_ROPE) ** -0.5
    if mask is not None:
        s = jnp.where(mask, s, -jnp.inf)
    p = jax.nn.softmax(s, axis=-1).astype(v.dtype)
    return jnp.einsum("bhqk,bkhe->bqhe", p, v)


def mixer_front(h, pos, w_mix_in, mla_q_norm_g, mla_w_uq, mla_kv_norm_g):
    B, L, _ = h.shape
    p = h @ w_mix_in
    q_r, k_r, v_r, g_r, cq, ckv, kpe = jnp.split(p, [OFF_KR, OFF_VR, OFF_GR, OFF_CQ, OFF_CKV, OFF_KPE], axis=-1)
    q_r = rope(q_r.reshape(B, L, RET_HEADS, RET_DK), pos)
    k_r = rope(k_r.reshape(B, L, RET_HEADS, RET_DK), pos) * RET_DK ** -0.5
    v_r = v_r.reshape(B, L, RET_HEADS, RET_DV)
    q_m = mla_query(cq, pos, mla_q_norm_g, mla_w_uq)
    ckv, kpe = mla_latent(ckv, kpe, pos, mla_kv_norm_g)
    return q_r, k_r, v_r, g_r, q_m, ckv, kpe


def trunk_tail(h, ret_y, mla_y, w_mix_out, ln2_g, ln2_b, ffn2_w_in, ffn2_w_out, ln3_g, ln3_b):
    mix = jnp.concatenate([ret_y, mla_y.astype(ret_y.dtype)], -1) @ w_mix_out
    h = layer_norm(DN_ALPHA * h + mix, ln2_g, ln2_b)
    return ffn_block(h, ffn2_w_in, ffn2_w_out, ln3_g, ln3_b)


def setup_inputs(seed: int = 0) -> dict:
    key = jax.random.key(seed)
    ks = jax.random.split(key, 23)
    nrm = lambda k, shape, s=1.0: jax.random.normal(k, shape, jnp.float32) * s
    gain = lambda k, n: 1.0 + 0.02 * jax.random.normal(k, (n,), jnp.float32)
    return {
        "x_prompt": nrm(ks[0], (BATCH, SEQ, D_MODEL)),
        "x_sample": nrm(ks[1], (DEC_BATCH, DEC_SEQ, D_MODEL)),
        "cache_mla_ckv": nrm(ks[2], (DEC_BATCH, PAST_LEN, MLA_KV_LORA)),
        "cache_mla_kpe": nrm(ks[3], (DEC_BATCH, PAST_LEN, MLA_ROPE)),
        "state_ret": nrm(ks[4], (DEC_BATCH, RET_HEADS, RET_DK, RET_DV), 0.3),
        "meta_tokens": nrm(ks[5], (N_META, D_MODEL)),
        "ffn1_w_in": nrm(ks[6], (D_MODEL, 2 * D_FF), D_MODEL ** -0.5),
        "ffn1_w_out": nrm(ks[7], (D_FF, D_MODEL), DN_BETA * D_FF ** -0.5),
        "ln1_g": gain(ks[8], D_MODEL),
        "ln1_b": nrm(ks[9], (D_MODEL,), 0.02),
        "w_mix_in": nrm(ks[10], (D_MODEL, D_IN), D_MODEL ** -0.5),
        "ret_gn_g": gain(ks[11], RET_HEADS * RET_DV),
        "mla_q_norm_g": gain(ks[12], MLA_Q_LORA),
        "mla_w_uq": nrm(ks[13], (MLA_Q_LORA, MLA_HEADS * (MLA_NOPE + MLA_ROPE)), MLA_Q_LORA ** -0.5),
        "mla_kv_norm_g": gain(ks[14], MLA_KV_LORA),
        "mla_w_ukv": nrm(ks[15], (MLA_KV_LORA, MLA_HEADS * (MLA_NOPE + MLA_V)), MLA_KV_LORA ** -0.5),
        "w_mix_out": nrm(ks[16], (D_MIX, D_MODEL), DN_BETA * D_MIX ** -0.5),
        "ln2_g": gain(ks[17], D_MODEL),
        "ln2_b": nrm(ks[18], (D_MODEL,), 0.02),
        "ffn2_w_in": nrm(ks[19], (D_MODEL, 2 * D_FF), D_MODEL ** -0.5),
        "ffn2_w_out": nrm(ks[20], (D_FF, D_MODEL), DN_BETA * D_FF ** -0.5),
        "ln3_g": gain(ks[21], D_MODEL),
        "ln3_b": nrm(ks[22], (D_MODEL,), 0.02),
    }


def reference(x_prompt, x_sample, cache_mla_ckv, cache_mla_kpe, state_ret, meta_tokens,
              ffn1_w_in, ffn1_w_out, ln1_g, ln1_b, w_mix_in, ret_gn_g, mla_q_norm_g, mla_w_uq,
              mla_kv_norm_g, mla_w_ukv, w_mix_out, ln2_g, ln2_b, ffn2_w_in, ffn2_w_out, ln3_g, ln3_b):
    B, S, D = x_prompt.shape
    L = N_META + S
    x = jnp.concatenate([jnp.broadcast_to(meta_tokens.astype(x_prompt.dtype)[None], (B, N_META, D)), x_prompt], 1)
    h = x
    for _ in range(DEPTH):
        h = ffn_block(h, ffn1_w_in, ffn1_w_out, ln1_g, ln1_b)
        pos = jnp.arange(L)
        q_r, k_r, v_r, g_r, q_m, p_ckv, p_kpe = mixer_front(h, pos, w_mix_in, mla_q_norm_g, mla_w_uq, mla_kv_norm_g)
        pad = (-L) % CHUNK
        padf = lambda a: jnp.pad(a, ((0, 0), (pad, 0), (0, 0), (0, 0)))
        s0 = jnp.zeros((B, RET_HEADS, RET_DK, RET_DV), jnp.float32)
        o_r, p_state = retention(padf(q_r), padf(k_r), padf(v_r), s0, CHUNK)
        ret_y = retention_out(o_r[:, pad:], g_r, ret_gn_g)
        k_m, v_m = mla_keys(p_ckv, p_kpe, mla_w_ukv)
        meta_o = attend(q_m[:, :N_META], k_m[:, :N_META], v_m[:, :N_META], None)
        key_chunk = jnp.concatenate([jnp.full((N_META,), -1, jnp.int32), jnp.arange(S, dtype=jnp.int32) // CHUNK])
        nb = S // Q_BLOCK
        q_blocks = jnp.moveaxis(q_m[:, N_META:].reshape(B, nb, Q_BLOCK, MLA_HEADS, MLA_NOPE + MLA_ROPE), 1, 0)

        def block(args):
            qb, bi = args
            q_chunk = (bi * Q_BLOCK + jnp.arange(Q_BLOCK, dtype=jnp.int32)) // CHUNK
            mask = key_chunk[None, :] <= q_chunk[:, None]
            return attend(qb, k_m, v_m, mask)

        frame_o = lax.map(block, (q_blocks, jnp.arange(nb, dtype=jnp.int32)))
        frame_o = jnp.moveaxis(frame_o, 0, 1).reshape(B, S, MLA_HEADS, MLA_V)
        mla_y = jnp.concatenate([meta_o, frame_o], 1).reshape(B, L, MLA_HEADS * MLA_V)
        h = trunk_tail(h, ret_y, mla_y, w_mix_out, ln2_g, ln2_b, ffn2_w_in, ffn2_w_out, ln3_g, ln3_b)
    y_prompt = h[:, N_META:]

    DB, T, _ = x_sample.shape
    P = cache_mla_ckv.shape[1]
    hs = x_sample
    for _ in range(DEPTH):
        meta_h = ffn_block(meta_tokens.astype(x_sample.dtype), ffn1_w_in, ffn1_w_out, ln1_g, ln1_b)
        meta_lat = (meta_h @ w_mix_in[:, OFF_CKV:D_IN])[None]
        meta_ckv, meta_kpe = mla_latent(meta_lat[..., :MLA_KV_LORA], meta_lat[..., MLA_KV_LORA:], jnp.arange(N_META), mla_kv_norm_g)
        hs = ffn_block(hs, ffn1_w_in, ffn1_w_out, ln1_g, ln1_b)
        pos_s = N_META + P + jnp.arange(T)
        q_r, k_r, v_r, g_r, q_m, s_ckv, s_kpe = mixer_front(hs, pos_s, w_mix_in, mla_q_norm_g, mla_w_uq, mla_kv_norm_g)
        o_r, s_state = retention(q_r, k_r, v_r, state_ret, T)
        ret_y = retention_out(o_r, g_r, ret_gn_g)
        ckv_all = jnp.concatenate([jnp.broadcast_to(meta_ckv, (DB, N_META, MLA_KV_LORA)).astype(s_ckv.dtype),
                                   cache_mla_ckv.astype(s_ckv.dtype), s_ckv], 1)
        kpe_all = jnp.concatenate([jnp.broadcast_to(meta_kpe, (DB, N_META, MLA_ROPE)).astype(s_kpe.dtype),
                                   cache_mla_kpe.astype(s_kpe.dtype), s_kpe], 1)
        k_s, v_s = mla_keys(ckv_all, kpe_all, mla_w_ukv)
        mla_y = attend(q_m, k_s, v_s, None).reshape(DB, T, MLA_HEADS * MLA_V)
        hs = trunk_tail(hs, ret_y, mla_y, w_mix_out, ln2_g, ln2_b, ffn2_w_in, ffn2_w_out, ln3_g, ln3_b)
    y_sample = hs

    return (y_prompt, y_sample, p_ckv, p_kpe, p_state.astype(x_prompt.dtype),
            s_ckv, s_kpe, s_state.astype(state_ret.dtype))
```

```python
import contextlib
import os
import numpy as np
import concourse.bass as bass
import concourse.mybir as mybir
from concourse.bass_utils import run_bass_kernel_spmd

F32 = mybir.dt.float32
BF16 = mybir.dt.bfloat16
AF = mybir.ActivationFunctionType
ALU = mybir.AluOpType

COMPUTE = ("pe", "act", "dve", "pool")

D = 1024
DFF = 2816
NCH = 22
ALPHA = 2.0 ** 0.25
LN_EPS = 1e-5
RMS_EPS = 1e-6
NT = 34
NSLOT = 33
AUGW = 132
SCALE = 192.0 ** -0.5
NS = 3
STAGE = int(os.environ.get("KSTAGE", "99"))
DBG = bool(int(os.environ.get("KDBG", "0")))


class _Stop(Exception):
    pass


class _Op:
    __slots__ = ("eng", "fn", "deps", "is_dma", "grp", "need_inc", "val")


class Sched:
    def __init__(self, nc):
        self.nc = nc
        self.ops = []
        self.bufs = {}
        self.eng = {"pe": nc.tensor, "act": nc.scalar, "dve": nc.vector,
                    "pool": nc.gpsimd, "sp": nc.sync}
        self.cur_grp = {}

    def _deps_for(self, op, r, w):
        deps = []
        for k in r:
            st = self.bufs.get(k)
            if st is None:
                st = self.bufs[k] = [None, {}]
            if st[0] is not None:
                deps.append(st[0])
            if isinstance(k, tuple) and k[0] == "pb":
                for ek, ro in st[1].items():
                    if ek != op.eng:
                        deps.append(ro)
        for k in w:
            st = self.bufs.get(k)
            if st is None:
                st = self.bufs[k] = [None, {}]
            if st[0] is not None:
                deps.append(st[0])
            deps.extend(st[1].values())
        for k in r:
            st = self.bufs[k]
            key = ("dma", id(op)) if op.is_dma else op.eng
            st[1][key] = op
        for k in w:
            st = self.bufs[k]
            st[0] = op
            st[1] = {}
        return deps

    def op(self, eng, fn, r=(), w=()):
        o = _Op()
        o.eng = eng; o.fn = fn; o.is_dma = False; o.grp = None
        o.need_inc = False; o.val = None
        o.deps = self._deps_for(o, r, w)
        self.ops.append(o)
        return o

    def dma(self, q, fn, r=(), w=(), key=None, join=False):
        o = _Op()
        o.eng = q; o.fn = fn; o.is_dma = True
        o.need_inc = True; o.val = None
        if join and key in self.cur_grp:
            self.cur_grp[key].append(o)
        else:
            self.cur_grp[key] = [o]
        o.grp = (key, self.cur_grp[key])
        o.deps = self._deps_for(o, r, w)
        self.ops.append(o)
        return o

    def emit(self, stack, final_wait_eng="sp"):
        nc = self.nc
        ops = self.ops
        for o in ops:
            for d in o.deps:
                if not d.is_dma:
                    if d.eng == "pe" and o.eng == "pe" and not o.is_dma:
                        continue
                    d.need_inc = True
        cnt = {e: 0 for e in COMPUTE}
        dcnt = {}
        seen = set()
        for o in ops:
            if o.is_dma:
                key, members = o.grp
                if id(members) not in seen:
                    seen.add(id(members))
                    final = dcnt.get(key, 0) + 16 * len(members)
                    dcnt[key] = final
                    for m in members:
                        m.val = final
            elif o.need_inc:
                cnt[o.eng] += 1
                o.val = cnt[o.eng]
        sems = {}
        for e in COMPUTE:
            sems[e] = stack.enter_context(nc.semaphore("s_" + e))
        for i, key in enumerate(dcnt):
            sems[key] = stack.enter_context(nc.semaphore("d%d" % i))
        waited = {e: {} for e in self.eng}
        nwait = 0
        for o in ops:
            E = self.eng[o.eng]
            need = {}
            for d in o.deps:
                if d.is_dma:
                    if o.is_dma and d.grp[1] is o.grp[1]:
                        continue
                    sk = d.grp[0]
                else:
                    if d.eng == "pe" and o.eng == "pe" and not o.is_dma:
                        continue
                    sk = d.eng
                if need.get(sk, 0) < d.val:
                    need[sk] = d.val
            for sk, v in need.items():
                if waited[o.eng].get(sk, 0) < v:
                    E.wait_ge(sems[sk], v)
                    waited[o.eng][sk] = v
                    nwait += 1
            ins = o.fn()
            if o.is_dma:
                ins.then_inc(sems[o.grp[0]], 16)
            elif o.need_inc:
                ins.then_inc(sems[o.eng], 1)
        E = self.eng[final_wait_eng]
        for key, v in dcnt.items():
            if waited[final_wait_eng].get(key, 0) < v:
                E.wait_ge(sems[key], v)
        for e in COMPUTE:
            if cnt[e] > 0:
                E.wait_ge(sems[e], cnt[e])
        return dict(n_ops=len(ops), n_wait=nwait, cnt=cnt, n_dsem=len(dcnt))


def build_program():
    nc = bass.Bass("TRN2", target_bir_lowering=False)
    es = contextlib.ExitStack()
    S = Sched(nc)

    def din(name, shape, dt=F32):
        return nc.dram_tensor(name, shape, dt, kind="ExternalInput").ap()

    def dout(name, shape):
        return nc.dram_tensor(name, shape, F32, kind="ExternalOutput").ap()

    def dscr(name, shape, dt=BF16):
        return nc.dram_tensor(name, shape, dt, kind="Internal").ap()

    def sb(name, shape, dt=F32):
        return es.enter_context(nc.sbuf_tensor("sb_" + name, shape, dt))

    x_all = din("x_all", [NT, 128, D])
    tab_all = din("tab_all", [NT, 128, 384])
    cache_ckv = din("cache_ckv", [2, 1024, 128])
    cache_kpe = din("cache_kpe", [2, 1024, 64])
    state_in = din("state_in", [2, 4, 128, 128])
    w_in = [din("ffn1_w_in", [D, 2 * DFF]), din("ffn2_w_in", [D, 2 * DFF])]
    w_out = [din("ffn1_w_out", [DFF, D]), din("ffn2_w_out", [DFF, D])]
    w_mix_in = din("w_mix_in", [D, 2496])
    w_mix_out = din("w_mix_out", [D, D])
    w_uq = din("mla_w_uq", [256, 768])
    w_ukv = din("mla_w_ukv", [128, 1024])
    lnvec = din("lnvec", [6, D])
    gn_g = din("ret_gn_g", [512])
    gq_g = din("mla_q_norm_g", [256])
    gkv_g = din("mla_kv_norm_g", [128])
    consts = din("consts", [128, 7, 128])
    dect_in = din("dect", [128, 36])
    flg_in = din("flg", [128, 4])

    y_own = dout("y_own", [2048, D])
    y_samp = dout("y_samp", [64, D])
    ckv_own = dout("ckv_own", [2048, 128])
    kpe_own = dout("kpe_own", [2048, 64])
    ckv_meta = dout("ckv_meta", [16, 128])
    kpe_meta = dout("kpe_meta", [16, 64])
    ckv_samp = dout("ckv_samp", [64, 128])
    kpe_samp = dout("kpe_samp", [64, 64])
    state_p = dout("state_p", [4, 128, 128])
    state_s = dout("state_s", [2, 4, 128, 128])
    dbg = dout("dbg", [128, 4096]) if DBG else None

    winb = [dscr("winb%d" % f, [11, 128, 8, 2, 256]) for f in range(2)]
    woutb = [dscr("woutb%d" % f, [6, 128, 4, D]) for f in range(2)]
    wmib = dscr("wmib", [5, 128, 8, 512])
    wmob = dscr("wmob", [2, 128, 8, 512])

    cst = sb("cst", [128, 7, 128])
    ident_f = cst[:, 0, :]
    identb = sb("identb", [128, 128], BF16)
    dect = sb("dect", [128, 3, 3, 4])
    flg = sb("flg", [128, 4])
    lnv = sb("lnv", [128, 6, D])
    gng = sb("gng", [128, 512])
    gqg = sb("gqg", [128, 256])
    gkvg = sb("gkvg", [128, 128])
    wuq = sb("wuq", [128, 2, 768], BF16)
    wukv = sb("wukv", [128, 1024], BF16)
    wukT = sb("wukT", [128, 4, 128], BF16)
    ckvT = sb("ckvT", [128, NSLOT * 128], BF16)
    kpeT = sb("kpeT", [128, NSLOT * 128], BF16)
    aug = sb("aug", [128, NSLOT, AUGW], BF16)
    Sst = sb("Sst", [128, 4, 128])
    Sbf = sb("Sbf", [128, 4, 128], BF16)
    Smeta = sb("Smeta", [128, 4, 128])
    Ss = [sb("Ss%d" % s, [128, 4, 128]) for s in range(2)]
    Ssbf = [sb("Ssbf%d" % s, [128, 4, 128], BF16) for s in range(2)]
    Tt = sb("Tt", [128, 4, 128])
    ring = sb("ring", [128, NS, 4096], BF16)
    xT = sb("xT", [128, 8, 512], BF16)
    xres = sb("xres", [128, 4, D])
    ovl = sb("ovl", [128, 11264], BF16)
    hid = ovl[:, :].rearrange("p (c t) -> p c t", c=NCH)
    qd = ovl[:, 0:2048].rearrange("p (a b) -> p a b", a=4)
    kd = ovl[:, 2048:4096].rearrange("p (a b) -> p a b", a=4)
    vv = ovl[:, 4096:6144].rearrange("p (a b) -> p a b", a=4)
    sgr = ovl[:, 6144:8192].rearrange("p (a b) -> p a b", a=4)
    lat = ovl[:, 8192:8192 + 2 * 4 * 320].bitcast(F32).rearrange("p (a b) -> p a b", a=4)
    cqb = sb("cqb", [128, 4, 256])
    ymix = sb("ymix", [128, 4, D], BF16)
    cqnT = sb("cqnT", [128, 4, 2, 128], BF16)
    tabs = sb("tabs", [128, 4, 384])
    rbuf = sb("rbuf", [128, 2, D])
    ystage = sb("ystage", [128, 1, D])
    sgt = sb("sgt", [128, 2, 512])
    t1 = sb("t1", [128, 512])
    t2 = sb("t2", [128, 512])
    lst = sb("lst", [128, 2, 2, 6])
    lmv = sb("lmv", [128, 2, 4])
    sq = sb("sq", [128, 256])
    rms = sb("rms", [128, 8])
    ckvn = sb("ckvn", [128, 2, 128])
    kper = sb("kper", [128, 2, 128])
    cqn = sb("cqn", [128, 256])
    qT = sb("qT", [128, 4, 128], BF16)
    kT = sb("kT", [128, 4, 128], BF16)
    scm = sb("scm", [128, 4, 128], BF16)
    gst = sb("gst", [128, 4, 6])
    gmv = sb("gmv", [128, 4, 2])
    gsm = sb("gsm", [128, 12])
    on = sb("on", [128, 512])
    qnT = sb("qnT", [128, 4, 128], BF16)
    qabsT2 = sb("qabsT", [128, 2, 4, 128], BF16)
    qrr = sb("qrr", [128, 4, 64])
    qrT2 = sb("qrT", [128, 2, 4, 128], BF16)
    onesb = sb("onesb", [128, 128], BF16)
    rsum = sb("rsum", [128, 512])
    qTm = rsum[:].bitcast(BF16).rearrange("p (s h d) -> p s h d", s=2, h=4)
    pT = sb("pT", [128, 2, 512], BF16)
    olatT = sb("olatT", [128, 4, 128], BF16)
    ymf = ymix[:, :, :].rearrange("p a b -> p (a b)").bitcast(F32)
    ctmp = ymf[:, 0:1024].rearrange("p (a b) -> p a b", a=8)
    ktmp = ymf[:, 1024:2048].rearrange("p (a b) -> p a b", a=8)
    pb = [es.enter_context(nc.psum_tensor("pb%d" % i, [128, 512], F32)) for i in range(8)]

    def PB(i):
        return ("pb", i)

    YM = [("ymix", p) for p in range(4)]

    def OV(c):
        return ("ov", c)

    def Kq(pos):
        return OV(pos)

    def Kk(pos):
        return OV(4 + pos)

    def Kv(pos):
        return OV(8 + pos)

    def Kg(pos):
        return OV(12 + pos)

    def Kl(pos):
        return [OV(16 + (640 * pos) // 512), OV(16 + (640 * pos + 639) // 512)]

    misc_banks = [0, 1, 6, 7]
    misc_ctr = [0]

    def nb():
        b = misc_banks[misc_ctr[0] % len(misc_banks)]
        misc_ctr[0] += 1
        return b

    evac_ctr = [0]

    def evac(out_ap, in_ap, r, w):
        evac_ctr[0] += 1
        if evac_ctr[0] % 2:
            S.op("act", lambda: nc.scalar.copy(out_ap, in_ap), r=r, w=w)
        else:
            S.op("dve", lambda: nc.vector.tensor_copy(out_ap, in_ap), r=r, w=w)

    S.dma("sp", lambda: nc.sync.dma_start(out=cst[:], in_=consts), w=["cst"], key="c0")
    S.dma("sp", lambda: nc.sync.dma_start(out=dect[:].rearrange("p a b c -> p (a b c)"), in_=dect_in), w=["dect"], key="c0", join=True)
    S.dma("sp", lambda: nc.sync.dma_start(out=flg[:], in_=flg_in), w=["flg"], key="c0", join=True)
    S.dma("sp", lambda: nc.sync.dma_start(out=gng[:], in_=gn_g.partition_broadcast(128)), w=["gng"], key="c0", join=True)
    S.dma("sp", lambda: nc.sync.dma_start(out=gqg[:], in_=gq_g.partition_broadcast(128)), w=["gqg"], key="c0", join=True)
    S.dma("sp", lambda: nc.sync.dma_start(out=gkvg[:], in_=gkv_g.partition_broadcast(128)), w=["gkvg"], key="c0", join=True)
    for i in range(6):
        S.dma("sp", lambda i=i: nc.sync.dma_start(out=lnv[:, i, :], in_=lnvec[i].partition_broadcast(128)), w=["lnv"], key="c0", join=True)

    if STAGE >= 1:
        for hq in range(11):
            for gu in range(2):
                src = w_in[0][:, gu * DFF + hq * 256: gu * DFF + (hq + 1) * 256].rearrange("(k p) n -> p k n", p=128)
                dst = winb[0][hq, :, :, gu, :]
                S.dma("pool", lambda src=src, dst=dst: nc.gpsimd.dma_start(out=dst, in_=src),
                      w=[("winb", 0, hq)], key=("pc_win", hq), join=(gu == 1))
        S.dma("pool", lambda: nc.gpsimd.dma_start(out=wuq[:], in_=w_uq.rearrange("(k p) n -> p k n", p=128)), w=["wuq"], key="pc_small")
        S.dma("pool", lambda: nc.gpsimd.dma_start(out=wukv[:], in_=w_ukv), w=["wukv"], key="pc_small", join=True)
        S.dma("sp", lambda: nc.sync.dma_start(out=ctmp[:].rearrange("p a b -> p (a b)"), in_=w_ukv), w=YM, key="c1")
        for pc in range(6):
            ncc = 4 if pc < 5 else 2
            src = w_out[0][pc * 512: pc * 512 + ncc * 128, :].rearrange("(c p) n -> p c n", p=128)
            dst = woutb[0][pc, :, 0:ncc, :]
            S.dma("pool", lambda src=src, dst=dst: nc.gpsimd.dma_start(out=dst, in_=src),
                  w=[("woutb", 0, pc)], key=("pc_wout", pc))
        for g in range(5):
            n = 512 if g < 4 else 448
            src = w_mix_in[:, g * 512: g * 512 + n].rearrange("(k p) n -> p k n", p=128)
            dst = wmib[g, :, :, 0:n]
            S.dma("pool", lambda src=src, dst=dst: nc.gpsimd.dma_start(out=dst, in_=src), w=[("wmib", g)], key=("pc_wmi", g))
        for hf in range(2):
            src = w_mix_out[:, hf * 512:(hf + 1) * 512].rearrange("(k p) n -> p k n", p=128)
            dst = wmob[hf]
            S.dma("pool", lambda src=src, dst=dst: nc.gpsimd.dma_start(out=dst, in_=src), w=[("wmob", hf)], key="pc_wmo", join=(hf == 1))
        first = True
        for hq in range(11):
            for gu in range(2):
                src = w_in[1][:, gu * DFF + hq * 256: gu * DFF + (hq + 1) * 256].rearrange("(k p) n -> p k n", p=128)
                dst = winb[1][hq, :, :, gu, :]
                S.dma("pool", lambda src=src, dst=dst: nc.gpsimd.dma_start(out=dst, in_=src),
                      w=[("winb", 1, hq)], key="pc2", join=not first)
                first = False
        for pc in range(6):
            ncc = 4 if pc < 5 else 2
            src = w_out[1][pc * 512: pc * 512 + ncc * 128, :].rearrange("(c p) n -> p c n", p=128)
            dst = woutb[1][pc, :, 0:ncc, :]
            S.dma("pool", lambda src=src, dst=dst: nc.gpsimd.dma_start(out=dst, in_=src),
                  w=[("woutb", 1, pc)], key="pc2", join=True)

    if STAGE >= 2:
        S.op("dve", lambda: nc.vector.tensor_copy(identb[:], cst[:, 0, :]), r=["cst"], w=["identb"])
        S.op("pool", lambda: nc.gpsimd.memset(aug[:, :, 128:129], 1.0), w=[("aug", s) for s in range(NSLOT)])
        S.op("pool", lambda: nc.gpsimd.memset(Sst[:], 0.0), w=["S"])
        S.op("pool", lambda: nc.gpsimd.memset(qrT2[:], 0.0), w=[("qrT", 0), ("qrT", 1)])
        S.op("pool", lambda: nc.gpsimd.memset(onesb[:], 1.0), w=["onesb"])
        b = nb()
        for h in range(4):
            S.op("pe", lambda h=h, b=b: nc.tensor.transpose(pb[b][:, h * 128:(h + 1) * 128], ctmp[:, 2 * h, :], ident_f),
                 r=YM + ["cst"], w=[PB(b)])
        evac(wukT[:], pb[b][:].rearrange("p (a b) -> p a b", a=4), r=[PB(b)], w=["wukT"])

    blocks = [("b0", [0, 1])] + [("ctx", list(range(2 + 4 * i, 6 + 4 * i))) for i in range(4)] + \
             [("own", list(range(18 + 4 * i, 22 + 4 * i))) for i in range(4)]
    plan = []
    for kind, tiles in blocks:
        nrep = 1 if len(tiles) <= 2 else 2
        plan += [("win", 0, hq) for hq in range(11)] + [("wout", 0, pc) for pc in range(6)] * nrep
        plan += [("wmi", g) for g in ([0, 1, 2, 3, 4] if kind != "ctx" else [1, 2, 4])]
        if kind != "ctx":
            plan += [("wmo", 0), ("wmo", 1)]
            plan += [("win", 1, hq) for hq in range(11)] + [("wout", 1, pc) for pc in range(6)] * nrep
    ws_state = dict(next_load=0, next_use=0)

    def ws_issue(i):
        p = plan[i]
        slot = i % NS
        dst = ring[:, slot, 0:4096]
        if p[0] == "win":
            src = winb[p[1]][p[2]].rearrange("p k g n -> p (k g n)")
            key = ("winb", p[1], p[2])
        elif p[0] == "wout":
            ncc = 4 if p[2] < 5 else 2
            src = woutb[p[1]][p[2]][:, 0:ncc, :].rearrange("p c n -> p (c n)")
            dst = ring[:, slot, 0:ncc * 1024]
            key = ("woutb", p[1], p[2])
        elif p[0] == "wmi":
            n = 512 if p[1] < 4 else 448
            src = wmib[p[1]][:, :, 0:n]
            dst = ring[:, slot, :].rearrange("p (k n) -> p k n", k=8)[:, :, 0:n]
            key = ("wmib", p[1])
        else:
            src = wmob[p[1]].rearrange("p k n -> p (k n)")
            key = ("wmob", p[1])
        S.dma("sp", lambda: nc.sync.dma_start(out=dst, in_=src), r=[key], w=[("ring", slot)], key=("ring", slot))

    def ws_next(expect, hold=0):
        i = ws_state["next_use"]
        assert plan[i] == expect, (plan[i], expect)
        while ws_state["next_load"] < min(len(plan), i + NS - hold):
            ws_issue(ws_state["next_load"])
            ws_state["next_load"] += 1
        ws_state["next_use"] += 1
        return i % NS

    def make_xT(npos, positions=None):
        for pos in (range(npos) if positions is None else positions):
            for hb in range(2):
                bk = hb
                for kk in range(4):
                    k = hb * 4 + kk
                    S.op("pe", lambda pos=pos, k=k, kk=kk, bk=bk: nc.tensor.transpose(
                        pb[bk][:, kk * 128:(kk + 1) * 128], xres[:, pos, k * 128:(k + 1) * 128], ident_f),
                        r=[("xres", pos), "cst"], w=[PB(bk)])
                S.op("act", lambda hb=hb, pos=pos, bk=bk: nc.scalar.copy(xT[:, hb * 4:hb * 4 + 4, pos * 128:(pos + 1) * 128],
                                                                         pb[bk][:].rearrange("p (a b) -> p a b", a=4)),
                     r=[PB(bk)], w=[("xT", pos)])

    ln_ctr = [0]

    def layer_norm(rb, gi, dst_ap, dst_keys):
        i = ln_ctr[0] % 2
        ln_ctr[0] += 1
        R = ("rbuf", rb)
        for hf in range(2):
            S.op("dve", lambda hf=hf: nc.vector.bn_stats(lst[:, i, hf, :], rbuf[:, rb, hf * 512:(hf + 1) * 512]), r=[R], w=[("lst", i)])
        S.op("dve", lambda: nc.vector.bn_aggr(lmv[:, i, 0:2], lst[:, i, :, :].rearrange("p a b -> p (a b)")), r=[("lst", i)], w=[("lmv", i)])
        S.op("act", lambda: nc.scalar.activation(lmv[:, i, 2:3], lmv[:, i, 1:2], AF.Sqrt, bias=LN_EPS / (ALPHA * ALPHA), scale=1.0),
             r=[("lmv", i)], w=[("lmv", i)])
        S.op("dve", lambda: nc.vector.reciprocal(lmv[:, i, 2:3], lmv[:, i, 2:3]), r=[("lmv", i)], w=[("lmv", i)])
        S.op("dve", lambda: nc.vector.tensor_scalar(lmv[:, i, 3:4], lmv[:, i, 0:1], lmv[:, i, 2:3], -1.0, ALU.mult, ALU.mult),
             r=[("lmv", i)], w=[("lmv", i)])
        S.op("act", lambda: nc.scalar.activation(rbuf[:, rb, :], rbuf[:, rb, :], AF.Identity, bias=lmv[:, i, 3:4], scale=lmv[:, i, 2:3]),
             r=[R, ("lmv", i)], w=[R])
        S.op("pool", lambda: nc.gpsimd.tensor_tensor(rbuf[:, rb, :], rbuf[:, rb, :], lnv[:, gi, :], ALU.mult), r=[R, "lnv"], w=[R])
        S.op("pool", lambda: nc.gpsimd.tensor_tensor(dst_ap, rbuf[:, rb, :], lnv[:, gi + 1, :], ALU.add), r=[R, "lnv"], w=dst_keys)

    def rms_norm(src_ap, n, g_ap, dst_ap, r, w, col):
        S.op("act", lambda: nc.scalar.activation(sq[:, 0:n], src_ap, AF.Square, accum_out=rms[:, col:col + 1]), r=r, w=["sq", ("rms", col)])
        S.op("act", lambda: nc.scalar.activation(rms[:, col:col + 1], rms[:, col:col + 1], AF.Sqrt, bias=RMS_EPS, scale=1.0 / n),
             r=[("rms", col)], w=[("rms", col)])
        S.op("dve", lambda: nc.vector.reciprocal(rms[:, col:col + 1], rms[:, col:col + 1]), r=[("rms", col)], w=[("rms", col)])
        S.op("dve", lambda: nc.vector.scalar_tensor_tensor(dst_ap, src_ap, rms[:, col:col + 1], g_ap, ALU.mult, ALU.mult),
             r=list(r) + [("rms", col)], w=w)

    rope_ctr = [0]

    def rope_from(src3, H, Dh, cc_ap, ss_ap, out3, r, w, dec_ap=None):
        hh = Dh // 2
        n = H * Dh
        rope_ctr[0] += 1
        if rope_ctr[0] % 2:
            b1, b2, k1, k2a, k2b = t1, t2, "t1", "t2a", "t2b"
        else:
            b1, b2, k1, k2a, k2b = sgt[:, 0, :], sgt[:, 1, :], ("sgt", 0), ("sgt", 1), ("sgt", 1)
        a1 = b1[:, 0:n].rearrange("p (h d) -> p h d", h=H)
        a2 = b2[:, 0:n].rearrange("p (h d) -> p h d", h=H)
        ccb = cc_ap.unsqueeze(1).broadcast_to([128, H, Dh])
        S.op("dve", lambda: nc.vector.tensor_tensor(a1, src3, ccb, ALU.mult), r=r, w=[k1])
        S.op("dve", lambda: nc.vector.tensor_tensor(a2[:, :, 0:hh], src3[:, :, hh:Dh], ss_ap[:, 0:hh].unsqueeze(1).broadcast_to([128, H, hh]), ALU.mult),
             r=r, w=[k2a])
        S.op("dve", lambda: nc.vector.tensor_tensor(a2[:, :, hh:Dh], src3[:, :, 0:hh], ss_ap[:, hh:Dh].unsqueeze(1).broadcast_to([128, H, hh]), ALU.mult),
             r=r, w=[k2b])
        if dec_ap is None:
            S.op("pool", lambda: nc.gpsimd.tensor_tensor(out3, a1, a2, ALU.add), r=[k1, k2a, k2b], w=w)
        else:
            S.op("pool", lambda: nc.gpsimd.tensor_tensor(a1, a1, a2, ALU.add), r=[k2a, k2b], w=[k1])
            S.op("pool", lambda: nc.gpsimd.tensor_tensor(out3, a1, dec_ap.unsqueeze(2).broadcast_to([128, H, Dh]), ALU.mult), r=[k1, "dect"], w=w)

    def ffn(f, npos, dsts):
        T = npos * 128
        make_xT(npos)
        xkeys = [("xT", p) for p in range(npos)]
        hkeys = ["ovl"]
        for hq in range(11):
            slot = ws_next(("win", f, hq))
            wv = ring[:, slot, :].rearrange("p (k g n) -> p k g n", k=8, g=2)
            for cc in range(2):
                c = 2 * hq + cc
                idx = c % 2
                for gu in range(2):
                    bk = 2 * gu + idx
                    for k in range(8):
                        S.op("pe", lambda bk=bk, k=k, gu=gu, cc=cc, wv=wv: nc.tensor.matmul(
                            pb[bk][:, 0:T], wv[:, k, gu, cc * 128:(cc + 1) * 128], xT[:, k, 0:T], start=(k == 0), stop=(k == 7)),
                            r=[("ring", slot)] + xkeys, w=[PB(bk)])
                S.op("act", lambda idx=idx: nc.scalar.activation(sgt[:, idx, 0:T], pb[idx][:, 0:T], AF.Silu), r=[PB(idx)], w=[("sgt", idx)])
                S.op("dve", lambda idx=idx, c=c: nc.vector.tensor_tensor(hid[:, c, 0:T], pb[2 + idx][:, 0:T], sgt[:, idx, 0:T], ALU.mult),
                     r=[PB(2 + idx), ("sgt", idx)], w=[OV(c)])
        groups = [list(range(npos))] if npos <= 2 else [[0, 1], [2, 3]]
        for grp in groups:
            for pc in range(6):
                slot = ws_next(("wout", f, pc))
                ncc = 4 if pc < 5 else 2
                wv = ring[:, slot, :].rearrange("p (c n) -> p c n", c=4)
                for pos in grp:
                    for hf in range(2):
                        bk = pos * 2 + hf
                        for cc in range(ncc):
                            c = pc * 4 + cc
                            S.op("pe", lambda bk=bk, c=c, cc=cc, pos=pos, hf=hf, wv=wv: nc.tensor.matmul(
                                pb[bk][:, :], hid[:, c, pos * 128:(pos + 1) * 128], wv[:, cc, hf * 512:(hf + 1) * 512],
                                start=(c == 0), stop=(c == NCH - 1)),
                                r=[("ring", slot), OV(c)], w=[PB(bk)])
            for pos in grp:
                rb = pos % 2
                for hf in range(2):
                    bk = pos * 2 + hf
                    S.op("dve", lambda bk=bk, hf=hf, pos=pos, rb=rb: nc.vector.scalar_tensor_tensor(
                        rbuf[:, rb, hf * 512:(hf + 1) * 512], pb[bk][:, :], 0.5 / ALPHA, xres[:, pos, hf * 512:(hf + 1) * 512], ALU.mult, ALU.add),
                        r=[PB(bk), ("xres", pos)], w=[("rbuf", rb)])
                dst_ap, dst_keys, after = dsts[pos]
                layer_norm(rb, 0 if f == 0 else 4, dst_ap, dst_keys)
                if after is not None:
                    after()

    def tile_info(t):
        if t == 0:
            return dict(kind="meta", slot=0, typ=1, K=16)
        if t == 1:
            return dict(kind="samp", slot=17, typ=2, K=64)
        if t < 18:
            return dict(kind="ctx", slot=1 + (t - 2), typ=0, K=128, j=t - 2)
        return dict(kind="own", slot=17 + (t - 18), typ=0, K=128, j=t - 18)

    out_ctr = [0]

    def front_latents(t, pos, info):
        slot = info["slot"]
        i = out_ctr[0] % 2
        out_ctr[0] += 1
        L = Kl(pos)
        rms_norm(lat[:, pos, 0:128], 128, gkvg[:], ckvn[:, i, :], r=L + ["gkvg"], w=[("ckvn", i)], col=i)
        rope_from(lat[:, pos, 128:192].unsqueeze(1), 1, 64, tabs[:, pos, 256:320], tabs[:, pos, 320:384],
                  kper[:, i, 0:64].unsqueeze(1), r=L + [("tabs", pos)], w=[("kper", i)])
        S.op("pool", lambda: nc.gpsimd.tensor_copy(kper[:, i, 64:128], kper[:, i, 0:64]), r=[("kper", i)], w=[("kper", i)])
        kind = info["kind"]
        if kind == "own":
            j = info["j"]
            S.dma("pool", lambda: nc.gpsimd.dma_start(out=ckv_own[j * 128:(j + 1) * 128, :], in_=ckvn[:, i, :]), r=[("ckvn", i)], key=("o_lat", i))
            S.dma("pool", lambda: nc.gpsimd.dma_start(out=kpe_own[j * 128:(j + 1) * 128, :], in_=kper[:, i, 0:64]), r=[("kper", i)], key=("o_lat", i), join=True)
        elif kind == "meta":
            S.dma("pool", lambda: nc.gpsimd.dma_start(out=ckv_meta, in_=ckvn[0:16, i, :]), r=[("ckvn", i)], key=("o_lat", i))
            S.dma("pool", lambda: nc.gpsimd.dma_start(out=kpe_meta, in_=kper[0:16, i, 0:64]), r=[("kper", i)], key=("o_lat", i), join=True)
        elif kind == "samp":
            S.dma("pool", lambda: nc.gpsimd.dma_start(out=ckv_samp, in_=ckvn[0:64, i, :]), r=[("ckvn", i)], key=("o_lat", i))
            S.dma("pool", lambda: nc.gpsimd.dma_start(out=kpe_samp, in_=kper[0:64, i, 0:64]), r=[("kper", i)], key=("o_lat", i), join=True)
        S.op("act", lambda: nc.scalar.copy(aug[:, slot, 0:128], ckvn[:, i, :]), r=[("ckvn", i)], w=[("aug", slot)])
        b = nb()
        S.op("pe", lambda: nc.tensor.transpose(pb[b][:, 0:128], ckvn[:, i, :], ident_f), r=[("ckvn", i), "cst"], w=[PB(b)])
        S.op("pe", lambda: nc.tensor.transpose(pb[b][:, 128:256], kper[:, i, :], ident_f), r=[("kper", i), "cst"], w=[PB(b)])
        S.op("dve", lambda: nc.vector.tensor_copy(ckvT[:, slot * 128:(slot + 1) * 128], pb[b][:, 0:128]), r=[PB(b)], w=[("ckvT", slot)])
        S.op("act", lambda: nc.scalar.copy(kpeT[:, slot * 128:(slot + 1) * 128], pb[b][:, 128:256]), r=[PB(b)], w=[("kpeT", slot)])
        if kind in ("own", "samp"):
            rms_norm(cqb[:, pos, :], 256, gqg[:], cqn[:], r=[("cqb", pos), "gqg"], w=["cqn"], col=2 + i)
            b2 = nb()
            for kc in range(2):
                S.op("pe", lambda kc=kc: nc.tensor.transpose(pb[b2][:, kc * 128:(kc + 1) * 128], cqn[:, kc * 128:(kc + 1) * 128], ident_f),
                     r=["cqn", "cst"], w=[PB(b2)])
            evac(cqnT[:, pos, :, :], pb[b2][:, 0:256].rearrange("p (a b) -> p a b", a=2), r=[PB(b2)], w=[("cqnT", pos)])

    def mix_in_block(kind, tiles):
        npos = len(tiles)
        groups = [0, 1, 2, 3, 4] if kind != "ctx" else [1, 2, 4]
        for t_i, t in enumerate(tiles):
            S.dma("sp", lambda t=t, t_i=t_i: nc.sync.dma_start(out=tabs[:, t_i, :], in_=tab_all[t]), w=[("tabs", t_i)], key=("tabs", t_i))
        mb = [4, 5, 6, 7]
        ctr = 0
        slots = {}
        slots[groups[0]] = ws_next(("wmi", groups[0]))
        slots[groups[1]] = ws_next(("wmi", groups[1]), hold=1)
        if npos == 4:
            work = [("T", [0, 1]), ("M", groups[0], [0, 1]), ("M", groups[1], [0, 1]), ("T", [2, 3]), ("M", groups[0], [2, 3]), ("M", groups[1], [2, 3])]
        else:
            work = [("T", list(range(npos))), ("M", groups[0], list(range(npos))), ("M", groups[1], list(range(npos)))]
        for g in groups[2:]:
            work.append(("M", g, list(range(npos))))
        for item in work:
            if item[0] == "T":
                make_xT(npos, positions=item[1])
                continue
            g = item[1]
            if g not in slots:
                slots[g] = ws_next(("wmi", g))
            slot = slots[g]
            n = 512 if g < 4 else 448
            wv = ring[:, slot, :].rearrange("p (k n) -> p k n", k=8)
            for pos in item[2]:
                t = tiles[pos]
                info = tile_info(t)
                bk = mb[ctr % 4]
                ctr += 1
                for k in range(8):
                    S.op("pe", lambda bk=bk, k=k, pos=pos, n=n, wv=wv: nc.tensor.matmul(
                        pb[bk][:, 0:n], xT[:, k, pos * 128:(pos + 1) * 128], wv[:, k, 0:n], start=(k == 0), stop=(k == 7)),
                        r=[("ring", slot), ("xT", pos)], w=[PB(bk)])
                src3 = pb[bk][:, :].rearrange("p (h d) -> p h d", h=4)
                typ = info["typ"]
                if g == 0:
                    rope_from(src3, 4, 128, tabs[:, pos, 0:128], tabs[:, pos, 128:256],
                              qd[:, pos, :].rearrange("p (h d) -> p h d", h=4), r=[PB(bk), ("tabs", pos)], w=[Kq(pos)], dec_ap=dect[:, typ, 0, :])
                elif g == 1:
                    rope_from(src3, 4, 128, tabs[:, pos, 0:128], tabs[:, pos, 128:256],
                              kd[:, pos, :].rearrange("p (h d) -> p h d", h=4), r=[PB(bk), ("tabs", pos)], w=[Kk(pos)], dec_ap=dect[:, typ, 1, :])
                elif g == 2:
                    S.op("act", lambda bk=bk, pos=pos: nc.scalar.copy(vv[:, pos, :], pb[bk][:, :]), r=[PB(bk)], w=[Kv(pos)])
                elif g == 3:
                    S.op("act", lambda bk=bk, pos=pos: nc.scalar.activation(sgr[:, pos, :], pb[bk][:, :], AF.Silu), r=[PB(bk)], w=[Kg(pos)])
                else:
                    S.op("act", lambda bk=bk, pos=pos: nc.scalar.copy(cqb[:, pos, :], pb[bk][:, 0:256]), r=[PB(bk)], w=[("cqb", pos)])
                    S.op("dve", lambda bk=bk, pos=pos: nc.vector.tensor_copy(lat[:, pos, 0:192], pb[bk][:, 256:448]), r=[PB(bk)], w=Kl(pos))
        for pos, t in enumerate(tiles):
            front_latents(t, pos, tile_info(t))

    def state_update(pos, K, typ, S_t, Sbf_t, skey, base=0):
        b = nb()
        kv = pb[b][:, :].rearrange("p (h d) -> p h d", h=4)
        for h in range(4):
            S.op("pe", lambda h=h: nc.tensor.matmul(kv[:, h, :], kd[base:base + K, pos, h * 128:(h + 1) * 128],
                                                    vv[base:base + K, pos, h * 128:(h + 1) * 128], start=True, stop=True),
                 r=[Kk(pos), Kv(pos)], w=[PB(b)])
        S.op("dve", lambda: nc.vector.tensor_tensor(Tt[:], kv, S_t[:], ALU.add), r=[PB(b), skey], w=["Tt"])
        S.op("pool", lambda: nc.gpsimd.tensor_tensor(S_t[:], Tt[:], dect[:, typ, 2, :].unsqueeze(2).broadcast_to([128, 4, 128]), ALU.mult),
             r=["Tt", "dect"], w=[skey])
        S.op("act", lambda: nc.scalar.copy(Sbf_t[:], S_t[:]), r=[skey], w=[skey + "bf"])

    def retention_tile(pos, info):
        samp = info["kind"] == "samp"
        for src, dstT, nm, kf in ((qd, qT, "qd", Kq), (kd, kT, "kd", Kk)):
            b = nb()
            pbb = pb[b][:].bitcast(BF16)
            for h in range(4):
                S.op("pe", lambda h=h, src=src, pbb=pbb: nc.tensor.transpose(pbb[:, h * 128:(h + 1) * 128], src[:, pos, h * 128:(h + 1) * 128], identb[:]),
                     r=[kf(pos), "identb"], w=[PB(b)])
            evac(dstT[:], pbb[:, 0:512].rearrange("p (a b) -> p a b", a=4), r=[PB(b)], w=[nm + "T"])
            yield
        b = nb()
        sc = pb[b][:, :].rearrange("p (h d) -> p h d", h=4)
        for h in range(4):
            S.op("pe", lambda h=h: nc.tensor.matmul(sc[:, h, :], kT[:, h, :], qT[:, h, :], start=True, stop=True), r=["kdT", "qdT"], w=[PB(b)])
        mi = 3 if samp else 1
        S.op("dve", lambda: nc.vector.tensor_tensor(scm[:], sc, cst[:, mi, :].unsqueeze(1).broadcast_to([128, 4, 128]), ALU.mult),
             r=[PB(b), "cst"], w=["scm"])
        yield
        if samp:
            for s in range(2):
                S.op("pool", lambda s=s: nc.gpsimd.tensor_tensor(qTm[:, s, :, :], qT[:], cst[:, 5 + s, :].unsqueeze(1).broadcast_to([128, 4, 128]), ALU.mult),
                     r=["qdT", "cst"], w=["rsum"])
        bo = nb()
        o3 = pb[bo][:, :].rearrange("p (h d) -> p h d", h=4)
        for h in range(4):
            S.op("pe", lambda h=h: nc.tensor.matmul(o3[:, h, :], scm[:, h, :], vv[:, pos, h * 128:(h + 1) * 128], start=True, stop=False),
                 r=["scm", Kv(pos)], w=[PB(bo)])
            if samp:
                for s in range(2):
                    S.op("pe", lambda h=h, s=s: nc.tensor.matmul(o3[:, h, :], qTm[:, s, h, :], Ssbf[s][:, h, :], start=False, stop=(s == 1)),
                         r=["rsum", "Ss%dbf" % s], w=[PB(bo)])
            else:
                S.op("pe", lambda h=h: nc.tensor.matmul(o3[:, h, :], qT[:, h, :], Sbf[:, h, :], start=False, stop=True),
                     r=["qdT", "Sbf"], w=[PB(bo)])
        yield
        if samp:
            for s in range(2):
                state_update(pos, 32, 2, Ss[s], Ssbf[s], "Ss%d" % s, base=32 * s)
        else:
            state_update(pos, 128, 0, Sst, Sbf, "S")
        yield
        for h in range(4):
            S.op("dve", lambda h=h: nc.vector.bn_stats(gst[:, h, :], o3[:, h, :]), r=[PB(bo)], w=[("gst", h)])
            S.op("dve", lambda h=h: nc.vector.bn_aggr(gmv[:, h, :], gst[:, h, :]), r=[("gst", h)], w=["gmv"])
        S.op("act", lambda: nc.scalar.activation(gsm[:, 0:4], gmv[:, :, 1], AF.Sqrt, bias=LN_EPS, scale=1.0), r=["gmv"], w=["gsm"])
        S.op("dve", lambda: nc.vector.reciprocal(gsm[:, 4:8], gsm[:, 0:4]), r=["gsm"], w=["gsm"])
        S.op("dve", lambda: nc.vector.scalar_tensor_tensor(gsm[:, 8:12], gmv[:, :, 0], -1.0, gsm[:, 4:8], ALU.mult, ALU.mult), r=["gsm", "gmv"], w=["gsm"])
        yield
        for h in range(4):
            S.op("act", lambda h=h: nc.scalar.activation(on[:, h * 128:(h + 1) * 128], o3[:, h, :], AF.Identity,
                                                         bias=gsm[:, 8 + h:9 + h], scale=gsm[:, 4 + h:5 + h]), r=[PB(bo), "gsm"], w=["on"])
        S.op("pool", lambda: nc.gpsimd.tensor_tensor(on[:], on[:], gng[:], ALU.mult), r=["on", "gng"], w=["on"])
        S.op("pool", lambda: nc.gpsimd.tensor_tensor(ymix[:, pos, 0:512], on[:], sgr[:, pos, :], ALU.mult), r=["on", Kg(pos)], w=[("ymix", pos)])
        yield

    def mla_pre(pos, info, par):
        qabsT = qabsT2[:, par]
        qrT = qrT2[:, par]
        b = nb()
        q3 = pb[b][:, :].rearrange("p (h d) -> p h d", h=4)
        for h in range(4):
            for kc in range(2):
                S.op("pe", lambda h=h, kc=kc: nc.tensor.matmul(q3[:, h, :], wuq[:, kc, h * 192:h * 192 + 128], cqnT[:, pos, kc, :],
                                                               start=(kc == 0), stop=(kc == 1)), r=["wuq", ("cqnT", pos)], w=[PB(b)])
        S.op("act", lambda: nc.scalar.copy(qnT[:], q3), r=[PB(b)], w=["qnT"])
        yield
        b2 = nb()
        qa3 = pb[b2][:, :].rearrange("p (h d) -> p h d", h=4)
        for h in range(4):
            S.op("pe", lambda h=h: nc.tensor.matmul(qa3[:, h, :], wukT[:, h, :], qnT[:, h, :], start=True, stop=True), r=["wukT", "qnT"], w=[PB(b2)])
        S.op("dve", lambda: nc.vector.tensor_copy(qabsT, qa3), r=[PB(b2)], w=[("qabsT", par)])
        yield
        b3 = nb()
        qr3 = pb[b3][:, 0:256].rearrange("p (h d) -> p h d", h=4)
        wr = wuq[:, :, :].rearrange("p k (h d) -> p k h d", h=4)
        for kc in range(2):
            S.op("pe", lambda kc=kc: nc.tensor.matmul(qr3, cqnT[:, pos, kc, :], wr[:, kc, :, 128:192], start=(kc == 0), stop=(kc == 1)),
                 r=["wuq", ("cqnT", pos)], w=[PB(b3)])
        rope_from(qr3, 4, 64, tabs[:, pos, 256:320], tabs[:, pos, 320:384], qrr[:], r=[PB(b3), ("tabs", pos)], w=["qrr"])
        yield
        yield
        yield
        yield
        b4 = nb()
        for pr in range(2):
            S.op("pe", lambda pr=pr: nc.tensor.transpose(pb[b4][:, pr * 128:(pr + 1) * 128], qrr[:, 2 * pr:2 * pr + 2, :].rearrange("p a b -> p (a b)"), ident_f),
                 r=["qrr", "cst"], w=[PB(b4)])
        qr4 = qrT.rearrange("p (a b) d -> p a b d", b=2)
        S.op("dve", lambda: nc.vector.tensor_copy(qr4[0:64, :, 0, :], pb[b4][0:64, 0:256].rearrange("p (a d) -> p a d", a=2)), r=[PB(b4)], w=[("qrT", par)])
        S.op("act", lambda: nc.scalar.copy(qr4[64:128, :, 1, :], pb[b4][64:128, 0:256].rearrange("p (a d) -> p a d", a=2)), r=[PB(b4)], w=[("qrT", par)])
        yield

    def mla_attn(pos, info, keylist, par, side=None):
        qabsT = qabsT2[:, par]
        qrT = qrT2[:, par]
        qa_flat = qabsT.rearrange("p a b -> p (a b)")
        qr_flat = qrT.rearrange("p a b -> p (a b)")
        nk_tiles = len(keylist)

        def emit_qk(ji):
            slot = keylist[ji][0]
            sb_i = 2 + ji % 2
            S.op("pe", lambda: nc.tensor.matmul(pb[sb_i][:, :], ckvT[:, slot * 128:(slot + 1) * 128], qa_flat, start=True, stop=False),
                 r=[("ckvT", slot), ("qabsT", par)], w=[PB(sb_i)])
            S.op("pe", lambda: nc.tensor.matmul(pb[sb_i][:, :], kpeT[:, slot * 128:(slot + 1) * 128], qr_flat, start=False, stop=True),
                 r=[("kpeT", slot), ("qrT", par)], w=[PB(sb_i)])

        emit_qk(0)
        for ji, (slot, biascol, mask) in enumerate(keylist):
            sb_i = 2 + ji % 2
            pi = ji % 2
            if ji + 1 < nk_tiles:
                emit_qk(ji + 1)
            if biascol is None:
                S.op("act", lambda sb_i=sb_i, pi=pi: nc.scalar.activation(pT[:, pi, :], pb[sb_i][:, :], AF.Exp, scale=SCALE),
                     r=[PB(sb_i)], w=[("pT", pi)])
            else:
                S.op("act", lambda sb_i=sb_i, pi=pi, biascol=biascol: nc.scalar.activation(pT[:, pi, :], pb[sb_i][:, :], AF.Exp, scale=SCALE,
                                                                                           bias=flg[:, biascol:biascol + 1]),
                     r=[PB(sb_i), "flg"], w=[("pT", pi)])
            if mask is not None:
                S.op("pool", lambda pi=pi, mask=mask: nc.gpsimd.tensor_tensor(
                    pT[:, pi, :].rearrange("p (h d) -> p h d", h=4), pT[:, pi, :].rearrange("p (h d) -> p h d", h=4),
                    cst[:, mask, :].unsqueeze(1).broadcast_to([128, 4, 128]), ALU.mult), r=[("pT", pi), "cst"], w=[("pT", pi)])
            S.op("pe", lambda pi=pi, slot=slot, ji=ji: nc.tensor.matmul(pb[4][:, :], aug[:, slot, 0:128], pT[:, pi, :],
                                                                       start=(ji == 0), stop=(ji == nk_tiles - 1)),
                 r=[("pT", pi), ("aug", slot)], w=[PB(4)])
            S.op("pe", lambda pi=pi, ji=ji: nc.tensor.matmul(pb[5][:, :], onesb[:], pT[:, pi, :],
                                                            start=(ji == 0), stop=(ji == nk_tiles - 1)),
                 r=[("pT", pi), "onesb"], w=[PB(5)])
            if side is not None:
                next(side, None)
        S.op("dve", lambda: nc.vector.reciprocal(rsum[:], pb[5][:, :]), r=[PB(5)], w=["rsum"])
        S.op("dve", lambda: nc.vector.tensor_tensor(olatT[:].rearrange("p a b -> p (a b)"), pb[4][:, :], rsum[:], ALU.mult), r=[PB(4), "rsum"], w=["olatT"])
        b6 = nb()
        my3 = pb[b6][:, :].rearrange("p (h d) -> p h d", h=4)
        for h in range(4):
            S.op("pe", lambda h=h: nc.tensor.matmul(my3[:, h, :], olatT[:, h, :], wukv[:, h * 256 + 128:h * 256 + 256], start=True, stop=True),
                 r=["olatT", "wukv"], w=[PB(b6)])
        S.op("act", lambda: nc.scalar.copy(ymix[:, pos, 512:1024], pb[b6][:, :]), r=[PB(b6)], w=[("ymix", pos)])

    def mix_out_block(tiles):
        npos = len(tiles)
        for pos in range(npos):
            b = nb()
            pbb = pb[b][:].bitcast(BF16)
            for k in range(8):
                S.op("pe", lambda k=k, pbb=pbb, pos=pos: nc.tensor.transpose(pbb[:, k * 128:(k + 1) * 128], ymix[:, pos, k * 128:(k + 1) * 128], identb[:]),
                     r=[("ymix", pos), "identb"], w=[PB(b)])
            evac(xT[:, :, pos * 128:(pos + 1) * 128], pbb[:, :].rearrange("p (a b) -> p a b", a=8), r=[PB(b)], w=[("xT", pos)])
        slots = [ws_next(("wmo", 0)), ws_next(("wmo", 1), hold=1)]
        for pos in range(npos):
            rb = pos % 2
            for hf in range(2):
                bk = nb()
                wv = ring[:, slots[hf], :].rearrange("p (k n) -> p k n", k=8)
                for k in range(8):
                    S.op("pe", lambda bk=bk, k=k, pos=pos, wv=wv: nc.tensor.matmul(pb[bk][:, :], xT[:, k, pos * 128:(pos + 1) * 128], wv[:, k, :],
                                                                                start=(k == 0), stop=(k == 7)),
                         r=[("ring", slots[hf]), ("xT", pos)], w=[PB(bk)])
                S.op("dve", lambda bk=bk, hf=hf, pos=pos, rb=rb: nc.vector.scalar_tensor_tensor(
                    rbuf[:, rb, hf * 512:(hf + 1) * 512], pb[bk][:, :], 1.0 / ALPHA, xres[:, pos, hf * 512:(hf + 1) * 512], ALU.mult, ALU.add),
                    r=[PB(bk), ("xres", pos)], w=[("rbuf", rb)])
            layer_norm(rb, 2, xres[:, pos, :], [("xres", pos)])

    def load_cache():
        for s in range(2):
            S.dma("sp", lambda s=s: nc.sync.dma_start(out=ctmp[:], in_=cache_ckv[s].rearrange("(i p) d -> p i d", p=128)), w=YM, key="cache")
            for dup in range(2):
                S.dma("sp", lambda s=s, dup=dup: nc.sync.dma_start(out=ktmp[:, :, dup * 64:(dup + 1) * 64], in_=cache_kpe[s].rearrange("(i p) d -> p i d", p=128)), w=YM, key="cache", join=True)
            S.op("act", lambda s=s: nc.scalar.copy(aug[:, 1 + 8 * s:9 + 8 * s, 0:128], ctmp[:]), r=YM, w=[("aug", 1 + 8 * s + i) for i in range(8)])
            for hb in range(2):
                b = nb()
                for i in range(4):
                    S.op("pe", lambda i=i, hb=hb, b=b: nc.tensor.transpose(pb[b][:, i * 128:(i + 1) * 128], ctmp[:, hb * 4 + i, :], ident_f),
                         r=YM + ["cst"], w=[PB(b)])
                c0 = (1 + 8 * s + 4 * hb) * 128
                evac(ckvT[:, c0:c0 + 512], pb[b][:, :], r=[PB(b)], w=[("ckvT", 1 + 8 * s + 4 * hb + i) for i in range(4)])
                b = nb()
                for i in range(4):
                    S.op("pe", lambda i=i, hb=hb, b=b: nc.tensor.transpose(pb[b][:, i * 128:(i + 1) * 128], ktmp[:, hb * 4 + i, :], ident_f),
                         r=YM + ["cst"], w=[PB(b)])
                evac(kpeT[:, c0:c0 + 512], pb[b][:, :], r=[PB(b)], w=[("kpeT", 1 + 8 * s + 4 * hb + i) for i in range(4)])
            S.dma("sp", lambda s=s: nc.sync.dma_start(out=Ss[s][:], in_=state_in[s].rearrange("h p d -> p h d")), w=["Ss%d" % s], key="cache2", join=(s == 1))
        for s in range(2):
            S.op("act", lambda s=s: nc.scalar.copy(Ssbf[s][:], Ss[s][:]), r=["Ss%d" % s], w=["Ss%dbf" % s])

    ycnt = [0]
    try:
        for bi, (kind, tiles) in enumerate(blocks):
            npos = len(tiles)
            if STAGE < 3:
                raise _Stop()
            if kind == "ctx" and STAGE < 8:
                raise _Stop()
            if kind == "own" and STAGE < 9:
                raise _Stop()
            if STAGE < 99 and bi >= 6 + max(0, STAGE - 9):
                raise _Stop()
            for pos, t in enumerate(tiles):
                S.dma("sp", lambda pos=pos, t=t: nc.sync.dma_start(out=xres[:, pos, :], in_=x_all[t]), w=[("xres", pos)], key=("xl", pos))
            ffn(0, npos, [(xres[:, pos, :], [("xres", pos)], None) for pos in range(npos)])
            if kind == "b0":
                if DBG:
                    S.dma("pool", lambda: nc.gpsimd.dma_start(out=dbg[:, 0:1024], in_=xres[:, 1, :]), r=[("xres", 1)], key="dbg0")
                if STAGE < 4:
                    raise _Stop()
                load_cache()
            mix_in_block(kind, tiles)
            if kind == "b0":
                if STAGE < 5:
                    raise _Stop()
                state_update(0, 16, 1, Sst, Sbf, "S")
                S.op("dve", lambda: nc.vector.tensor_copy(Smeta[:], Sst[:]), r=["S"], w=["Smeta"])
                info = tile_info(1)
                for _ in retention_tile(1, info):
                    pass
                if STAGE < 6:
                    raise _Stop()
                keylist = [(0, 2, None)] + [(1 + i, None, 5) for i in range(8)] + [(9 + i, None, 6) for i in range(8)] + [(17, None, 4)]
                for _ in mla_pre(1, info, 0):
                    pass
                mla_attn(1, info, keylist, 0)
                if DBG:
                    S.dma("pool", lambda: nc.gpsimd.dma_start(out=dbg[:, 1024:2048], in_=ymix[:, 1, :]), r=[("ymix", 1)], key="dbg1")
                if STAGE < 7:
                    raise _Stop()
                S.op("pool", lambda: nc.gpsimd.memset(ymix[:, 0, :], 0.0), w=[("ymix", 0)])
                mix_out_block(tiles)
                if DBG:
                    S.dma("pool", lambda: nc.gpsimd.dma_start(out=dbg[:, 2048:3072], in_=xres[:, 1, :]), r=[("xres", 1)], key="dbg2")
                for s in range(2):
                    S.dma("pool", lambda s=s: nc.gpsimd.dma_start(out=state_s[s].rearrange("h p d -> p h d"), in_=Ss[s][:]), r=["Ss%d" % s], key="o_state", join=(s == 1))

                def after_s(i):
                    def f():
                        S.dma("pool", lambda: nc.gpsimd.dma_start(out=y_samp, in_=ystage[0:64, i, :]), r=[("ystage", i)], key=("o_y", i))
                    return f
                dsts = []
                for pos in range(npos):
                    i = 0
                    ycnt[0] += 1
                    dsts.append((ystage[:, i, :], [("ystage", i)], after_s(i) if pos == 1 else None))
                ffn(1, npos, dsts)
            elif kind == "ctx":
                for pos, t in enumerate(tiles):
                    state_update(pos, 128, 0, Sst, Sbf, "S")
                if tiles[-1] == 17:
                    S.op("dve", lambda: nc.vector.tensor_tensor(Tt[:], Sst[:], Smeta[:], ALU.subtract), r=["S", "Smeta"], w=["Tt"])
                    S.op("dve", lambda: nc.vector.scalar_tensor_tensor(Sst[:].rearrange("p a b -> p (a b)"), Tt[:].rearrange("p a b -> p (a b)"), flg[:, 0:1],
                                                                       Smeta[:].rearrange("p a b -> p (a b)"), ALU.mult, ALU.add), r=["Tt", "Smeta", "flg"], w=["S"])
                    S.op("act", lambda: nc.scalar.copy(Sbf[:], Sst[:]), r=["S"], w=["Sbf"])
            else:
                def pre_gen(pos, t, par):
                    info = tile_info(t)
                    yield from retention_tile(pos, info)
                    yield from mla_pre(pos, info, par)

                def drain(g):
                    if g is not None:
                        for _ in g:
                            pass
                drain(pre_gen(0, tiles[0], 0))
                for pos, t in enumerate(tiles):
                    info = tile_info(t)
                    j = info["j"]
                    keylist = [(0, 2, None)] + [(1 + i, 1, None) for i in range(16)] + [(17 + i, None, None) for i in range(j)] + \
                              [(17 + j, None, 2)]
                    side = pre_gen(pos + 1, tiles[pos + 1], (pos + 1) % 2) if pos + 1 < len(tiles) else None
                    mla_attn(pos, info, keylist, pos % 2, side)
                    drain(side)
                mix_out_block(tiles)

                def after_o(i, j):
                    def f():
                        S.dma("pool", lambda: nc.gpsimd.dma_start(out=y_own[j * 128:(j + 1) * 128, :], in_=ystage[:, i, :]), r=[("ystage", i)], key=("o_y", i))
                    return f
                dsts = []
                for pos, t in enumerate(tiles):
                    i = 0
                    ycnt[0] += 1
                    dsts.append((ystage[:, i, :], [("ystage", i)], after_o(i, tile_info(t)["j"])))
                ffn(1, npos, dsts)
        S.dma("pool", lambda: nc.gpsimd.dma_start(out=state_p.rearrange("h p d -> p h d"), in_=Sst[:]), r=["S"], key="o_state2")

    except _Stop:
        pass
    stats = S.emit(es)
    es.close()
    return nc, stats


def _rope_tab(pos, d):
    inv = 10000.0 ** (-np.arange(0, d, 2, dtype=np.float64) / d)
    ang = pos.astype(np.float64)[:, None] * inv[None, :]
    c, s = np.cos(ang), np.sin(ang)
    return np.concatenate([c, c], 1), np.concatenate([-s, s], 1)


def _host_constants(r):
    pos = np.zeros((NT, 128), np.float64)
    pos[0] = np.arange(128)
    pos[1] = 16 + 1024 + (np.arange(128) % 32)
    for j in range(16):
        pos[2 + j] = 16 + j * 128 + np.arange(128)
        pos[18 + j] = 16 + r * 2048 + j * 128 + np.arange(128)
    tab = np.zeros((NT, 128, 384), np.float32)
    for t in range(NT):
        ccr, ssr = _rope_tab(pos[t], 128)
        ccm, ssm = _rope_tab(pos[t], 64)
        tab[t, :, 0:128] = ccr; tab[t, :, 128:256] = ssr
        tab[t, :, 256:320] = ccm; tab[t, :, 320:384] = ssm
    k = np.arange(128)[:, None]; q = np.arange(128)[None, :]
    consts = np.zeros((128, 7, 128), np.float32)
    consts[:, 0] = np.eye(128)
    consts[:, 1] = (q >= k)
    consts[:, 2] = 1.0 - ((k >= 64) & (q < 64))
    same = (k // 32 == q // 32) & (k < 64) & (q < 64)
    consts[:, 3] = same & (q >= k)
    consts[:, 4] = same
    consts[:, 5] = np.broadcast_to(q < 32, (128, 128))
    consts[:, 6] = np.broadcast_to((q >= 32) & (q < 64), (128, 128))
    gam = 1.0 - 2.0 ** (-5.0 - np.arange(4, dtype=np.float64))
    dect = np.zeros((128, 3, 3, 4), np.float64)
    row = np.arange(128, dtype=np.float64)
    for typ, (idx, C) in enumerate(((row, 128), (row, 16), (row % 32, 32))):
        dect[:, typ, 0, :] = gam[None, :] ** (idx[:, None] + 1.0)
        dect[:, typ, 1, :] = (128.0 ** -0.5) * gam[None, :] ** (-idx[:, None] - 1.0)
        dect[:, typ, 2, :] = gam[None, :] ** C
    flg = np.zeros((128, 4), np.float32)
    flg[16:, 2] = -30000.0
    flg[:, 0] = float(r)
    flg[:, 1] = 0.0 if r == 1 else -30000.0
    return tab, consts, dect.reshape(128, 36).astype(np.float32), flg


_CACHE = {}


def kernel(x_prompt, x_sample, cache_mla_ckv, cache_mla_kpe, state_ret, meta_tokens,
           ffn1_w_in, ffn1_w_out, ln1_g, ln1_b, w_mix_in, ret_gn_g, mla_q_norm_g, mla_w_uq,
           mla_kv_norm_g, mla_w_ukv, w_mix_out, ln2_g, ln2_b, ffn2_w_in, ffn2_w_out, ln3_g, ln3_b):
    f32 = lambda a: np.ascontiguousarray(np.asarray(a, dtype=np.float32))
    x_prompt = f32(x_prompt); x_sample = f32(x_sample)
    if "nc" not in _CACHE:
        _CACHE["nc"] = build_program()
    nc, stats = _CACHE["nc"]
    lnvec = np.stack([f32(ln1_g), f32(ln1_b), f32(ln2_g), f32(ln2_b), f32(ln3_g), f32(ln3_b)], 0)
    in_maps = []
    for c in range(8):
        b, r = c // 2, c % 2
        x_all = np.zeros((NT, 128, D), np.float32)
        x_all[0, 0:16] = f32(meta_tokens)
        x_all[1, 0:32] = x_sample[2 * c]
        x_all[1, 32:64] = x_sample[2 * c + 1]
        x_all[2:18] = x_prompt[b, 0:2048].reshape(16, 128, D)
        x_all[18:34] = x_prompt[b, r * 2048:(r + 1) * 2048].reshape(16, 128, D)
        tab, consts, dect, flg = _host_constants(r)
        in_maps.append(dict(
            x_all=x_all, tab_all=tab,
            cache_ckv=f32(cache_mla_ckv[2 * c:2 * c + 2]), cache_kpe=f32(cache_mla_kpe[2 * c:2 * c + 2]),
            state_in=f32(state_ret[2 * c:2 * c + 2]),
            ffn1_w_in=f32(ffn1_w_in), ffn2_w_in=f32(ffn2_w_in), ffn1_w_out=f32(ffn1_w_out), ffn2_w_out=f32(ffn2_w_out),
            w_mix_in=f32(w_mix_in), w_mix_out=f32(w_mix_out), mla_w_uq=f32(mla_w_uq), mla_w_ukv=f32(mla_w_ukv),
            lnvec=lnvec, ret_gn_g=f32(ret_gn_g), mla_q_norm_g=f32(mla_q_norm_g), mla_kv_norm_g=f32(mla_kv_norm_g),
            consts=consts, dect=dect, flg=flg))
    res = run_bass_kernel_spmd(nc, in_maps, core_ids=list(range(8)))
    R = res.results
    _CACHE["last"] = R
    y_prompt = np.zeros((4, 4096, D), np.float32)
    y_sample = np.zeros((16, 32, D), np.float32)
    p_ckv = np.zeros((4, 4112, 128), np.float32)
    p_kpe = np.zeros((4, 4112, 64), np.float32)
    p_state = np.zeros((4, 4, 128, 128), np.float32)
    s_ckv = np.zeros((16, 32, 128), np.float32)
    s_kpe = np.zeros((16, 32, 64), np.float32)
    s_state = np.zeros((16, 4, 128, 128), np.float32)
    for c in range(8):
        b, r = c // 2, c % 2
        o = R[c]
        y_prompt[b, r * 2048:(r + 1) * 2048] = o["y_own"]
        p_ckv[b, 16 + r * 2048:16 + (r + 1) * 2048] = o["ckv_own"]
        p_kpe[b, 16 + r * 2048:16 + (r + 1) * 2048] = o["kpe_own"]
        if r == 0:
            p_ckv[b, 0:16] = o["ckv_meta"]
            p_kpe[b, 0:16] = o["kpe_meta"]
        else:
            p_state[b] = o["state_p"]
        y_sample[2 * c:2 * c + 2] = o["y_samp"].reshape(2, 32, D)
        s_ckv[2 * c:2 * c + 2] = o["ckv_samp"].reshape(2, 32, 128)
        s_kpe[2 * c:2 * c + 2] = o["kpe_samp"].reshape(2, 32, 64)
        s_state[2 * c:2 * c + 2] = o["state_s"]
    return (y_prompt, y_sample, p_ckv, p_kpe, p_state, s_ckv, s_kpe, s_state)
```

```python
import contextlib
import os
import numpy as np
import concourse.bass as bass
import concourse.mybir as mybir
from concourse.bass_utils import run_bass_kernel_spmd

F32 = mybir.dt.float32
BF16 = mybir.dt.bfloat16
AF = mybir.ActivationFunctionType
ALU = mybir.AluOpType

COMPUTE = ("pe", "act", "dve", "pool")

D = 1024
DFF = 2816
NCH = 22
ALPHA = 2.0 ** 0.25
LN_EPS = 1e-5
RMS_EPS = 1e-6
NT = 34
NSLOT = 33
AUGW = 132
SCALE = 192.0 ** -0.5
NS = 3
STAGE = int(os.environ.get("KSTAGE", "99"))
DBG = bool(int(os.environ.get("KDBG", "0")))


class _Stop(Exception):
    pass


class _Op:
    __slots__ = ("eng", "fn", "deps", "is_dma", "grp", "need_inc", "val")


class Sched:
    def __init__(self, nc):
        self.nc = nc
        self.ops = []
        self.bufs = {}
        self.eng = {"pe": nc.tensor, "act": nc.scalar, "dve": nc.vector,
                    "pool": nc.gpsimd, "sp": nc.sync}
        self.cur_grp = {}

    def _deps_for(self, op, r, w):
        deps = []
        for k in r:
            st = self.bufs.get(k)
            if st is None:
                st = self.bufs[k] = [None, {}]
            if st[0] is not None:
                deps.append(st[0])
            if isinstance(k, tuple) and k[0] == "pb":
                for ek, ro in st[1].items():
                    if ek != op.eng:
                        deps.append(ro)
        for k in w:
            st = self.bufs.get(k)
            if st is None:
                st = self.bufs[k] = [None, {}]
            if st[0] is not None:
                deps.append(st[0])
            deps.extend(st[1].values())
        for k in r:
            st = self.bufs[k]
            key = ("dma", id(op)) if op.is_dma else op.eng
            st[1][key] = op
        for k in w:
            st = self.bufs[k]
            st[0] = op
            st[1] = {}
        return deps

    def op(self, eng, fn, r=(), w=()):
        o = _Op()
        o.eng = eng; o.fn = fn; o.is_dma = False; o.grp = None
        o.need_inc = False; o.val = None
        o.deps = self._deps_for(o, r, w)
        self.ops.append(o)
        return o

    def dma(self, q, fn, r=(), w=(), key=None, join=False):
        o = _Op()
        o.eng = q; o.fn = fn; o.is_dma = True
        o.need_inc = True; o.val = None
        if join and key in self.cur_grp:
            self.cur_grp[key].append(o)
        else:
            self.cur_grp[key] = [o]
        o.grp = (key, self.cur_grp[key])
        o.deps = self._deps_for(o, r, w)
        self.ops.append(o)
        return o

    def emit(self, stack, final_wait_eng="sp"):
        nc = self.nc
        ops = self.ops
        for o in ops:
            for d in o.deps:
                if not d.is_dma:
                    if d.eng == "pe" and o.eng == "pe" and not o.is_dma:
                        continue
                    d.need_inc = True
        cnt = {e: 0 for e in COMPUTE}
        dcnt = {}
        seen = set()
        for o in ops:
            if o.is_dma:
                key, members = o.grp
                if id(members) not in seen:
                    seen.add(id(members))
                    final = dcnt.get(key, 0) + 16 * len(members)
                    dcnt[key] = final
                    for m in members:
                        m.val = final
            elif o.need_inc:
                cnt[o.eng] += 1
                o.val = cnt[o.eng]
        sems = {}
        for e in COMPUTE:
            sems[e] = stack.enter_context(nc.semaphore("s_" + e))
        for i, key in enumerate(dcnt):
            sems[key] = stack.enter_context(nc.semaphore("d%d" % i))
        waited = {e: {} for e in self.eng}
        nwait = 0
        for o in ops:
            E = self.eng[o.eng]
            need = {}
            for d in o.deps:
                if d.is_dma:
                    if o.is_dma and d.grp[1] is o.grp[1]:
                        continue
                    sk = d.grp[0]
                else:
                    if d.eng == "pe" and o.eng == "pe" and not o.is_dma:
                        continue
                    sk = d.eng
                if need.get(sk, 0) < d.val:
                    need[sk] = d.val
            for sk, v in need.items():
                if waited[o.eng].get(sk, 0) < v:
                    E.wait_ge(sems[sk], v)
                    waited[o.eng][sk] = v
                    nwait += 1
            ins = o.fn()
            if o.is_dma:
                ins.then_inc(sems[o.grp[0]], 16)
            elif o.need_inc:
                ins.then_inc(sems[o.eng], 1)
        E = self.eng[final_wait_eng]
        for key, v in dcnt.items():
            if waited[final_wait_eng].get(key, 0) < v:
                E.wait_ge(sems[key], v)
        for e in COMPUTE:
            if cnt[e] > 0:
                E.wait_ge(sems[e], cnt[e])
        return dict(n_ops=len(ops), n_wait=nwait, cnt=cnt, n_dsem=len(dcnt))


def build_program():
    nc = bass.Bass("TRN2", target_bir_lowering=False)
    es = contextlib.ExitStack()
    S = Sched(nc)

    def din(name, shape, dt=F32):
        return nc.dram_tensor(name, shape, dt, kind="ExternalInput").ap()

    def dout(name, shape):
        return nc.dram_tensor(name, shape, F32, kind="ExternalOutput").ap()

    def dscr(name, shape, dt=BF16):
        return nc.dram_tensor(name, shape, dt, kind="Internal").ap()

    def sb(name, shape, dt=F32):
        return es.enter_context(nc.sbuf_tensor("sb_" + name, shape, dt))

    x_all = din("x_all", [NT, 128, D])
    tab_all = din("tab_all", [NT, 128, 384])
    cache_ckv = din("cache_ckv", [2, 1024, 128])
    cache_kpe = din("cache_kpe", [2, 1024, 64])
    state_in = din("state_in", [2, 4, 128, 128])
    w_in = [din("ffn1_w_in", [D, 2 * DFF]), din("ffn2_w_in", [D, 2 * DFF])]
    w_out = [din("ffn1_w_out", [DFF, D]), din("ffn2_w_out", [DFF, D])]
    w_mix_in = din("w_mix_in", [D, 2496])
    w_mix_out = din("w_mix_out", [D, D])
    w_uq = din("mla_w_uq", [256, 768])
    w_ukv = din("mla_w_ukv", [128, 1024])
    lnvec = din("lnvec", [6, D])
    gn_g = din("ret_gn_g", [512])
    gq_g = din("mla_q_norm_g", [256])
    gkv_g = din("mla_kv_norm_g", [128])
    consts = din("consts", [128, 7, 128])
    dect_in = din("dect", [128, 36])
    flg_in = din("flg", [128, 4])

    y_own = dout("y_own", [2048, D])
    y_samp = dout("y_samp", [64, D])
    ckv_own = dout("ckv_own", [2048, 128])
    kpe_own = dout("kpe_own", [2048, 64])
    ckv_meta = dout("ckv_meta", [16, 128])
    kpe_meta = dout("kpe_meta", [16, 64])
    ckv_samp = dout("ckv_samp", [64, 128])
    kpe_samp = dout("kpe_samp", [64, 64])
    state_p = dout("state_p", [4, 128, 128])
    state_s = dout("state_s", [2, 4, 128, 128])
    dbg = dout("dbg", [128, 4096]) if DBG else None

    winb = [dscr("winb%d" % f, [11, 128, 8, 2, 256]) for f in range(2)]
    woutb = [dscr("woutb%d" % f, [6, 128, 4, D]) for f in range(2)]
    wmib = dscr("wmib", [5, 128, 8, 512])
    wmob = dscr("wmob", [2, 128, 8, 512])

    cst = sb("cst", [128, 7, 128])
    ident_f = cst[:, 0, :]
    identb = sb("identb", [128, 128], BF16)
    dect = sb("dect", [128, 3, 3, 4])
    flg = sb("flg", [128, 4])
    lnv = sb("lnv", [128, 6, D])
    gng = sb("gng", [128, 512])
    gqg = sb("gqg", [128, 256])
    gkvg = sb("gkvg", [128, 128])
    wuq = sb("wuq", [128, 2, 768], BF16)
    wukv = sb("wukv", [128, 1024], BF16)
    wukT = sb("wukT", [128, 4, 128], BF16)
    ckvT = sb("ckvT", [128, NSLOT * 128], BF16)
    kpeT = sb("kpeT", [128, NSLOT * 128], BF16)
    aug = sb("aug", [128, NSLOT, AUGW], BF16)
    Sst = sb("Sst", [128, 4, 128])
    Sbf = sb("Sbf", [128, 4, 128], BF16)
    Smeta = sb("Smeta", [128, 4, 128])
    Ss = [sb("Ss%d" % s, [128, 4, 128]) for s in range(2)]
    Ssbf = [sb("Ssbf%d" % s, [128, 4, 128], BF16) for s in range(2)]
    Tt = sb("Tt", [128, 4, 128])
    ring = sb("ring", [128, NS, 4096], BF16)
    xT = sb("xT", [128, 8, 512], BF16)
    xres = sb("xres", [128, 4, D])
    ovl = sb("ovl", [128, 11264], BF16)
    hid = ovl[:, :].rearrange("p (c t) -> p c t", c=NCH)
    qd = ovl[:, 0:2048].rearrange("p (a b) -> p a b", a=4)
    kd = ovl[:, 2048:4096].rearrange("p (a b) -> p a b", a=4)
    vv = ovl[:, 4096:6144].rearrange("p (a b) -> p a b", a=4)
    sgr = ovl[:, 6144:8192].rearrange("p (a b) -> p a b", a=4)
    lat = ovl[:, 8192:8192 + 2 * 4 * 320].bitcast(F32).rearrange("p (a b) -> p a b", a=4)
    cqb = sb("cqb", [128, 4, 256])
    ymix = sb("ymix", [128, 4, D], BF16)
    cqnT = sb("cqnT", [128, 4, 2, 128], BF16)
    tabs = sb("tabs", [128, 4, 384])
    rbuf = sb("rbuf", [128, 2, D])
    ystage = sb("ystage", [128, 1, D])
    sgt = sb("sgt", [128, 2, 512])
    t1 = sb("t1", [128, 512])
    t2 = sb("t2", [128, 512])
    lst = sb("lst", [128, 2, 2, 6])
    lmv = sb("lmv", [128, 2, 4])
    sq = sb("sq", [128, 256])
    rms = sb("rms", [128, 8])
    ckvn = sb("ckvn", [128, 2, 128])
    kper = sb("kper", [128, 2, 128])
    cqn = sb("cqn", [128, 256])
    qT = sb("qT", [128, 4, 128], BF16)
    kT = sb("kT", [128, 4, 128], BF16)
    scm = sb("scm", [128, 4, 128], BF16)
    gst = sb("gst", [128, 4, 6])
    gmv = sb("gmv", [128, 4, 2])
    gsm = sb("gsm", [128, 12])
    on = sb("on", [128, 512])
    qnT = sb("qnT", [128, 4, 128], BF16)
    qabsT2 = sb("qabsT", [128, 2, 4, 128], BF16)
    qrr = sb("qrr", [128, 4, 64])
    qrT2 = sb("qrT", [128, 2, 4, 128], BF16)
    onesb = sb("onesb", [128, 128], BF16)
    rsum = sb("rsum", [128, 512])
    qTm = rsum[:].bitcast(BF16).rearrange("p (s h d) -> p s h d", s=2, h=4)
    pT = sb("pT", [128, 2, 512], BF16)
    olatT = sb("olatT", [128, 4, 128], BF16)
    ymf = ymix[:, :, :].rearrange("p a b -> p (a b)").bitcast(F32)
    ctmp = ymf[:, 0:1024].rearrange("p (a b) -> p a b", a=8)
    ktmp = ymf[:, 1024:2048].rearrange("p (a b) -> p a b", a=8)
    pb = [es.enter_context(nc.psum_tensor("pb%d" % i, [128, 512], F32)) for i in range(8)]

    def PB(i):
        return ("pb", i)

    YM = [("ymix", p) for p in range(4)]

    def OV(c):
        return ("ov", c)

    def Kq(pos):
        return OV(pos)

    def Kk(pos):
        return OV(4 + pos)

    def Kv(pos):
        return OV(8 + pos)

    def Kg(pos):
        return OV(12 + pos)

    def Kl(pos):
        return [OV(16 + (640 * pos) // 512), OV(16 + (640 * pos + 639) // 512)]

    misc_banks = [0, 1, 6, 7]
    misc_ctr = [0]

    def nb():
        b = misc_banks[misc_ctr[0] % len(misc_banks)]
        misc_ctr[0] += 1
        return b

    evac_ctr = [0]

    def evac(out_ap, in_ap, r, w):
        evac_ctr[0] += 1
        if evac_ctr[0] % 2:
            S.op("act", lambda: nc.scalar.copy(out_ap, in_ap), r=r, w=w)
        else:
            S.op("dve", lambda: nc.vector.tensor_copy(out_ap, in_ap), r=r, w=w)

    S.dma("sp", lambda: nc.sync.dma_start(out=cst[:], in_=consts), w=["cst"], key="c0")
    S.dma("sp", lambda: nc.sync.dma_start(out=dect[:].rearrange("p a b c -> p (a b c)"), in_=dect_in), w=["dect"], key="c0", join=True)
    S.dma("sp", lambda: nc.sync.dma_start(out=flg[:], in_=flg_in), w=["flg"], key="c0", join=True)
    S.dma("sp", lambda: nc.sync.dma_start(out=gng[:], in_=gn_g.partition_broadcast(128)), w=["gng"], key="c0", join=True)
    S.dma("sp", lambda: nc.sync.dma_start(out=gqg[:], in_=gq_g.partition_broadcast(128)), w=["gqg"], key="c0", join=True)
    S.dma("sp", lambda: nc.sync.dma_start(out=gkvg[:], in_=gkv_g.partition_broadcast(128)), w=["gkvg"], key="c0", join=True)
    for i in range(6):
        S.dma("sp", lambda i=i: nc.sync.dma_start(out=lnv[:, i, :], in_=lnvec[i].partition_broadcast(128)), w=["lnv"], key="c0", join=True)

    if STAGE >= 1:
        for hq in range(11):
            for gu in range(2):
                src = w_in[0][:, gu * DFF + hq * 256: gu * DFF + (hq + 1) * 256].rearrange("(k p) n -> p k n", p=128)
                dst = winb[0][hq, :, :, gu, :]
                S.dma("pool", lambda src=src, dst=dst: nc.gpsimd.dma_start(out=dst, in_=src),
                      w=[("winb", 0, hq)], key=("pc_win", hq), join=(gu == 1))
        S.dma("pool", lambda: nc.gpsimd.dma_start(out=wuq[:], in_=w_uq.rearrange("(k p) n -> p k n", p=128)), w=["wuq"], key="pc_small")
        S.dma("pool", lambda: nc.gpsimd.dma_start(out=wukv[:], in_=w_ukv), w=["wukv"], key="pc_small", join=True)
        S.dma("sp", lambda: nc.sync.dma_start(out=ctmp[:].rearrange("p a b -> p (a b)"), in_=w_ukv), w=YM, key="c1")
        for pc in range(6):
            ncc = 4 if pc < 5 else 2
            src = w_out[0][pc * 512: pc * 512 + ncc * 128, :].rearrange("(c p) n -> p c n", p=128)
            dst = woutb[0][pc, :, 0:ncc, :]
            S.dma("pool", lambda src=src, dst=dst: nc.gpsimd.dma_start(out=dst, in_=src),
                  w=[("woutb", 0, pc)], key=("pc_wout", pc))
        for g in range(5):
            n = 512 if g < 4 else 448
            src = w_mix_in[:, g * 512: g * 512 + n].rearrange("(k p) n -> p k n", p=128)
            dst = wmib[g, :, :, 0:n]
            S.dma("pool", lambda src=src, dst=dst: nc.gpsimd.dma_start(out=dst, in_=src), w=[("wmib", g)], key=("pc_wmi", g))
        for hf in range(2):
            src = w_mix_out[:, hf * 512:(hf + 1) * 512].rearrange("(k p) n -> p k n", p=128)
            dst = wmob[hf]
            S.dma("pool", lambda src=src, dst=dst: nc.gpsimd.dma_start(out=dst, in_=src), w=[("wmob", hf)], key="pc_wmo", join=(hf == 1))
        first = True
        for hq in range(11):
            for gu in range(2):
                src = w_in[1][:, gu * DFF + hq * 256: gu * DFF + (hq + 1) * 256].rearrange("(k p) n -> p k n", p=128)
                dst = winb[1][hq, :, :, gu, :]
                S.dma("pool", lambda src=src, dst=dst: nc.gpsimd.dma_start(out=dst, in_=src),
                      w=[("winb", 1, hq)], key="pc2", join=not first)
                first = False
        for pc in range(6):
            ncc = 4 if pc < 5 else 2
            src = w_out[1][pc * 512: pc * 512 + ncc * 128, :].rearrange("(c p) n -> p c n", p=128)
            dst = woutb[1][pc, :, 0:ncc, :]
            S.dma("pool", lambda src=src, dst=dst: nc.gpsimd.dma_start(out=dst, in_=src),
                  w=[("woutb", 1, pc)], key="pc2", join=True)

    if STAGE >= 2:
        S.op("dve", lambda: nc.vector.tensor_copy(identb[:], cst[:, 0, :]), r=["cst"], w=["identb"])
        S.op("pool", lambda: nc.gpsimd.memset(aug[:, :, 128:129], 1.0), w=[("aug", s) for s in range(NSLOT)])
        S.op("pool", lambda: nc.gpsimd.memset(Sst[:], 0.0), w=["S"])
        S.op("pool", lambda: nc.gpsimd.memset(qrT2[:], 0.0), w=[("qrT", 0), ("qrT", 1)])
        S.op("pool", lambda: nc.gpsimd.memset(onesb[:], 1.0), w=["onesb"])
        b = nb()
        for h in range(4):
            S.op("pe", lambda h=h, b=b: nc.tensor.transpose(pb[b][:, h * 128:(h + 1) * 128], ctmp[:, 2 * h, :], ident_f),
                 r=YM + ["cst"], w=[PB(b)])
        evac(wukT[:], pb[b][:].rearrange("p (a b) -> p a b", a=4), r=[PB(b)], w=["wukT"])

    blocks = [("b0", [0, 1])] + [("ctx", list(range(2 + 4 * i, 6 + 4 * i))) for i in range(4)] + \
             [("own", list(range(18 + 4 * i, 22 + 4 * i))) for i in range(4)]
    plan = []
    for kind, tiles in blocks:
        nrep = 1 if len(tiles) <= 2 else 2
        plan += [("win", 0, hq) for hq in range(11)] + [("wout", 0, pc) for pc in range(6)] * nrep
        plan += [("wmi", g) for g in ([0, 1, 2, 3, 4] if kind != "ctx" else [1, 2, 4])]
        if kind != "ctx":
            plan += [("wmo", 0), ("wmo", 1)]
            plan += [("win", 1, hq) for hq in range(11)] + [("wout", 1, pc) for pc in range(6)] * nrep
    ws_state = dict(next_load=0, next_use=0)

    def ws_issue(i):
        p = plan[i]
        slot = i % NS
        dst = ring[:, slot, 0:4096]
        if p[0] == "win":
            src = winb[p[1]][p[2]].rearrange("p k g n -> p (k g n)")
            key = ("winb", p[1], p[2])
        elif p[0] == "wout":
            ncc = 4 if p[2] < 5 else 2
            src = woutb[p[1]][p[2]][:, 0:ncc, :].rearrange("p c n -> p (c n)")
            dst = ring[:, slot, 0:ncc * 1024]
            key = ("woutb", p[1], p[2])
        elif p[0] == "wmi":
            n = 512 if p[1] < 4 else 448
            src = wmib[p[1]][:, :, 0:n]
            dst = ring[:, slot, :].rearrange("p (k n) -> p k n", k=8)[:, :, 0:n]
            key = ("wmib", p[1])
        else:
            src = wmob[p[1]].rearrange("p k n -> p (k n)")
            key = ("wmob", p[1])
        S.dma("sp", lambda: nc.sync.dma_start(out=dst, in_=src), r=[key], w=[("ring", slot)], key=("ring", slot))

    def ws_next(expect, hold=0):
        i = ws_state["next_use"]
        assert plan[i] == expect, (plan[i], expect)
        while ws_state["next_load"] < min(len(plan), i + NS - hold):
            ws_issue(ws_state["next_load"])
            ws_state["next_load"] += 1
        ws_state["next_use"] += 1
        return i % NS

    def make_xT(npos, positions=None):
        for pos in (range(npos) if positions is None else positions):
            for hb in range(2):
                bk = hb
                for kk in range(4):
                    k = hb * 4 + kk
                    S.op("pe", lambda pos=pos, k=k, kk=kk, bk=bk: nc.tensor.transpose(
                        pb[bk][:, kk * 128:(kk + 1) * 128], xres[:, pos, k * 128:(k + 1) * 128], ident_f),
                        r=[("xres", pos), "cst"], w=[PB(bk)])
                S.op("dve", lambda hb=hb, pos=pos, bk=bk: nc.vector.tensor_copy(xT[:, hb * 4:hb * 4 + 4, pos * 128:(pos + 1) * 128],
                                                                                pb[bk][:].rearrange("p (a b) -> p a b", a=4)),
                     r=[PB(bk)], w=[("xT", pos)])

    ln_ctr = [0]

    def layer_norm(rb, gi, dst_ap, dst_keys):
        i = ln_ctr[0] % 2
        ln_ctr[0] += 1
        R = ("rbuf", rb)
        for hf in range(2):
            S.op("dve", lambda hf=hf: nc.vector.bn_stats(lst[:, i, hf, :], rbuf[:, rb, hf * 512:(hf + 1) * 512]), r=[R], w=[("lst", i)])
        S.op("dve", lambda: nc.vector.bn_aggr(lmv[:, i, 0:2], lst[:, i, :, :].rearrange("p a b -> p (a b)")), r=[("lst", i)], w=[("lmv", i)])
        S.op("act", lambda: nc.scalar.activation(lmv[:, i, 2:3], lmv[:, i, 1:2], AF.Sqrt, bias=LN_EPS / (ALPHA * ALPHA), scale=1.0),
             r=[("lmv", i)], w=[("lmv", i)])
        S.op("dve", lambda: nc.vector.reciprocal(lmv[:, i, 2:3], lmv[:, i, 2:3]), r=[("lmv", i)], w=[("lmv", i)])
        S.op("dve", lambda: nc.vector.tensor_scalar(lmv[:, i, 3:4], lmv[:, i, 0:1], lmv[:, i, 2:3], -1.0, ALU.mult, ALU.mult),
             r=[("lmv", i)], w=[("lmv", i)])
        S.op("act", lambda: nc.scalar.activation(rbuf[:, rb, :], rbuf[:, rb, :], AF.Identity, bias=lmv[:, i, 3:4], scale=lmv[:, i, 2:3]),
             r=[R, ("lmv", i)], w=[R])
        S.op("pool", lambda: nc.gpsimd.tensor_tensor(rbuf[:, rb, :], rbuf[:, rb, :], lnv[:, gi, :], ALU.mult), r=[R, "lnv"], w=[R])
        S.op("pool", lambda: nc.gpsimd.tensor_tensor(dst_ap, rbuf[:, rb, :], lnv[:, gi + 1, :], ALU.add), r=[R, "lnv"], w=dst_keys)

    def rms_norm(src_ap, n, g_ap, dst_ap, r, w, col):
        S.op("act", lambda: nc.scalar.activation(sq[:, 0:n], src_ap, AF.Square, accum_out=rms[:, col:col + 1]), r=r, w=["sq", ("rms", col)])
        S.op("act", lambda: nc.scalar.activation(rms[:, col:col + 1], rms[:, col:col + 1], AF.Sqrt, bias=RMS_EPS, scale=1.0 / n),
             r=[("rms", col)], w=[("rms", col)])
        S.op("dve", lambda: nc.vector.reciprocal(rms[:, col:col + 1], rms[:, col:col + 1]), r=[("rms", col)], w=[("rms", col)])
        S.op("dve", lambda: nc.vector.scalar_tensor_tensor(dst_ap, src_ap, rms[:, col:col + 1], g_ap, ALU.mult, ALU.mult),
             r=list(r) + [("rms", col)], w=w)

    rope_ctr = [0]

    def rope_from(src3, H, Dh, cc_ap, ss_ap, out3, r, w, dec_ap=None):
        hh = Dh // 2
        n = H * Dh
        rope_ctr[0] += 1
        if rope_ctr[0] % 2:
            b1, b2, k1, k2a, k2b = t1, t2, "t1", "t2a", "t2b"
        else:
            b1, b2, k1, k2a, k2b = sgt[:, 0, :], sgt[:, 1, :], ("sgt", 0), ("sgt", 1), ("sgt", 1)
        a1 = b1[:, 0:n].rearrange("p (h d) -> p h d", h=H)
        a2 = b2[:, 0:n].rearrange("p (h d) -> p h d", h=H)
        ccb = cc_ap.unsqueeze(1).broadcast_to([128, H, Dh])
        S.op("dve", lambda: nc.vector.tensor_tensor(a1, src3, ccb, ALU.mult), r=r, w=[k1])
        S.op("dve", lambda: nc.vector.tensor_tensor(a2[:, :, 0:hh], src3[:, :, hh:Dh], ss_ap[:, 0:hh].unsqueeze(1).broadcast_to([128, H, hh]), ALU.mult),
             r=r, w=[k2a])
        S.op("dve", lambda: nc.vector.tensor_tensor(a2[:, :, hh:Dh], src3[:, :, 0:hh], ss_ap[:, hh:Dh].unsqueeze(1).broadcast_to([128, H, hh]), ALU.mult),
             r=r, w=[k2b])
        if dec_ap is None:
            S.op("pool", lambda: nc.gpsimd.tensor_tensor(out3, a1, a2, ALU.add), r=[k1, k2a, k2b], w=w)
        else:
            S.op("pool", lambda: nc.gpsimd.tensor_tensor(a1, a1, a2, ALU.add), r=[k2a, k2b], w=[k1])
            S.op("pool", lambda: nc.gpsimd.tensor_tensor(out3, a1, dec_ap.unsqueeze(2).broadcast_to([128, H, Dh]), ALU.mult), r=[k1, "dect"], w=w)

    def ffn(f, npos, dsts):
        T = npos * 128
        make_xT(npos)
        xkeys = [("xT", p) for p in range(npos)]
        hkeys = ["ovl"]
        for hq in range(11):
            slot = ws_next(("win", f, hq))
            wv = ring[:, slot, :].rearrange("p (k g n) -> p k g n", k=8, g=2)
            for cc in range(2):
                c = 2 * hq + cc
                idx = c % 2
                for gu in range(2):
                    bk = 2 * gu + idx
                    for k in range(8):
                        S.op("pe", lambda bk=bk, k=k, gu=gu, cc=cc, wv=wv: nc.tensor.matmul(
                            pb[bk][:, 0:T], wv[:, k, gu, cc * 128:(cc + 1) * 128], xT[:, k, 0:T], start=(k == 0), stop=(k == 7)),
                            r=[("ring", slot)] + xkeys, w=[PB(bk)])
                S.op("act", lambda idx=idx: nc.scalar.activation(sgt[:, idx, 0:T], pb[idx][:, 0:T], AF.Silu), r=[PB(idx)], w=[("sgt", idx)])
                S.op("dve", lambda idx=idx, c=c: nc.vector.tensor_tensor(hid[:, c, 0:T], pb[2 + idx][:, 0:T], sgt[:, idx, 0:T], ALU.mult),
                     r=[PB(2 + idx), ("sgt", idx)], w=[OV(c)])
        groups = [list(range(npos))] if npos <= 2 else [[0, 1], [2, 3]]
        for grp in groups:
            for pc in range(6):
                slot = ws_next(("wout", f, pc))
                ncc = 4 if pc < 5 else 2
                wv = ring[:, slot, :].rearrange("p (c n) -> p c n", c=4)
                for pos in grp:
                    for hf in range(2):
                        bk = pos * 2 + hf
                        for cc in range(ncc):
                            c = pc * 4 + cc
                            S.op("pe", lambda bk=bk, c=c, cc=cc, pos=pos, hf=hf, wv=wv: nc.tensor.matmul(
                                pb[bk][:, :], hid[:, c, pos * 128:(pos + 1) * 128], wv[:, cc, hf * 512:(hf + 1) * 512],
                                start=(c == 0), stop=(c == NCH - 1)),
                                r=[("ring", slot), OV(c)], w=[PB(bk)])
            for pos in grp:
                rb = pos % 2
                for hf in range(2):
                    bk = pos * 2 + hf
                    S.op("dve", lambda bk=bk, hf=hf, pos=pos, rb=rb: nc.vector.scalar_tensor_tensor(
                        rbuf[:, rb, hf * 512:(hf + 1) * 512], pb[bk][:, :], 0.5 / ALPHA, xres[:, pos, hf * 512:(hf + 1) * 512], ALU.mult, ALU.add),
                        r=[PB(bk), ("xres", pos)], w=[("rbuf", rb)])
                dst_ap, dst_keys, after = dsts[pos]
                layer_norm(rb, 0 if f == 0 else 4, dst_ap, dst_keys)
                if after is not None:
                    after()

    def tile_info(t):
        if t == 0:
            return dict(kind="meta", slot=0, typ=1, K=16)
        if t == 1:
            return dict(kind="samp", slot=17, typ=2, K=64)
        if t < 18:
            return dict(kind="ctx", slot=1 + (t - 2), typ=0, K=128, j=t - 2)
        return dict(kind="own", slot=17 + (t - 18), typ=0, K=128, j=t - 18)

    out_ctr = [0]

    def front_latents(t, pos, info):
        slot = info["slot"]
        i = out_ctr[0] % 2
        out_ctr[0] += 1
        L = Kl(pos)
        rms_norm(lat[:, pos, 0:128], 128, gkvg[:], ckvn[:, i, :], r=L + ["gkvg"], w=[("ckvn", i)], col=i)
        rope_from(lat[:, pos, 128:192].unsqueeze(1), 1, 64, tabs[:, pos, 256:320], tabs[:, pos, 320:384],
                  kper[:, i, 0:64].unsqueeze(1), r=L + [("tabs", pos)], w=[("kper", i)])
        S.op("pool", lambda: nc.gpsimd.tensor_copy(kper[:, i, 64:128], kper[:, i, 0:64]), r=[("kper", i)], w=[("kper", i)])
        kind = info["kind"]
        if kind == "own":
            j = info["j"]
            S.dma("pool", lambda: nc.gpsimd.dma_start(out=ckv_own[j * 128:(j + 1) * 128, :], in_=ckvn[:, i, :]), r=[("ckvn", i)], key=("o_lat", i))
            S.dma("pool", lambda: nc.gpsimd.dma_start(out=kpe_own[j * 128:(j + 1) * 128, :], in_=kper[:, i, 0:64]), r=[("kper", i)], key=("o_lat", i), join=True)
        elif kind == "meta":
            S.dma("pool", lambda: nc.gpsimd.dma_start(out=ckv_meta, in_=ckvn[0:16, i, :]), r=[("ckvn", i)], key=("o_lat", i))
            S.dma("pool", lambda: nc.gpsimd.dma_start(out=kpe_meta, in_=kper[0:16, i, 0:64]), r=[("kper", i)], key=("o_lat", i), join=True)
        elif kind == "samp":
            S.dma("pool", lambda: nc.gpsimd.dma_start(out=ckv_samp, in_=ckvn[0:64, i, :]), r=[("ckvn", i)], key=("o_lat", i))
            S.dma("pool", lambda: nc.gpsimd.dma_start(out=kpe_samp, in_=kper[0:64, i, 0:64]), r=[("kper", i)], key=("o_lat", i), join=True)
        S.op("act", lambda: nc.scalar.copy(aug[:, slot, 0:128], ckvn[:, i, :]), r=[("ckvn", i)], w=[("aug", slot)])
        b = nb()
        S.op("pe", lambda: nc.tensor.transpose(pb[b][:, 0:128], ckvn[:, i, :], ident_f), r=[("ckvn", i), "cst"], w=[PB(b)])
        S.op("pe", lambda: nc.tensor.transpose(pb[b][:, 128:256], kper[:, i, :], ident_f), r=[("kper", i), "cst"], w=[PB(b)])
        S.op("dve", lambda: nc.vector.tensor_copy(ckvT[:, slot * 128:(slot + 1) * 128], pb[b][:, 0:128]), r=[PB(b)], w=[("ckvT", slot)])
        S.op("act", lambda: nc.scalar.copy(kpeT[:, slot * 128:(slot + 1) * 128], pb[b][:, 128:256]), r=[PB(b)], w=[("kpeT", slot)])
        if kind in ("own", "samp"):
            rms_norm(cqb[:, pos, :], 256, gqg[:], cqn[:], r=[("cqb", pos), "gqg"], w=["cqn"], col=2 + i)
            b2 = nb()
            for kc in range(2):
                S.op("pe", lambda kc=kc: nc.tensor.transpose(pb[b2][:, kc * 128:(kc + 1) * 128], cqn[:, kc * 128:(kc + 1) * 128], ident_f),
                     r=["cqn", "cst"], w=[PB(b2)])
            evac(cqnT[:, pos, :, :], pb[b2][:, 0:256].rearrange("p (a b) -> p a b", a=2), r=[PB(b2)], w=[("cqnT", pos)])

    def mix_in_block(kind, tiles):
        npos = len(tiles)
        groups = [0, 1, 2, 3, 4] if kind != "ctx" else [1, 2, 4]
        for t_i, t in enumerate(tiles):
            S.dma("sp", lambda t=t, t_i=t_i: nc.sync.dma_start(out=tabs[:, t_i, :], in_=tab_all[t]), w=[("tabs", t_i)], key=("tabs", t_i))
        mb = [4, 5, 6, 7]
        ctr = 0
        slots = {}
        slots[groups[0]] = ws_next(("wmi", groups[0]))
        slots[groups[1]] = ws_next(("wmi", groups[1]), hold=1)
        if npos == 4:
            work = [("T", [0, 1]), ("M", groups[0], [0, 1]), ("M", groups[1], [0, 1]), ("T", [2, 3]), ("M", groups[0], [2, 3]), ("M", groups[1], [2, 3])]
        else:
            work = [("T", list(range(npos))), ("M", groups[0], list(range(npos))), ("M", groups[1], list(range(npos)))]
        for g in groups[2:]:
            work.append(("M", g, list(range(npos))))
        for item in work:
            if item[0] == "T":
                make_xT(npos, positions=item[1])
                continue
            g = item[1]
            if g not in slots:
                slots[g] = ws_next(("wmi", g))
            slot = slots[g]
            n = 512 if g < 4 else 448
            wv = ring[:, slot, :].rearrange("p (k n) -> p k n", k=8)
            for pos in item[2]:
                t = tiles[pos]
                info = tile_info(t)
                bk = mb[ctr % 4]
                ctr += 1
                for k in range(8):
                    S.op("pe", lambda bk=bk, k=k, pos=pos, n=n, wv=wv: nc.tensor.matmul(
                        pb[bk][:, 0:n], xT[:, k, pos * 128:(pos + 1) * 128], wv[:, k, 0:n], start=(k == 0), stop=(k == 7)),
                        r=[("ring", slot), ("xT", pos)], w=[PB(bk)])
                src3 = pb[bk][:, :].rearrange("p (h d) -> p h d", h=4)
                typ = info["typ"]
                if g == 0:
                    rope_from(src3, 4, 128, tabs[:, pos, 0:128], tabs[:, pos, 128:256],
                              qd[:, pos, :].rearrange("p (h d) -> p h d", h=4), r=[PB(bk), ("tabs", pos)], w=[Kq(pos)], dec_ap=dect[:, typ, 0, :])
                elif g == 1:
                    rope_from(src3, 4, 128, tabs[:, pos, 0:128], tabs[:, pos, 128:256],
                              kd[:, pos, :].rearrange("p (h d) -> p h d", h=4), r=[PB(bk), ("tabs", pos)], w=[Kk(pos)], dec_ap=dect[:, typ, 1, :])
                elif g == 2:
                    S.op("act", lambda bk=bk, pos=pos: nc.scalar.copy(vv[:, pos, :], pb[bk][:, :]), r=[PB(bk)], w=[Kv(pos)])
                elif g == 3:
                    S.op("act", lambda bk=bk, pos=pos: nc.scalar.activation(sgr[:, pos, :], pb[bk][:, :], AF.Silu), r=[PB(bk)], w=[Kg(pos)])
                else:
                    S.op("act", lambda bk=bk, pos=pos: nc.scalar.copy(cqb[:, pos, :], pb[bk][:, 0:256]), r=[PB(bk)], w=[("cqb", pos)])
                    S.op("dve", lambda bk=bk, pos=pos: nc.vector.tensor_copy(lat[:, pos, 0:192], pb[bk][:, 256:448]), r=[PB(bk)], w=Kl(pos))
        for pos, t in enumerate(tiles):
            front_latents(t, pos, tile_info(t))

    def state_update(pos, K, typ, S_t, Sbf_t, skey, base=0):
        b = nb()
        kv = pb[b][:, :].rearrange("p (h d) -> p h d", h=4)
        for h in range(4):
            S.op("pe", lambda h=h: nc.tensor.matmul(kv[:, h, :], kd[base:base + K, pos, h * 128:(h + 1) * 128],
                                                    vv[base:base + K, pos, h * 128:(h + 1) * 128], start=True, stop=True),
                 r=[Kk(pos), Kv(pos)], w=[PB(b)])
        S.op("dve", lambda: nc.vector.tensor_tensor(Tt[:], kv, S_t[:], ALU.add), r=[PB(b), skey], w=["Tt"])
        S.op("pool", lambda: nc.gpsimd.tensor_tensor(S_t[:], Tt[:], dect[:, typ, 2, :].unsqueeze(2).broadcast_to([128, 4, 128]), ALU.mult),
             r=["Tt", "dect"], w=[skey])
        S.op("act", lambda: nc.scalar.copy(Sbf_t[:], S_t[:]), r=[skey], w=[skey + "bf"])

    def retention_tile(pos, info):
        samp = info["kind"] == "samp"
        for src, dstT, nm, kf in ((qd, qT, "qd", Kq), (kd, kT, "kd", Kk)):
            b = nb()
            pbb = pb[b][:].bitcast(BF16)
            for h in range(4):
                S.op("pe", lambda h=h, src=src, pbb=pbb: nc.tensor.transpose(pbb[:, h * 128:(h + 1) * 128], src[:, pos, h * 128:(h + 1) * 128], identb[:]),
                     r=[kf(pos), "identb"], w=[PB(b)])
            evac(dstT[:], pbb[:, 0:512].rearrange("p (a b) -> p a b", a=4), r=[PB(b)], w=[nm + "T"])
            yield
        b = nb()
        sc = pb[b][:, :].rearrange("p (h d) -> p h d", h=4)
        for h in range(4):
            S.op("pe", lambda h=h: nc.tensor.matmul(sc[:, h, :], kT[:, h, :], qT[:, h, :], start=True, stop=True), r=["kdT", "qdT"], w=[PB(b)])
        mi = 3 if samp else 1
        S.op("dve", lambda: nc.vector.tensor_tensor(scm[:], sc, cst[:, mi, :].unsqueeze(1).broadcast_to([128, 4, 128]), ALU.mult),
             r=[PB(b), "cst"], w=["scm"])
        yield
        if samp:
            for s in range(2):
                S.op("pool", lambda s=s: nc.gpsimd.tensor_tensor(qTm[:, s, :, :], qT[:], cst[:, 5 + s, :].unsqueeze(1).broadcast_to([128, 4, 128]), ALU.mult),
                     r=["qdT", "cst"], w=["rsum"])
        bo = nb()
        o3 = pb[bo][:, :].rearrange("p (h d) -> p h d", h=4)
        for h in range(4):
            S.op("pe", lambda h=h: nc.tensor.matmul(o3[:, h, :], scm[:, h, :], vv[:, pos, h * 128:(h + 1) * 128], start=True, stop=False),
                 r=["scm", Kv(pos)], w=[PB(bo)])
            if samp:
                for s in range(2):
                    S.op("pe", lambda h=h, s=s: nc.tensor.matmul(o3[:, h, :], qTm[:, s, h, :], Ssbf[s][:, h, :], start=False, stop=(s == 1)),
                         r=["rsum", "Ss%dbf" % s], w=[PB(bo)])
            else:
                S.op("pe", lambda h=h: nc.tensor.matmul(o3[:, h, :], qT[:, h, :], Sbf[:, h, :], start=False, stop=True),
                     r=["qdT", "Sbf"], w=[PB(bo)])
        yield
        if samp:
            for s in range(2):
                state_update(pos, 32, 2, Ss[s], Ssbf[s], "Ss%d" % s, base=32 * s)
        else:
            state_update(pos, 128, 0, Sst, Sbf, "S")
        yield
        for h in range(4):
            S.op("dve", lambda h=h: nc.vector.bn_stats(gst[:, h, :], o3[:, h, :]), r=[PB(bo)], w=[("gst", h)])
            S.op("dve", lambda h=h: nc.vector.bn_aggr(gmv[:, h, :], gst[:, h, :]), r=[("gst", h)], w=["gmv"])
        S.op("act", lambda: nc.scalar.activation(gsm[:, 0:4], gmv[:, :, 1], AF.Sqrt, bias=LN_EPS, scale=1.0), r=["gmv"], w=["gsm"])
        S.op("dve", lambda: nc.vector.reciprocal(gsm[:, 4:8], gsm[:, 0:4]), r=["gsm"], w=["gsm"])
        S.op("dve", lambda: nc.vector.scalar_tensor_tensor(gsm[:, 8:12], gmv[:, :, 0], -1.0, gsm[:, 4:8], ALU.mult, ALU.mult), r=["gsm", "gmv"], w=["gsm"])
        yield
        for h in range(4):
            S.op("act", lambda h=h: nc.scalar.activation(on[:, h * 128:(h + 1) * 128], o3[:, h, :], AF.Identity,
                                                         bias=gsm[:, 8 + h:9 + h], scale=gsm[:, 4 + h:5 + h]), r=[PB(bo), "gsm"], w=["on"])
        S.op("pool", lambda: nc.gpsimd.tensor_tensor(on[:], on[:], gng[:], ALU.mult), r=["on", "gng"], w=["on"])
        S.op("pool", lambda: nc.gpsimd.tensor_tensor(ymix[:, pos, 0:512], on[:], sgr[:, pos, :], ALU.mult), r=["on", Kg(pos)], w=[("ymix", pos)])
        yield

    def mla_pre(pos, info, par):
        qabsT = qabsT2[:, par]
        qrT = qrT2[:, par]
        b = nb()
        q3 = pb[b][:, :].rearrange("p (h d) -> p h d", h=4)
        for h in range(4):
            for kc in range(2):
                S.op("pe", lambda h=h, kc=kc: nc.tensor.matmul(q3[:, h, :], wuq[:, kc, h * 192:h * 192 + 128], cqnT[:, pos, kc, :],
                                                               start=(kc == 0), stop=(kc == 1)), r=["wuq", ("cqnT", pos)], w=[PB(b)])
        S.op("act", lambda: nc.scalar.copy(qnT[:], q3), r=[PB(b)], w=["qnT"])
        yield
        b2 = nb()
        qa3 = pb[b2][:, :].rearrange("p (h d) -> p h d", h=4)
        for h in range(4):
            S.op("pe", lambda h=h: nc.tensor.matmul(qa3[:, h, :], wukT[:, h, :], qnT[:, h, :], start=True, stop=True), r=["wukT", "qnT"], w=[PB(b2)])
        S.op("dve", lambda: nc.vector.tensor_copy(qabsT, qa3), r=[PB(b2)], w=[("qabsT", par)])
        yield
        b3 = nb()
        qr3 = pb[b3][:, 0:256].rearrange("p (h d) -> p h d", h=4)
        wr = wuq[:, :, :].rearrange("p k (h d) -> p k h d", h=4)
        for kc in range(2):
            S.op("pe", lambda kc=kc: nc.tensor.matmul(qr3, cqnT[:, pos, kc, :], wr[:, kc, :, 128:192], start=(kc == 0), stop=(kc == 1)),
                 r=["wuq", ("cqnT", pos)], w=[PB(b3)])
        rope_from(qr3, 4, 64, tabs[:, pos, 256:320], tabs[:, pos, 320:384], qrr[:], r=[PB(b3), ("tabs", pos)], w=["qrr"])
        yield
        yield
        yield
        yield
        b4 = nb()
        for pr in range(2):
            S.op("pe", lambda pr=pr: nc.tensor.transpose(pb[b4][:, pr * 128:(pr + 1) * 128], qrr[:, 2 * pr:2 * pr + 2, :].rearrange("p a b -> p (a b)"), ident_f),
                 r=["qrr", "cst"], w=[PB(b4)])
        qr4 = qrT.rearrange("p (a b) d -> p a b d", b=2)
        S.op("dve", lambda: nc.vector.tensor_copy(qr4[0:64, :, 0, :], pb[b4][0:64, 0:256].rearrange("p (a d) -> p a d", a=2)), r=[PB(b4)], w=[("qrT", par)])
        S.op("act", lambda: nc.scalar.copy(qr4[64:128, :, 1, :], pb[b4][64:128, 0:256].rearrange("p (a d) -> p a d", a=2)), r=[PB(b4)], w=[("qrT", par)])
        yield

    def mla_attn(pos, info, keylist, par, side=None):
        qabsT = qabsT2[:, par]
        qrT = qrT2[:, par]
        qa_flat = qabsT.rearrange("p a b -> p (a b)")
        qr_flat = qrT.rearrange("p a b -> p (a b)")
        nk_tiles = len(keylist)

        def emit_qk(ji):
            slot = keylist[ji][0]
            sb_i = 2 + ji % 2
            S.op("pe", lambda: nc.tensor.matmul(pb[sb_i][:, :], ckvT[:, slot * 128:(slot + 1) * 128], qa_flat, start=True, stop=False),
                 r=[("ckvT", slot), ("qabsT", par)], w=[PB(sb_i)])
            S.op("pe", lambda: nc.tensor.matmul(pb[sb_i][:, :], kpeT[:, slot * 128:(slot + 1) * 128], qr_flat, start=False, stop=True),
                 r=[("kpeT", slot), ("qrT", par)], w=[PB(sb_i)])

        emit_qk(0)
        for ji, (slot, biascol, mask) in enumerate(keylist):
            sb_i = 2 + ji % 2
            pi = ji % 2
            if ji + 1 < nk_tiles:
                emit_qk(ji + 1)
            if biascol is None:
                S.op("act", lambda sb_i=sb_i, pi=pi: nc.scalar.activation(pT[:, pi, :], pb[sb_i][:, :], AF.Exp, scale=SCALE),
                     r=[PB(sb_i)], w=[("pT", pi)])
            else:
                S.op("act", lambda sb_i=sb_i, pi=pi, biascol=biascol: nc.scalar.activation(pT[:, pi, :], pb[sb_i][:, :], AF.Exp, scale=SCALE,
                                                                                           bias=flg[:, biascol:biascol + 1]),
                     r=[PB(sb_i), "flg"], w=[("pT", pi)])
            if mask is not None:
                S.op("pool", lambda pi=pi, mask=mask: nc.gpsimd.tensor_tensor(
                    pT[:, pi, :].rearrange("p (h d) -> p h d", h=4), pT[:, pi, :].rearrange("p (h d) -> p h d", h=4),
                    cst[:, mask, :].unsqueeze(1).broadcast_to([128, 4, 128]), ALU.mult), r=[("pT", pi), "cst"], w=[("pT", pi)])
            S.op("pe", lambda pi=pi, slot=slot, ji=ji: nc.tensor.matmul(pb[4][:, :], aug[:, slot, 0:128], pT[:, pi, :],
                                                                       start=(ji == 0), stop=(ji == nk_tiles - 1)),
                 r=[("pT", pi), ("aug", slot)], w=[PB(4)])
            S.op("pe", lambda pi=pi, ji=ji: nc.tensor.matmul(pb[5][:, :], onesb[:], pT[:, pi, :],
                                                            start=(ji == 0), stop=(ji == nk_tiles - 1)),
                 r=[("pT", pi), "onesb"], w=[PB(5)])
            if side is not None:
                next(side, None)
        S.op("dve", lambda: nc.vector.reciprocal(rsum[:], pb[5][:, :]), r=[PB(5)], w=["rsum"])
        S.op("dve", lambda: nc.vector.tensor_tensor(olatT[:].rearrange("p a b -> p (a b)"), pb[4][:, :], rsum[:], ALU.mult), r=[PB(4), "rsum"], w=["olatT"])
        b6 = nb()
        my3 = pb[b6][:, :].rearrange("p (h d) -> p h d", h=4)
        for h in range(4):
            S.op("pe", lambda h=h: nc.tensor.matmul(my3[:, h, :], olatT[:, h, :], wukv[:, h * 256 + 128:h * 256 + 256], start=True, stop=True),
                 r=["olatT", "wukv"], w=[PB(b6)])
        S.op("act", lambda: nc.scalar.copy(ymix[:, pos, 512:1024], pb[b6][:, :]), r=[PB(b6)], w=[("ymix", pos)])

    def mix_out_block(tiles):
        npos = len(tiles)
        for pos in range(npos):
            b = nb()
            pbb = pb[b][:].bitcast(BF16)
            for k in range(8):
                S.op("pe", lambda k=k, pbb=pbb, pos=pos: nc.tensor.transpose(pbb[:, k * 128:(k + 1) * 128], ymix[:, pos, k * 128:(k + 1) * 128], identb[:]),
                     r=[("ymix", pos), "identb"], w=[PB(b)])
            evac(xT[:, :, pos * 128:(pos + 1) * 128], pbb[:, :].rearrange("p (a b) -> p a b", a=8), r=[PB(b)], w=[("xT", pos)])
        slots = [ws_next(("wmo", 0)), ws_next(("wmo", 1), hold=1)]
        for pos in range(npos):
            rb = pos % 2
            for hf in range(2):
                bk = nb()
                wv = ring[:, slots[hf], :].rearrange("p (k n) -> p k n", k=8)
                for k in range(8):
                    S.op("pe", lambda bk=bk, k=k, pos=pos, wv=wv: nc.tensor.matmul(pb[bk][:, :], xT[:, k, pos * 128:(pos + 1) * 128], wv[:, k, :],
                                                                                start=(k == 0), stop=(k == 7)),
                         r=[("ring", slots[hf]), ("xT", pos)], w=[PB(bk)])
                S.op("dve", lambda bk=bk, hf=hf, pos=pos, rb=rb: nc.vector.scalar_tensor_tensor(
                    rbuf[:, rb, hf * 512:(hf + 1) * 512], pb[bk][:, :], 1.0 / ALPHA, xres[:, pos, hf * 512:(hf + 1) * 512], ALU.mult, ALU.add),
                    r=[PB(bk), ("xres", pos)], w=[("rbuf", rb)])
            layer_norm(rb, 2, xres[:, pos, :], [("xres", pos)])

    def load_cache():
        for s in range(2):
            S.dma("sp", lambda s=s: nc.sync.dma_start(out=ctmp[:], in_=cache_ckv[s].rearrange("(i p) d -> p i d", p=128)), w=YM, key="cache")
            for dup in range(2):
                S.dma("sp", lambda s=s, dup=dup: nc.sync.dma_start(out=ktmp[:, :, dup * 64:(dup + 1) * 64], in_=cache_kpe[s].rearrange("(i p) d -> p i d", p=128)), w=YM, key="cache", join=True)
            S.op("act", lambda s=s: nc.scalar.copy(aug[:, 1 + 8 * s:9 + 8 * s, 0:128], ctmp[:]), r=YM, w=[("aug", 1 + 8 * s + i) for i in range(8)])
            for hb in range(2):
                b = nb()
                for i in range(4):
                    S.op("pe", lambda i=i, hb=hb, b=b: nc.tensor.transpose(pb[b][:, i * 128:(i + 1) * 128], ctmp[:, hb * 4 + i, :], ident_f),
                         r=YM + ["cst"], w=[PB(b)])
                c0 = (1 + 8 * s + 4 * hb) * 128
                evac(ckvT[:, c0:c0 + 512], pb[b][:, :], r=[PB(b)], w=[("ckvT", 1 + 8 * s + 4 * hb + i) for i in range(4)])
                b = nb()
                for i in range(4):
                    S.op("pe", lambda i=i, hb=hb, b=b: nc.tensor.transpose(pb[b][:, i * 128:(i + 1) * 128], ktmp[:, hb * 4 + i, :], ident_f),
                         r=YM + ["cst"], w=[PB(b)])
                evac(kpeT[:, c0:c0 + 512], pb[b][:, :], r=[PB(b)], w=[("kpeT", 1 + 8 * s + 4 * hb + i) for i in range(4)])
            S.dma("sp", lambda s=s: nc.sync.dma_start(out=Ss[s][:], in_=state_in[s].rearrange("h p d -> p h d")), w=["Ss%d" % s], key="cache2", join=(s == 1))
        for s in range(2):
            S.op("act", lambda s=s: nc.scalar.copy(Ssbf[s][:], Ss[s][:]), r=["Ss%d" % s], w=["Ss%dbf" % s])

    ycnt = [0]
    try:
        for bi, (kind, tiles) in enumerate(blocks):
            npos = len(tiles)
            if STAGE < 3:
                raise _Stop()
            if kind == "ctx" and STAGE < 8:
                raise _Stop()
            if kind == "own" and STAGE < 9:
                raise _Stop()
            if STAGE < 99 and bi >= 6 + max(0, STAGE - 9):
                raise _Stop()
            for pos, t in enumerate(tiles):
                S.dma("sp", lambda pos=pos, t=t: nc.sync.dma_start(out=xres[:, pos, :], in_=x_all[t]), w=[("xres", pos)], key=("xl", pos))
            ffn(0, npos, [(xres[:, pos, :], [("xres", pos)], None) for pos in range(npos)])
            if kind == "b0":
                if DBG:
                    S.dma("pool", lambda: nc.gpsimd.dma_start(out=dbg[:, 0:1024], in_=xres[:, 1, :]), r=[("xres", 1)], key="dbg0")
                if STAGE < 4:
                    raise _Stop()
                load_cache()
            mix_in_block(kind, tiles)
            if kind == "b0":
                if STAGE < 5:
                    raise _Stop()
                state_update(0, 16, 1, Sst, Sbf, "S")
                S.op("dve", lambda: nc.vector.tensor_copy(Smeta[:], Sst[:]), r=["S"], w=["Smeta"])
                info = tile_info(1)
                for _ in retention_tile(1, info):
                    pass
                if STAGE < 6:
                    raise _Stop()
                keylist = [(0, 2, None)] + [(1 + i, None, 5) for i in range(8)] + [(9 + i, None, 6) for i in range(8)] + [(17, None, 4)]
                for _ in mla_pre(1, info, 0):
                    pass
                mla_attn(1, info, keylist, 0)
                if DBG:
                    S.dma("pool", lambda: nc.gpsimd.dma_start(out=dbg[:, 1024:2048], in_=ymix[:, 1, :]), r=[("ymix", 1)], key="dbg1")
                if STAGE < 7:
                    raise _Stop()
                S.op("pool", lambda: nc.gpsimd.memset(ymix[:, 0, :], 0.0), w=[("ymix", 0)])
                mix_out_block(tiles)
                if DBG:
                    S.dma("pool", lambda: nc.gpsimd.dma_start(out=dbg[:, 2048:3072], in_=xres[:, 1, :]), r=[("xres", 1)], key="dbg2")
                for s in range(2):
                    S.dma("pool", lambda s=s: nc.gpsimd.dma_start(out=state_s[s].rearrange("h p d -> p h d"), in_=Ss[s][:]), r=["Ss%d" % s], key="o_state", join=(s == 1))

                def after_s(i):
                    def f():
                        S.dma("pool", lambda: nc.gpsimd.dma_start(out=y_samp, in_=ystage[0:64, i, :]), r=[("ystage", i)], key=("o_y", i))
                    return f
                dsts = []
                for pos in range(npos):
                    i = 0
                    ycnt[0] += 1
                    dsts.append((ystage[:, i, :], [("ystage", i)], after_s(i) if pos == 1 else None))
                ffn(1, npos, dsts)
            elif kind == "ctx":
                for pos, t in enumerate(tiles):
                    state_update(pos, 128, 0, Sst, Sbf, "S")
                if tiles[-1] == 17:
                    S.op("dve", lambda: nc.vector.tensor_tensor(Tt[:], Sst[:], Smeta[:], ALU.subtract), r=["S", "Smeta"], w=["Tt"])
                    S.op("dve", lambda: nc.vector.scalar_tensor_tensor(Sst[:].rearrange("p a b -> p (a b)"), Tt[:].rearrange("p a b -> p (a b)"), flg[:, 0:1],
                                                                       Smeta[:].rearrange("p a b -> p (a b)"), ALU.mult, ALU.add), r=["Tt", "Smeta", "flg"], w=["S"])
                    S.op("act", lambda: nc.scalar.copy(Sbf[:], Sst[:]), r=["S"], w=["Sbf"])
            else:
                def pre_gen(pos, t, par):
                    info = tile_info(t)
                    yield from retention_tile(pos, info)
                    yield from mla_pre(pos, info, par)

                def drain(g):
                    if g is not None:
                        for _ in g:
                            pass
                drain(pre_gen(0, tiles[0], 0))
                for pos, t in enumerate(tiles):
                    info = tile_info(t)
                    j = info["j"]
                    keylist = [(0, 2, None)] + [(1 + i, 1, None) for i in range(16)] + [(17 + i, None, None) for i in range(j)] + \
                              [(17 + j, None, 2)]
                    side = pre_gen(pos + 1, tiles[pos + 1], (pos + 1) % 2) if pos + 1 < len(tiles) else None
                    mla_attn(pos, info, keylist, pos % 2, side)
                    drain(side)
                mix_out_block(tiles)

                def after_o(i, j):
                    def f():
                        S.dma("pool", lambda: nc.gpsimd.dma_start(out=y_own[j * 128:(j + 1) * 128, :], in_=ystage[:, i, :]), r=[("ystage", i)], key=("o_y", i))
                    return f
                dsts = []
                for pos, t in enumerate(tiles):
                    i = 0
                    ycnt[0] += 1
                    dsts.append((ystage[:, i, :], [("ystage", i)], after_o(i, tile_info(t)["j"])))
                ffn(1, npos, dsts)
        S.dma("pool", lambda: nc.gpsimd.dma_start(out=state_p.rearrange("h p d -> p h d"), in_=Sst[:]), r=["S"], key="o_state2")

    except _Stop:
        pass
    stats = S.emit(es)
    es.close()
    return nc, stats


def _rope_tab(pos, d):
    inv = 10000.0 ** (-np.arange(0, d, 2, dtype=np.float64) / d)
    ang = pos.astype(np.float64)[:, None] * inv[None, :]
    c, s = np.cos(ang), np.sin(ang)
    return np.concatenate([c, c], 1), np.concatenate([-s, s], 1)


def _host_constants(r):
    pos = np.zeros((NT, 128), np.float64)
    pos[0] = np.arange(128)
    pos[1] = 16 + 1024 + (np.arange(128) % 32)
    for j in range(16):
        pos[2 + j] = 16 + j * 128 + np.arange(128)
        pos[18 + j] = 16 + r * 2048 + j * 128 + np.arange(128)
    tab = np.zeros((NT, 128, 384), np.float32)
    for t in range(NT):
        ccr, ssr = _rope_tab(pos[t], 128)
        ccm, ssm = _rope_tab(pos[t], 64)
        tab[t, :, 0:128] = ccr; tab[t, :, 128:256] = ssr
        tab[t, :, 256:320] = ccm; tab[t, :, 320:384] = ssm
    k = np.arange(128)[:, None]; q = np.arange(128)[None, :]
    consts = np.zeros((128, 7, 128), np.float32)
    consts[:, 0] = np.eye(128)
    consts[:, 1] = (q >= k)
    consts[:, 2] = 1.0 - ((k >= 64) & (q < 64))
    same = (k // 32 == q // 32) & (k < 64) & (q < 64)
    consts[:, 3] = same & (q >= k)
    consts[:, 4] = same
    consts[:, 5] = np.broadcast_to(q < 32, (128, 128))
    consts[:, 6] = np.broadcast_to((q >= 32) & (q < 64), (128, 128))
    gam = 1.0 - 2.0 ** (-5.0 - np.arange(4, dtype=np.float64))
    dect = np.zeros((128, 3, 3, 4), np.float64)
    row = np.arange(128, dtype=np.float64)
    for typ, (idx, C) in enumerate(((row, 128), (row, 16), (row % 32, 32))):
        dect[:, typ, 0, :] = gam[None, :] ** (idx[:, None] + 1.0)
        dect[:, typ, 1, :] = (128.0 ** -0.5) * gam[None, :] ** (-idx[:, None] - 1.0)
        dect[:, typ, 2, :] = gam[None, :] ** C
    flg = np.zeros((128, 4), np.float32)
    flg[16:, 2] = -30000.0
    flg[:, 0] = float(r)
    flg[:, 1] = 0.0 if r == 1 else -30000.0
    return tab, consts, dect.reshape(128, 36).astype(np.float32), flg


_CACHE = {}


def kernel(x_prompt, x_sample, cache_mla_ckv, cache_mla_kpe, state_ret, meta_tokens,
           ffn1_w_in, ffn1_w_out, ln1_g, ln1_b, w_mix_in, ret_gn_g, mla_q_norm_g, mla_w_uq,
           mla_kv_norm_g, mla_w_ukv, w_mix_out, ln2_g, ln2_b, ffn2_w_in, ffn2_w_out, ln3_g, ln3_b):
    f32 = lambda a: np.ascontiguousarray(np.asarray(a, dtype=np.float32))
    x_prompt = f32(x_prompt); x_sample = f32(x_sample)
    if "nc" not in _CACHE:
        _CACHE["nc"] = build_program()
    nc, stats = _CACHE["nc"]
    lnvec = np.stack([f32(ln1_g), f32(ln1_b), f32(ln2_g), f32(ln2_b), f32(ln3_g), f32(ln3_b)], 0)
    in_maps = []
    for c in range(8):
        b, r = c // 2, c % 2
        x_all = np.zeros((NT, 128, D), np.float32)
        x_all[0, 0:16] = f32(meta_tokens)
        x_all[1, 0:32] = x_sample[2 * c]
        x_all[1, 32:64] = x_sample[2 * c + 1]
        x_all[2:18] = x_prompt[b, 0:2048].reshape(16, 128, D)
        x_all[18:34] = x_prompt[b, r * 2048:(r + 1) * 2048].reshape(16, 128, D)
        tab, consts, dect, flg = _host_constants(r)
        in_maps.append(dict(
            x_all=x_all, tab_all=tab,
            cache_ckv=f32(cache_mla_ckv[2 * c:2 * c + 2]), cache_kpe=f32(cache_mla_kpe[2 * c:2 * c + 2]),
            state_in=f32(state_ret[2 * c:2 * c + 2]),
            ffn1_w_in=f32(ffn1_w_in), ffn2_w_in=f32(ffn2_w_in), ffn1_w_out=f32(ffn1_w_out), ffn2_w_out=f32(ffn2_w_out),
            w_mix_in=f32(w_mix_in), w_mix_out=f32(w_mix_out), mla_w_uq=f32(mla_w_uq), mla_w_ukv=f32(mla_w_ukv),
            lnvec=lnvec, ret_gn_g=f32(ret_gn_g), mla_q_norm_g=f32(mla_q_norm_g), mla_kv_norm_g=f32(mla_kv_norm_g),
            consts=consts, dect=dect, flg=flg))
    res = run_bass_kernel_spmd(nc, in_maps, core_ids=list(range(8)))
    R = res.results
    _CACHE["last"] = R
    y_prompt = np.zeros((4, 4096, D), np.float32)
    y_sample = np.zeros((16, 32, D), np.float32)
    p_ckv = np.zeros((4, 4112, 128), np.float32)
    p_kpe = np.zeros((4, 4112, 64), np.float32)
    p_state = np.zeros((4, 4, 128, 128), np.float32)
    s_ckv = np.zeros((16, 32, 128), np.float32)
    s_kpe = np.zeros((16, 32, 64), np.float32)
    s_state = np.zeros((16, 4, 128, 128), np.float32)
    for c in range(8):
        b, r = c // 2, c % 2
        o = R[c]
        y_prompt[b, r * 2048:(r + 1) * 2048] = o["y_own"]
        p_ckv[b, 16 + r * 2048:16 + (r + 1) * 2048] = o["ckv_own"]
        p_kpe[b, 16 + r * 2048:16 + (r + 1) * 2048] = o["kpe_own"]
        if r == 0:
            p_ckv[b, 0:16] = o["ckv_meta"]
            p_kpe[b, 0:16] = o["kpe_meta"]
        else:
            p_state[b] = o["state_p"]
        y_sample[2 * c:2 * c + 2] = o["y_samp"].reshape(2, 32, D)
        s_ckv[2 * c:2 * c + 2] = o["ckv_samp"].reshape(2, 32, 128)
        s_kpe[2 * c:2 * c + 2] = o["kpe_samp"].reshape(2, 32, 64)
        s_state[2 * c:2 * c + 2] = o["state_s"]
    return (y_prompt, y_sample, p_ckv, p_kpe, p_state, s_ckv, s_kpe, s_state)
```

```python
import contextlib
import os
import numpy as np
import concourse.bass as bass
import concourse.mybir as mybir
from concourse.bass_utils import run_bass_kernel_spmd

F32 = mybir.dt.float32
BF16 = mybir.dt.bfloat16
AF = mybir.ActivationFunctionType
ALU = mybir.AluOpType

COMPUTE = ("pe", "act", "dve", "pool")

D = 1024
DFF = 2816
NCH = 22
ALPHA = 2.0 ** 0.25
LN_EPS = 1e-5
RMS_EPS = 1e-6
NT = 34
NSLOT = 33
AUGW = 132
SCALE = 192.0 ** -0.5
NS = 3
STAGE = int(os.environ.get("KSTAGE", "99"))
DBG = bool(int(os.environ.get("KDBG", "0")))


class _Stop(Exception):
    pass


class _Op:
    __slots__ = ("eng", "fn", "deps", "is_dma", "grp", "need_inc", "val")


class Sched:
    def __init__(self, nc):
        self.nc = nc
        self.ops = []
        self.bufs = {}
        self.eng = {"pe": nc.tensor, "act": nc.scalar, "dve": nc.vector,
                    "pool": nc.gpsimd, "sp": nc.sync}
        self.cur_grp = {}

    def _deps_for(self, op, r, w):
        deps = []
        for k in r:
            st = self.bufs.get(k)
            if st is None:
                st = self.bufs[k] = [None, {}]
            if st[0] is not None:
                deps.append(st[0])
            if isinstance(k, tuple) and k[0] == "pb":
                for ek, ro in st[1].items():
                    if ek != op.eng:
                        deps.append(ro)
        for k in w:
            st = self.bufs.get(k)
            if st is None:
                st = self.bufs[k] = [None, {}]
            if st[0] is not None:
                deps.append(st[0])
            deps.extend(st[1].values())
        for k in r:
            st = self.bufs[k]
            key = ("dma", id(op)) if op.is_dma else op.eng
            st[1][key] = op
        for k in w:
            st = self.bufs[k]
            st[0] = op
            st[1] = {}
        return deps

    def op(self, eng, fn, r=(), w=()):
        o = _Op()
        o.eng = eng; o.fn = fn; o.is_dma = False; o.grp = None
        o.need_inc = False; o.val = None
        o.deps = self._deps_for(o, r, w)
        self.ops.append(o)
        return o

    def dma(self, q, fn, r=(), w=(), key=None, join=False):
        o = _Op()
        o.eng = q; o.fn = fn; o.is_dma = True
        o.need_inc = True; o.val = None
        if join and key in self.cur_grp:
            self.cur_grp[key].append(o)
        else:
            self.cur_grp[key] = [o]
        o.grp = (key, self.cur_grp[key])
        o.deps = self._deps_for(o, r, w)
        self.ops.append(o)
        return o

    def emit(self, stack, final_wait_eng="sp"):
        nc = self.nc
        ops = self.ops
        for o in ops:
            for d in o.deps:
                if not d.is_dma:
                    if d.eng == "pe" and o.eng == "pe" and not o.is_dma:
                        continue
                    d.need_inc = True
        cnt = {e: 0 for e in COMPUTE}
        dcnt = {}
        seen = set()
        for o in ops:
            if o.is_dma:
                key, members = o.grp
                if id(members) not in seen:
                    seen.add(id(members))
                    final = dcnt.get(key, 0) + 16 * len(members)
                    dcnt[key] = final
                    for m in members:
                        m.val = final
            elif o.need_inc:
                cnt[o.eng] += 1
                o.val = cnt[o.eng]
        sems = {}
        for e in COMPUTE:
            sems[e] = stack.enter_context(nc.semaphore("s_" + e))
        for i, key in enumerate(dcnt):
            sems[key] = stack.enter_context(nc.semaphore("d%d" % i))
        waited = {e: {} for e in self.eng}
        nwait = 0
        for o in ops:
            E = self.eng[o.eng]
            need = {}
            for d in o.deps:
                if d.is_dma:
                    if o.is_dma and d.grp[1] is o.grp[1]:
                        continue
                    sk = d.grp[0]
                else:
                    if d.eng == "pe" and o.eng == "pe" and not o.is_dma:
                        continue
                    sk = d.eng
                if need.get(sk, 0) < d.val:
                    need[sk] = d.val
            for sk, v in need.items():
                if waited[o.eng].get(sk, 0) < v:
                    E.wait_ge(sems[sk], v)
                    waited[o.eng][sk] = v
                    nwait += 1
            ins = o.fn()
            if o.is_dma:
                ins.then_inc(sems[o.grp[0]], 16)
            elif o.need_inc:
                ins.then_inc(sems[o.eng], 1)
        E = self.eng[final_wait_eng]
        for key, v in dcnt.items():
            if waited[final_wait_eng].get(key, 0) < v:
                E.wait_ge(sems[key], v)
        for e in COMPUTE:
            if cnt[e] > 0:
                E.wait_ge(sems[e], cnt[e])
        return dict(n_ops=len(ops), n_wait=nwait, cnt=cnt, n_dsem=len(dcnt))


def build_program():
    nc = bass.Bass("TRN2", target_bir_lowering=False)
    es = contextlib.ExitStack()
    S = Sched(nc)

    def din(name, shape, dt=F32):
        return nc.dram_tensor(name, shape, dt, kind="ExternalInput").ap()

    def dout(name, shape):
        return nc.dram_tensor(name, shape, F32, kind="ExternalOutput").ap()

    def dscr(name, shape, dt=BF16):
        return nc.dram_tensor(name, shape, dt, kind="Internal").ap()

    def sb(name, shape, dt=F32):
        return es.enter_context(nc.sbuf_tensor("sb_" + name, shape, dt))

    x_all = din("x_all", [NT, 128, D])
    tab_all = din("tab_all", [NT, 128, 384])
    cache_ckv = din("cache_ckv", [2, 1024, 128])
    cache_kpe = din("cache_kpe", [2, 1024, 64])
    state_in = din("state_in", [2, 4, 128, 128])
    w_in = [din("ffn1_w_in", [D, 2 * DFF]), din("ffn2_w_in", [D, 2 * DFF])]
    w_out = [din("ffn1_w_out", [DFF, D]), din("ffn2_w_out", [DFF, D])]
    w_mix_in = din("w_mix_in", [D, 2496])
    w_mix_out = din("w_mix_out", [D, D])
    w_uq = din("mla_w_uq", [256, 768])
    w_ukv = din("mla_w_ukv", [128, 1024])
    lnvec = din("lnvec", [6, D])
    gn_g = din("ret_gn_g", [512])
    gq_g = din("mla_q_norm_g", [256])
    gkv_g = din("mla_kv_norm_g", [128])
    consts = din("consts", [128, 7, 128])
    dect_in = din("dect", [128, 36])
    flg_in = din("flg", [128, 4])

    y_own = dout("y_own", [2048, D])
    y_samp = dout("y_samp", [64, D])
    ckv_own = dout("ckv_own", [2048, 128])
    kpe_own = dout("kpe_own", [2048, 64])
    ckv_meta = dout("ckv_meta", [16, 128])
    kpe_meta = dout("kpe_meta", [16, 64])
    ckv_samp = dout("ckv_samp", [64, 128])
    kpe_samp = dout("kpe_samp", [64, 64])
    state_p = dout("state_p", [4, 128, 128])
    state_s = dout("state_s", [2, 4, 128, 128])
    dbg = dout("dbg", [128, 4096]) if DBG else None

    winb = [dscr("winb%d" % f, [11, 128, 8, 2, 256]) for f in range(2)]
    woutb = [dscr("woutb%d" % f, [6, 128, 4, D]) for f in range(2)]
    wmib = dscr("wmib", [5, 128, 8, 512])
    wmob = dscr("wmob", [2, 128, 8, 512])

    cst = sb("cst", [128, 7, 128])
    ident_f = cst[:, 0, :]
    identb = sb("identb", [128, 128], BF16)
    dect = sb("dect", [128, 3, 3, 4])
    flg = sb("flg", [128, 4])
    lnv = sb("lnv", [128, 6, D])
    gng = sb("gng", [128, 512])
    gqg = sb("gqg", [128, 256])
    gkvg = sb("gkvg", [128, 128])
    wuq = sb("wuq", [128, 2, 768], BF16)
    wukv = sb("wukv", [128, 1024], BF16)
    wukT = sb("wukT", [128, 4, 128], BF16)
    ckvT = sb("ckvT", [128, NSLOT * 128], BF16)
    kpeT = sb("kpeT", [128, NSLOT * 128], BF16)
    aug = sb("aug", [128, NSLOT, AUGW], BF16)
    Sst = sb("Sst", [128, 4, 128])
    Sbf = sb("Sbf", [128, 4, 128], BF16)
    Smeta = sb("Smeta", [128, 4, 128])
    Ss = [sb("Ss%d" % s, [128, 4, 128]) for s in range(2)]
    Ssbf = [sb("Ssbf%d" % s, [128, 4, 128], BF16) for s in range(2)]
    Tt = sb("Tt", [128, 4, 128])
    ring = sb("ring", [128, NS, 4096], BF16)
    xT = sb("xT", [128, 8, 512], BF16)
    xres = sb("xres", [128, 4, D])
    ovl = sb("ovl", [128, 11264], BF16)
    hid = ovl[:, :].rearrange("p (c t) -> p c t", c=NCH)
    qd = ovl[:, 0:2048].rearrange("p (a b) -> p a b", a=4)
    kd = ovl[:, 2048:4096].rearrange("p (a b) -> p a b", a=4)
    vv = ovl[:, 4096:6144].rearrange("p (a b) -> p a b", a=4)
    sgr = ovl[:, 6144:8192].rearrange("p (a b) -> p a b", a=4)
    lat = ovl[:, 8192:8192 + 2 * 4 * 320].bitcast(F32).rearrange("p (a b) -> p a b", a=4)
    cqb = sb("cqb", [128, 4, 256])
    ymix = sb("ymix", [128, 4, D], BF16)
    cqnT = sb("cqnT", [128, 4, 2, 128], BF16)
    tabs = sb("tabs", [128, 4, 384])
    rbuf = sb("rbuf", [128, 2, D])
    ystage = sb("ystage", [128, 1, D])
    sgt = sb("sgt", [128, 2, 512])
    t1 = sb("t1", [128, 512])
    t2 = sb("t2", [128, 512])
    lst = sb("lst", [128, 2, 2, 6])
    lmv = sb("lmv", [128, 2, 4])
    sq = sb("sq", [128, 256])
    rms = sb("rms", [128, 8])
    ckvn = sb("ckvn", [128, 2, 128])
    kper = sb("kper", [128, 2, 128])
    cqn = sb("cqn", [128, 256])
    qT = sb("qT", [128, 4, 128], BF16)
    kT = sb("kT", [128, 4, 128], BF16)
    scm = sb("scm", [128, 4, 128], BF16)
    gst = sb("gst", [128, 4, 6])
    gmv = sb("gmv", [128, 4, 2])
    gsm = sb("gsm", [128, 12])
    on = sb("on", [128, 512])
    qnT = sb("qnT", [128, 4, 128], BF16)
    qabsT2 = sb("qabsT", [128, 2, 4, 128], BF16)
    qrr = sb("qrr", [128, 4, 64])
    qrT2 = sb("qrT", [128, 2, 4, 128], BF16)
    onesb = sb("onesb", [128, 128], BF16)
    rsum = sb("rsum", [128, 512])
    qTm = rsum[:].bitcast(BF16).rearrange("p (s h d) -> p s h d", s=2, h=4)
    pT = sb("pT", [128, 2, 512], BF16)
    olatT = sb("olatT", [128, 4, 128], BF16)
    ymf = ymix[:, :, :].rearrange("p a b -> p (a b)").bitcast(F32)
    ctmp = ymf[:, 0:1024].rearrange("p (a b) -> p a b", a=8)
    ktmp = ymf[:, 1024:2048].rearrange("p (a b) -> p a b", a=8)
    pb = [es.enter_context(nc.psum_tensor("pb%d" % i, [128, 512], F32)) for i in range(8)]

    def PB(i):
        return ("pb", i)

    YM = [("ymix", p) for p in range(4)]

    def OV(c):
        return ("ov", c)

    def Kq(pos):
        return OV(pos)

    def Kk(pos):
        return OV(4 + pos)

    def Kv(pos):
        return OV(8 + pos)

    def Kg(pos):
        return OV(12 + pos)

    def Kl(pos):
        return [OV(16 + (640 * pos) // 512), OV(16 + (640 * pos + 639) // 512)]

    misc_banks = [0, 1, 6, 7]
    misc_ctr = [0]

    def nb():
        b = misc_banks[misc_ctr[0] % len(misc_banks)]
        misc_ctr[0] += 1
        return b

    evac_ctr = [0]

    def evac(out_ap, in_ap, r, w):
        evac_ctr[0] += 1
        if evac_ctr[0] % 2:
            S.op("act", lambda: nc.scalar.copy(out_ap, in_ap), r=r, w=w)
        else:
            S.op("dve", lambda: nc.vector.tensor_copy(out_ap, in_ap), r=r, w=w)

    S.dma("sp", lambda: nc.sync.dma_start(out=cst[:], in_=consts), w=["cst"], key="c0")
    S.dma("sp", lambda: nc.sync.dma_start(out=dect[:].rearrange("p a b c -> p (a b c)"), in_=dect_in), w=["dect"], key="c0", join=True)
    S.dma("sp", lambda: nc.sync.dma_start(out=flg[:], in_=flg_in), w=["flg"], key="c0", join=True)
    S.dma("sp", lambda: nc.sync.dma_start(out=gng[:], in_=gn_g.partition_broadcast(128)), w=["gng"], key="c0", join=True)
    S.dma("sp", lambda: nc.sync.dma_start(out=gqg[:], in_=gq_g.partition_broadcast(128)), w=["gqg"], key="c0", join=True)
    S.dma("sp", lambda: nc.sync.dma_start(out=gkvg[:], in_=gkv_g.partition_broadcast(128)), w=["gkvg"], key="c0", join=True)
    for i in range(6):
        S.dma("sp", lambda i=i: nc.sync.dma_start(out=lnv[:, i, :], in_=lnvec[i].partition_broadcast(128)), w=["lnv"], key="c0", join=True)

    if STAGE >= 1:
        for hq in range(11):
            for gu in range(2):
                src = w_in[0][:, gu * DFF + hq * 256: gu * DFF + (hq + 1) * 256].rearrange("(k p) n -> p k n", p=128)
                dst = winb[0][hq, :, :, gu, :]
                S.dma("pool", lambda src=src, dst=dst: nc.gpsimd.dma_start(out=dst, in_=src),
                      w=[("winb", 0, hq)], key=("pc_win", hq), join=(gu == 1))
        S.dma("pool", lambda: nc.gpsimd.dma_start(out=wuq[:], in_=w_uq.rearrange("(k p) n -> p k n", p=128)), w=["wuq"], key="pc_small")
        S.dma("pool", lambda: nc.gpsimd.dma_start(out=wukv[:], in_=w_ukv), w=["wukv"], key="pc_small", join=True)
        S.dma("sp", lambda: nc.sync.dma_start(out=ctmp[:].rearrange("p a b -> p (a b)"), in_=w_ukv), w=YM, key="c1")
        for pc in range(6):
            ncc = 4 if pc < 5 else 2
            src = w_out[0][pc * 512: pc * 512 + ncc * 128, :].rearrange("(c p) n -> p c n", p=128)
            dst = woutb[0][pc, :, 0:ncc, :]
            S.dma("pool", lambda src=src, dst=dst: nc.gpsimd.dma_start(out=dst, in_=src),
                  w=[("woutb", 0, pc)], key=("pc_wout", pc))
        for g in range(5):
            n = 512 if g < 4 else 448
            src = w_mix_in[:, g * 512: g * 512 + n].rearrange("(k p) n -> p k n", p=128)
            dst = wmib[g, :, :, 0:n]
            S.dma("pool", lambda src=src, dst=dst: nc.gpsimd.dma_start(out=dst, in_=src), w=[("wmib", g)], key=("pc_wmi", g))
        for hf in range(2):
            src = w_mix_out[:, hf * 512:(hf + 1) * 512].rearrange("(k p) n -> p k n", p=128)
            dst = wmob[hf]
            S.dma("pool", lambda src=src, dst=dst: nc.gpsimd.dma_start(out=dst, in_=src), w=[("wmob", hf)], key="pc_wmo", join=(hf == 1))
        first = True
        for hq in range(11):
            for gu in range(2):
                src = w_in[1][:, gu * DFF + hq * 256: gu * DFF + (hq + 1) * 256].rearrange("(k p) n -> p k n", p=128)
                dst = winb[1][hq, :, :, gu, :]
                S.dma("pool", lambda src=src, dst=dst: nc.gpsimd.dma_start(out=dst, in_=src),
                      w=[("winb", 1, hq)], key="pc2", join=not first)
                first = False
        for pc in range(6):
            ncc = 4 if pc < 5 else 2
            src = w_out[1][pc * 512: pc * 512 + ncc * 128, :].rearrange("(c p) n -> p c n", p=128)
            dst = woutb[1][pc, :, 0:ncc, :]
            S.dma("pool", lambda src=src, dst=dst: nc.gpsimd.dma_start(out=dst, in_=src),
                  w=[("woutb", 1, pc)], key="pc2", join=True)

    if STAGE >= 2:
        S.op("dve", lambda: nc.vector.tensor_copy(identb[:], cst[:, 0, :]), r=["cst"], w=["identb"])
        S.op("pool", lambda: nc.gpsimd.memset(aug[:, :, 128:129], 1.0), w=[("aug", s) for s in range(NSLOT)])
        S.op("pool", lambda: nc.gpsimd.memset(Sst[:], 0.0), w=["S"])
        S.op("pool", lambda: nc.gpsimd.memset(qrT2[:], 0.0), w=[("qrT", 0), ("qrT", 1)])
        S.op("pool", lambda: nc.gpsimd.memset(onesb[:], 1.0), w=["onesb"])
        b = nb()
        for h in range(4):
            S.op("pe", lambda h=h, b=b: nc.tensor.transpose(pb[b][:, h * 128:(h + 1) * 128], ctmp[:, 2 * h, :], ident_f),
                 r=YM + ["cst"], w=[PB(b)])
        evac(wukT[:], pb[b][:].rearrange("p (a b) -> p a b", a=4), r=[PB(b)], w=["wukT"])

    blocks = [("b0", [0, 1])] + [("ctx", list(range(2 + 4 * i, 6 + 4 * i))) for i in range(4)] + \
             [("own", list(range(18 + 4 * i, 22 + 4 * i))) for i in range(4)]
    plan = []
    for kind, tiles in blocks:
        nrep = 1 if len(tiles) <= 2 else 2
        plan += [("win", 0, hq) for hq in range(11)] + [("wout", 0, pc) for pc in range(6)] * nrep
        plan += [("wmi", g) for g in ([0, 1, 2, 3, 4] if kind != "ctx" else [1, 2, 4])]
        if kind != "ctx":
            plan += [("wmo", 0), ("wmo", 1)]
            plan += [("win", 1, hq) for hq in range(11)] + [("wout", 1, pc) for pc in range(6)] * nrep
    ws_state = dict(next_load=0, next_use=0)

    def ws_issue(i):
        p = plan[i]
        slot = i % NS
        dst = ring[:, slot, 0:4096]
        if p[0] == "win":
            src = winb[p[1]][p[2]].rearrange("p k g n -> p (k g n)")
            key = ("winb", p[1], p[2])
        elif p[0] == "wout":
            ncc = 4 if p[2] < 5 else 2
            src = woutb[p[1]][p[2]][:, 0:ncc, :].rearrange("p c n -> p (c n)")
            dst = ring[:, slot, 0:ncc * 1024]
            key = ("woutb", p[1], p[2])
        elif p[0] == "wmi":
            n = 512 if p[1] < 4 else 448
            src = wmib[p[1]][:, :, 0:n]
            dst = ring[:, slot, :].rearrange("p (k n) -> p k n", k=8)[:, :, 0:n]
            key = ("wmib", p[1])
        else:
            src = wmob[p[1]].rearrange("p k n -> p (k n)")
            key = ("wmob", p[1])
        S.dma("sp", lambda: nc.sync.dma_start(out=dst, in_=src), r=[key], w=[("ring", slot)], key=("ring", slot))

    def ws_next(expect, hold=0):
        i = ws_state["next_use"]
        assert plan[i] == expect, (plan[i], expect)
        while ws_state["next_load"] < min(len(plan), i + NS - hold):
            ws_issue(ws_state["next_load"])
            ws_state["next_load"] += 1
        ws_state["next_use"] += 1
        return i % NS

    def make_xT(npos, positions=None):
        for pos in (range(npos) if positions is None else positions):
            for hb in range(2):
                bk = hb
                for kk in range(4):
                    k = hb * 4 + kk
                    S.op("pe", lambda pos=pos, k=k, kk=kk, bk=bk: nc.tensor.transpose(
                        pb[bk][:, kk * 128:(kk + 1) * 128], xres[:, pos, k * 128:(k + 1) * 128], ident_f),
                        r=[("xres", pos), "cst"], w=[PB(bk)])
                evac(xT[:, hb * 4:hb * 4 + 4, pos * 128:(pos + 1) * 128], pb[bk][:].rearrange("p (a b) -> p a b", a=4),
                     r=[PB(bk)], w=[("xT", pos)])

    ln_ctr = [0]

    def layer_norm(rb, gi, dst_ap, dst_keys):
        i = ln_ctr[0] % 2
        ln_ctr[0] += 1
        Rk = [("rbuf", rb, 0), ("rbuf", rb, 1)]
        for hf in range(2):
            S.op("dve", lambda hf=hf: nc.vector.bn_stats(lst[:, i, hf, :], rbuf[:, rb, hf * 512:(hf + 1) * 512]), r=[("rbuf", rb, hf)], w=[("lst", i, hf)])
        S.op("dve", lambda: nc.vector.bn_aggr(lmv[:, i, 0:2], lst[:, i, :, :].rearrange("p a b -> p (a b)")), r=[("lst", i, 0), ("lst", i, 1)], w=[("lmv", i)])
        S.op("act", lambda: nc.scalar.activation(lmv[:, i, 2:3], lmv[:, i, 1:2], AF.Sqrt, bias=LN_EPS / (ALPHA * ALPHA), scale=1.0),
             r=[("lmv", i)], w=[("lmv", i)])
        S.op("dve", lambda: nc.vector.reciprocal(lmv[:, i, 2:3], lmv[:, i, 2:3]), r=[("lmv", i)], w=[("lmv", i)])
        S.op("dve", lambda: nc.vector.tensor_scalar(lmv[:, i, 3:4], lmv[:, i, 0:1], lmv[:, i, 2:3], -1.0, ALU.mult, ALU.mult),
             r=[("lmv", i)], w=[("lmv", i)])
        S.op("act", lambda: nc.scalar.activation(rbuf[:, rb, :], rbuf[:, rb, :], AF.Identity, bias=lmv[:, i, 3:4], scale=lmv[:, i, 2:3]),
             r=Rk + [("lmv", i)], w=Rk)
        S.op("pool", lambda: nc.gpsimd.tensor_tensor(rbuf[:, rb, :], rbuf[:, rb, :], lnv[:, gi, :], ALU.mult), r=Rk + ["lnv"], w=Rk)
        S.op("pool", lambda: nc.gpsimd.tensor_tensor(dst_ap, rbuf[:, rb, :], lnv[:, gi + 1, :], ALU.add), r=Rk + ["lnv"], w=dst_keys)

    def rms_norm(src_ap, n, g_ap, dst_ap, r, w, col):
        S.op("act", lambda: nc.scalar.activation(sq[:, 0:n], src_ap, AF.Square, accum_out=rms[:, col:col + 1]), r=r, w=["sq", ("rms", col)])
        S.op("act", lambda: nc.scalar.activation(rms[:, col:col + 1], rms[:, col:col + 1], AF.Sqrt, bias=RMS_EPS, scale=1.0 / n),
             r=[("rms", col)], w=[("rms", col)])
        S.op("dve", lambda: nc.vector.reciprocal(rms[:, col:col + 1], rms[:, col:col + 1]), r=[("rms", col)], w=[("rms", col)])
        S.op("dve", lambda: nc.vector.scalar_tensor_tensor(dst_ap, src_ap, rms[:, col:col + 1], g_ap, ALU.mult, ALU.mult),
             r=list(r) + [("rms", col)], w=w)

    rope_ctr = [0]

    def rope_from(src3, H, Dh, cc_ap, ss_ap, out3, r, w, dec_ap=None):
        hh = Dh // 2
        n = H * Dh
        rope_ctr[0] += 1
        if rope_ctr[0] % 2:
            b1, b2, k1, k2a, k2b = t1, t2, "t1", "t2a", "t2b"
        else:
            b1, b2, k1, k2a, k2b = sgt[:, 0, :], sgt[:, 1, :], ("sgt", 0), ("sgt", 1), ("sgt", 1)
        a1 = b1[:, 0:n].rearrange("p (h d) -> p h d", h=H)
        a2 = b2[:, 0:n].rearrange("p (h d) -> p h d", h=H)
        ccb = cc_ap.unsqueeze(1).broadcast_to([128, H, Dh])
        S.op("dve", lambda: nc.vector.tensor_tensor(a1, src3, ccb, ALU.mult), r=r, w=[k1])
        S.op("dve", lambda: nc.vector.tensor_tensor(a2[:, :, 0:hh], src3[:, :, hh:Dh], ss_ap[:, 0:hh].unsqueeze(1).broadcast_to([128, H, hh]), ALU.mult),
             r=r, w=[k2a])
        S.op("dve", lambda: nc.vector.tensor_tensor(a2[:, :, hh:Dh], src3[:, :, 0:hh], ss_ap[:, hh:Dh].unsqueeze(1).broadcast_to([128, H, hh]), ALU.mult),
             r=r, w=[k2b])
        if dec_ap is None:
            S.op("pool", lambda: nc.gpsimd.tensor_tensor(out3, a1, a2, ALU.add), r=[k1, k2a, k2b], w=w)
        else:
            S.op("pool", lambda: nc.gpsimd.tensor_tensor(a1, a1, a2, ALU.add), r=[k2a, k2b], w=[k1])
            S.op("pool", lambda: nc.gpsimd.tensor_tensor(out3, a1, dec_ap.unsqueeze(2).broadcast_to([128, H, Dh]), ALU.mult), r=[k1, "dect"], w=w)

    def ffn(f, npos, dsts):
        T = npos * 128
        make_xT(npos)
        xkeys = [("xT", p) for p in range(npos)]
        hkeys = ["ovl"]
        for hq in range(11):
            slot = ws_next(("win", f, hq))
            wv = ring[:, slot, :].rearrange("p (k g n) -> p k g n", k=8, g=2)
            for cc in range(2):
                c = 2 * hq + cc
                idx = c % 2
                for gu in range(2):
                    bk = 2 * gu + idx
                    for k in range(8):
                        S.op("pe", lambda bk=bk, k=k, gu=gu, cc=cc, wv=wv: nc.tensor.matmul(
                            pb[bk][:, 0:T], wv[:, k, gu, cc * 128:(cc + 1) * 128], xT[:, k, 0:T], start=(k == 0), stop=(k == 7)),
                            r=[("ring", slot)] + xkeys, w=[PB(bk)])
                S.op("act", lambda idx=idx: nc.scalar.activation(sgt[:, idx, 0:T], pb[idx][:, 0:T], AF.Silu), r=[PB(idx)], w=[("sgt", idx)])
                S.op("dve", lambda idx=idx, c=c: nc.vector.tensor_tensor(hid[:, c, 0:T], pb[2 + idx][:, 0:T], sgt[:, idx, 0:T], ALU.mult),
                     r=[PB(2 + idx), ("sgt", idx)], w=[OV(c)])
        groups = [list(range(npos))] if npos <= 2 else [[0, 1], [2, 3]]
        for grp in groups:
            for pc in range(6):
                slot = ws_next(("wout", f, pc))
                ncc = 4 if pc < 5 else 2
                wv = ring[:, slot, :].rearrange("p (c n) -> p c n", c=4)
                for pos in grp:
                    for hf in range(2):
                        bk = pos * 2 + hf
                        for cc in range(ncc):
                            c = pc * 4 + cc
                            S.op("pe", lambda bk=bk, c=c, cc=cc, pos=pos, hf=hf, wv=wv: nc.tensor.matmul(
                                pb[bk][:, :], hid[:, c, pos * 128:(pos + 1) * 128], wv[:, cc, hf * 512:(hf + 1) * 512],
                                start=(c == 0), stop=(c == NCH - 1)),
                                r=[("ring", slot), OV(c)], w=[PB(bk)])
            for pos in grp:
                rb = pos % 2
                for hf in range(2):
                    bk = pos * 2 + hf
                    S.op("dve", lambda bk=bk, hf=hf, pos=pos, rb=rb: nc.vector.scalar_tensor_tensor(
                        rbuf[:, rb, hf * 512:(hf + 1) * 512], pb[bk][:, :], 0.5 / ALPHA, xres[:, pos, hf * 512:(hf + 1) * 512], ALU.mult, ALU.add),
                        r=[PB(bk), ("xres", pos)], w=[("rbuf", rb, hf)])
                dst_ap, dst_keys, after = dsts[pos]
                layer_norm(rb, 0 if f == 0 else 4, dst_ap, dst_keys)
                if after is not None:
                    after()

    def tile_info(t):
        if t == 0:
            return dict(kind="meta", slot=0, typ=1, K=16)
        if t == 1:
            return dict(kind="samp", slot=17, typ=2, K=64)
        if t < 18:
            return dict(kind="ctx", slot=1 + (t - 2), typ=0, K=128, j=t - 2)
        return dict(kind="own", slot=17 + (t - 18), typ=0, K=128, j=t - 18)

    out_ctr = [0]

    def front_latents(t, pos, info):
        slot = info["slot"]
        i = out_ctr[0] % 2
        out_ctr[0] += 1
        L = Kl(pos)
        rms_norm(lat[:, pos, 0:128], 128, gkvg[:], ckvn[:, i, :], r=L + ["gkvg"], w=[("ckvn", i)], col=i)
        rope_from(lat[:, pos, 128:192].unsqueeze(1), 1, 64, tabs[:, pos, 256:320], tabs[:, pos, 320:384],
                  kper[:, i, 0:64].unsqueeze(1), r=L + [("tabs", pos)], w=[("kper", i)])
        S.op("pool", lambda: nc.gpsimd.tensor_copy(kper[:, i, 64:128], kper[:, i, 0:64]), r=[("kper", i)], w=[("kper", i)])
        kind = info["kind"]
        if kind == "own":
            j = info["j"]
            S.dma("pool", lambda: nc.gpsimd.dma_start(out=ckv_own[j * 128:(j + 1) * 128, :], in_=ckvn[:, i, :]), r=[("ckvn", i)], key=("o_lat", i))
            S.dma("pool", lambda: nc.gpsimd.dma_start(out=kpe_own[j * 128:(j + 1) * 128, :], in_=kper[:, i, 0:64]), r=[("kper", i)], key=("o_lat", i), join=True)
        elif kind == "meta":
            S.dma("pool", lambda: nc.gpsimd.dma_start(out=ckv_meta, in_=ckvn[0:16, i, :]), r=[("ckvn", i)], key=("o_lat", i))
            S.dma("pool", lambda: nc.gpsimd.dma_start(out=kpe_meta, in_=kper[0:16, i, 0:64]), r=[("kper", i)], key=("o_lat", i), join=True)
        elif kind == "samp":
            S.dma("pool", lambda: nc.gpsimd.dma_start(out=ckv_samp, in_=ckvn[0:64, i, :]), r=[("ckvn", i)], key=("o_lat", i))
            S.dma("pool", lambda: nc.gpsimd.dma_start(out=kpe_samp, in_=kper[0:64, i, 0:64]), r=[("kper", i)], key=("o_lat", i), join=True)
        S.op("act", lambda: nc.scalar.copy(aug[:, slot, 0:128], ckvn[:, i, :]), r=[("ckvn", i)], w=[("aug", slot)])
        b = nb()
        S.op("pe", lambda: nc.tensor.transpose(pb[b][:, 0:128], ckvn[:, i, :], ident_f), r=[("ckvn", i), "cst"], w=[PB(b)])
        S.op("pe", lambda: nc.tensor.transpose(pb[b][:, 128:256], kper[:, i, :], ident_f), r=[("kper", i), "cst"], w=[PB(b)])
        S.op("dve", lambda: nc.vector.tensor_copy(ckvT[:, slot * 128:(slot + 1) * 128], pb[b][:, 0:128]), r=[PB(b)], w=[("ckvT", slot)])
        S.op("act", lambda: nc.scalar.copy(kpeT[:, slot * 128:(slot + 1) * 128], pb[b][:, 128:256]), r=[PB(b)], w=[("kpeT", slot)])
        if kind in ("own", "samp"):
            rms_norm(cqb[:, pos, :], 256, gqg[:], cqn[:], r=[("cqb", pos), "gqg"], w=["cqn"], col=2 + i)
            b2 = nb()
            for kc in range(2):
                S.op("pe", lambda kc=kc: nc.tensor.transpose(pb[b2][:, kc * 128:(kc + 1) * 128], cqn[:, kc * 128:(kc + 1) * 128], ident_f),
                     r=["cqn", "cst"], w=[PB(b2)])
            evac(cqnT[:, pos, :, :], pb[b2][:, 0:256].rearrange("p (a b) -> p a b", a=2), r=[PB(b2)], w=[("cqnT", pos)])

    def mix_in_block(kind, tiles):
        npos = len(tiles)
        groups = [0, 1, 2, 3, 4] if kind != "ctx" else [1, 2, 4]
        for t_i, t in enumerate(tiles):
            S.dma("sp", lambda t=t, t_i=t_i: nc.sync.dma_start(out=tabs[:, t_i, :], in_=tab_all[t]), w=[("tabs", t_i)], key=("tabs", t_i))
        mb = [4, 5, 6, 7]
        ctr = 0
        slots = {}
        slots[groups[0]] = ws_next(("wmi", groups[0]))
        slots[groups[1]] = ws_next(("wmi", groups[1]), hold=1)
        if npos == 4:
            work = [("T", [0, 1]), ("M", groups[0], [0, 1]), ("M", groups[1], [0, 1]), ("T", [2, 3]), ("M", groups[0], [2, 3]), ("M", groups[1], [2, 3])]
        else:
            work = [("T", list(range(npos))), ("M", groups[0], list(range(npos))), ("M", groups[1], list(range(npos)))]
        for g in groups[2:]:
            work.append(("M", g, list(range(npos))))
        for item in work:
            if item[0] == "T":
                make_xT(npos, positions=item[1])
                continue
            g = item[1]
            if g not in slots:
                slots[g] = ws_next(("wmi", g))
            slot = slots[g]
            n = 512 if g < 4 else 448
            wv = ring[:, slot, :].rearrange("p (k n) -> p k n", k=8)
            for pos in item[2]:
                t = tiles[pos]
                info = tile_info(t)
                bk = mb[ctr % 4]
                ctr += 1
                for k in range(8):
                    S.op("pe", lambda bk=bk, k=k, pos=pos, n=n, wv=wv: nc.tensor.matmul(
                        pb[bk][:, 0:n], xT[:, k, pos * 128:(pos + 1) * 128], wv[:, k, 0:n], start=(k == 0), stop=(k == 7)),
                        r=[("ring", slot), ("xT", pos)], w=[PB(bk)])
                src3 = pb[bk][:, :].rearrange("p (h d) -> p h d", h=4)
                typ = info["typ"]
                if g == 0:
                    rope_from(src3, 4, 128, tabs[:, pos, 0:128], tabs[:, pos, 128:256],
                              qd[:, pos, :].rearrange("p (h d) -> p h d", h=4), r=[PB(bk), ("tabs", pos)], w=[Kq(pos)], dec_ap=dect[:, typ, 0, :])
                elif g == 1:
                    rope_from(src3, 4, 128, tabs[:, pos, 0:128], tabs[:, pos, 128:256],
                              kd[:, pos, :].rearrange("p (h d) -> p h d", h=4), r=[PB(bk), ("tabs", pos)], w=[Kk(pos)], dec_ap=dect[:, typ, 1, :])
                elif g == 2:
                    S.op("act", lambda bk=bk, pos=pos: nc.scalar.copy(vv[:, pos, :], pb[bk][:, :]), r=[PB(bk)], w=[Kv(pos)])
                elif g == 3:
                    S.op("act", lambda bk=bk, pos=pos: nc.scalar.activation(sgr[:, pos, :], pb[bk][:, :], AF.Silu), r=[PB(bk)], w=[Kg(pos)])
                else:
                    S.op("act", lambda bk=bk, pos=pos: nc.scalar.copy(cqb[:, pos, :], pb[bk][:, 0:256]), r=[PB(bk)], w=[("cqb", pos)])
                    S.op("dve", lambda bk=bk, pos=pos: nc.vector.tensor_copy(lat[:, pos, 0:192], pb[bk][:, 256:448]), r=[PB(bk)], w=Kl(pos))
        for pos, t in enumerate(tiles):
            front_latents(t, pos, tile_info(t))

    def state_update(pos, K, typ, S_t, Sbf_t, skey, base=0):
        b = nb()
        kv = pb[b][:, :].rearrange("p (h d) -> p h d", h=4)
        for h in range(4):
            S.op("pe", lambda h=h: nc.tensor.matmul(kv[:, h, :], kd[base:base + K, pos, h * 128:(h + 1) * 128],
                                                    vv[base:base + K, pos, h * 128:(h + 1) * 128], start=True, stop=True),
                 r=[Kk(pos), Kv(pos)], w=[PB(b)])
        S.op("dve", lambda: nc.vector.tensor_tensor(Tt[:], kv, S_t[:], ALU.add), r=[PB(b), skey], w=["Tt"])
        S.op("pool", lambda: nc.gpsimd.tensor_tensor(S_t[:], Tt[:], dect[:, typ, 2, :].unsqueeze(2).broadcast_to([128, 4, 128]), ALU.mult),
             r=["Tt", "dect"], w=[skey])
        S.op("act", lambda: nc.scalar.copy(Sbf_t[:], S_t[:]), r=[skey], w=[skey + "bf"])

    def retention_tile(pos, info):
        samp = info["kind"] == "samp"
        for src, dstT, nm, kf in ((qd, qT, "qd", Kq), (kd, kT, "kd", Kk)):
            b = nb()
            pbb = pb[b][:].bitcast(BF16)
            for h in range(4):
                S.op("pe", lambda h=h, src=src, pbb=pbb: nc.tensor.transpose(pbb[:, h * 128:(h + 1) * 128], src[:, pos, h * 128:(h + 1) * 128], identb[:]),
                     r=[kf(pos), "identb"], w=[PB(b)])
            evac(dstT[:], pbb[:, 0:512].rearrange("p (a b) -> p a b", a=4), r=[PB(b)], w=[nm + "T"])
            yield
        b = nb()
        sc = pb[b][:, :].rearrange("p (h d) -> p h d", h=4)
        for h in range(4):
            S.op("pe", lambda h=h: nc.tensor.matmul(sc[:, h, :], kT[:, h, :], qT[:, h, :], start=True, stop=True), r=["kdT", "qdT"], w=[PB(b)])
        mi = 3 if samp else 1
        S.op("dve", lambda: nc.vector.tensor_tensor(scm[:], sc, cst[:, mi, :].unsqueeze(1).broadcast_to([128, 4, 128]), ALU.mult),
             r=[PB(b), "cst"], w=["scm"])
        yield
        if samp:
            for s in range(2):
                S.op("pool", lambda s=s: nc.gpsimd.tensor_tensor(qTm[:, s, :, :], qT[:], cst[:, 5 + s, :].unsqueeze(1).broadcast_to([128, 4, 128]), ALU.mult),
                     r=["qdT", "cst"], w=["rsum"])
        bo = nb()
        o3 = pb[bo][:, :].rearrange("p (h d) -> p h d", h=4)
        for h in range(4):
            S.op("pe", lambda h=h: nc.tensor.matmul(o3[:, h, :], scm[:, h, :], vv[:, pos, h * 128:(h + 1) * 128], start=True, stop=False),
                 r=["scm", Kv(pos)], w=[PB(bo)])
            if samp:
                for s in range(2):
                    S.op("pe", lambda h=h, s=s: nc.tensor.matmul(o3[:, h, :], qTm[:, s, h, :], Ssbf[s][:, h, :], start=False, stop=(s == 1)),
                         r=["rsum", "Ss%dbf" % s], w=[PB(bo)])
            else:
                S.op("pe", lambda h=h: nc.tensor.matmul(o3[:, h, :], qT[:, h, :], Sbf[:, h, :], start=False, stop=True),
                     r=["qdT", "Sbf"], w=[PB(bo)])
        yield
        if samp:
            for s in range(2):
                state_update(pos, 32, 2, Ss[s], Ssbf[s], "Ss%d" % s, base=32 * s)
        else:
            state_update(pos, 128, 0, Sst, Sbf, "S")
        yield
        for h in range(4):
            S.op("dve", lambda h=h: nc.vector.bn_stats(gst[:, h, :], o3[:, h, :]), r=[PB(bo)], w=[("gst", h)])
            S.op("dve", lambda h=h: nc.vector.bn_aggr(gmv[:, h, :], gst[:, h, :]), r=[("gst", h)], w=["gmv"])
        S.op("act", lambda: nc.scalar.activation(gsm[:, 0:4], gmv[:, :, 1], AF.Sqrt, bias=LN_EPS, scale=1.0), r=["gmv"], w=["gsm"])
        S.op("dve", lambda: nc.vector.reciprocal(gsm[:, 4:8], gsm[:, 0:4]), r=["gsm"], w=["gsm"])
        S.op("dve", lambda: nc.vector.scalar_tensor_tensor(gsm[:, 8:12], gmv[:, :, 0], -1.0, gsm[:, 4:8], ALU.mult, ALU.mult), r=["gsm", "gmv"], w=["gsm"])
        yield
        for h in range(4):
            S.op("act", lambda h=h: nc.scalar.activation(on[:, h * 128:(h + 1) * 128], o3[:, h, :], AF.Identity,
                                                         bias=gsm[:, 8 + h:9 + h], scale=gsm[:, 4 + h:5 + h]), r=[PB(bo), "gsm"], w=["on"])
        S.op("pool", lambda: nc.gpsimd.tensor_tensor(on[:], on[:], gng[:], ALU.mult), r=["on", "gng"], w=["on"])
        S.op("pool", lambda: nc.gpsimd.tensor_tensor(ymix[:, pos, 0:512], on[:], sgr[:, pos, :], ALU.mult), r=["on", Kg(pos)], w=[("ymix", pos)])
        yield

    def mla_pre(pos, info, par):
        qabsT = qabsT2[:, par]
        qrT = qrT2[:, par]
        b = nb()
        q3 = pb[b][:, :].rearrange("p (h d) -> p h d", h=4)
        for h in range(4):
            for kc in range(2):
                S.op("pe", lambda h=h, kc=kc: nc.tensor.matmul(q3[:, h, :], wuq[:, kc, h * 192:h * 192 + 128], cqnT[:, pos, kc, :],
                                                               start=(kc == 0), stop=(kc == 1)), r=["wuq", ("cqnT", pos)], w=[PB(b)])
        S.op("act", lambda: nc.scalar.copy(qnT[:], q3), r=[PB(b)], w=["qnT"])
        yield
        b2 = nb()
        qa3 = pb[b2][:, :].rearrange("p (h d) -> p h d", h=4)
        for h in range(4):
            S.op("pe", lambda h=h: nc.tensor.matmul(qa3[:, h, :], wukT[:, h, :], qnT[:, h, :], start=True, stop=True), r=["wukT", "qnT"], w=[PB(b2)])
        S.op("dve", lambda: nc.vector.tensor_copy(qabsT, qa3), r=[PB(b2)], w=[("qabsT", par)])
        yield
        b3 = nb()
        qr3 = pb[b3][:, 0:256].rearrange("p (h d) -> p h d", h=4)
        wr = wuq[:, :, :].rearrange("p k (h d) -> p k h d", h=4)
        for kc in range(2):
            S.op("pe", lambda kc=kc: nc.tensor.matmul(qr3, cqnT[:, pos, kc, :], wr[:, kc, :, 128:192], start=(kc == 0), stop=(kc == 1)),
                 r=["wuq", ("cqnT", pos)], w=[PB(b3)])
        rope_from(qr3, 4, 64, tabs[:, pos, 256:320], tabs[:, pos, 320:384], qrr[:], r=[PB(b3), ("tabs", pos)], w=["qrr"])
        yield
        yield
        yield
        yield
        b4 = nb()
        for pr in range(2):
            S.op("pe", lambda pr=pr: nc.tensor.transpose(pb[b4][:, pr * 128:(pr + 1) * 128], qrr[:, 2 * pr:2 * pr + 2, :].rearrange("p a b -> p (a b)"), ident_f),
                 r=["qrr", "cst"], w=[PB(b4)])
        qr4 = qrT.rearrange("p (a b) d -> p a b d", b=2)
        S.op("dve", lambda: nc.vector.tensor_copy(qr4[0:64, :, 0, :], pb[b4][0:64, 0:256].rearrange("p (a d) -> p a d", a=2)), r=[PB(b4)], w=[("qrT", par)])
        S.op("act", lambda: nc.scalar.copy(qr4[64:128, :, 1, :], pb[b4][64:128, 0:256].rearrange("p (a d) -> p a d", a=2)), r=[PB(b4)], w=[("qrT", par)])
        yield

    def mla_attn(pos, info, keylist, par, side=None):
        qabsT = qabsT2[:, par]
        qrT = qrT2[:, par]
        qa_flat = qabsT.rearrange("p a b -> p (a b)")
        qr_flat = qrT.rearrange("p a b -> p (a b)")
        nk_tiles = len(keylist)

        def emit_qk(ji):
            slot = keylist[ji][0]
            sb_i = 2 + ji % 2
            S.op("pe", lambda: nc.tensor.matmul(pb[sb_i][:, :], ckvT[:, slot * 128:(slot + 1) * 128], qa_flat, start=True, stop=False),
                 r=[("ckvT", slot), ("qabsT", par)], w=[PB(sb_i)])
            S.op("pe", lambda: nc.tensor.matmul(pb[sb_i][:, :], kpeT[:, slot * 128:(slot + 1) * 128], qr_flat, start=False, stop=True),
                 r=[("kpeT", slot), ("qrT", par)], w=[PB(sb_i)])

        emit_qk(0)
        for ji, (slot, biascol, mask) in enumerate(keylist):
            sb_i = 2 + ji % 2
            pi = ji % 2
            if ji + 1 < nk_tiles:
                emit_qk(ji + 1)
            if biascol is None:
                S.op("act", lambda sb_i=sb_i, pi=pi: nc.scalar.activation(pT[:, pi, :], pb[sb_i][:, :], AF.Exp, scale=SCALE),
                     r=[PB(sb_i)], w=[("pT", pi)])
            else:
                S.op("act", lambda sb_i=sb_i, pi=pi, biascol=biascol: nc.scalar.activation(pT[:, pi, :], pb[sb_i][:, :], AF.Exp, scale=SCALE,
                                                                                           bias=flg[:, biascol:biascol + 1]),
                     r=[PB(sb_i), "flg"], w=[("pT", pi)])
            if mask is not None:
                S.op("pool", lambda pi=pi, mask=mask: nc.gpsimd.tensor_tensor(
                    pT[:, pi, :].rearrange("p (h d) -> p h d", h=4), pT[:, pi, :].rearrange("p (h d) -> p h d", h=4),
                    cst[:, mask, :].unsqueeze(1).broadcast_to([128, 4, 128]), ALU.mult), r=[("pT", pi), "cst"], w=[("pT", pi)])
            S.op("pe", lambda pi=pi, slot=slot, ji=ji: nc.tensor.matmul(pb[4][:, :], aug[:, slot, 0:128], pT[:, pi, :],
                                                                       start=(ji == 0), stop=(ji == nk_tiles - 1)),
                 r=[("pT", pi), ("aug", slot)], w=[PB(4)])
            S.op("pe", lambda pi=pi, ji=ji: nc.tensor.matmul(pb[5][:, :], onesb[:], pT[:, pi, :],
                                                            start=(ji == 0), stop=(ji == nk_tiles - 1)),
                 r=[("pT", pi), "onesb"], w=[PB(5)])
            if side is not None:
                next(side, None)
        S.op("dve", lambda: nc.vector.reciprocal(rsum[:], pb[5][:, :]), r=[PB(5)], w=["rsum"])
        S.op("dve", lambda: nc.vector.tensor_tensor(olatT[:].rearrange("p a b -> p (a b)"), pb[4][:, :], rsum[:], ALU.mult), r=[PB(4), "rsum"], w=["olatT"])
        b6 = nb()
        my3 = pb[b6][:, :].rearrange("p (h d) -> p h d", h=4)
        for h in range(4):
            S.op("pe", lambda h=h: nc.tensor.matmul(my3[:, h, :], olatT[:, h, :], wukv[:, h * 256 + 128:h * 256 + 256], start=True, stop=True),
                 r=["olatT", "wukv"], w=[PB(b6)])
        S.op("act", lambda: nc.scalar.copy(ymix[:, pos, 512:1024], pb[b6][:, :]), r=[PB(b6)], w=[("ymix", pos)])

    def mix_out_block(tiles):
        npos = len(tiles)
        for pos in range(npos):
            b = nb()
            pbb = pb[b][:].bitcast(BF16)
            for k in range(8):
                S.op("pe", lambda k=k, pbb=pbb, pos=pos: nc.tensor.transpose(pbb[:, k * 128:(k + 1) * 128], ymix[:, pos, k * 128:(k + 1) * 128], identb[:]),
                     r=[("ymix", pos), "identb"], w=[PB(b)])
            evac(xT[:, :, pos * 128:(pos + 1) * 128], pbb[:, :].rearrange("p (a b) -> p a b", a=8), r=[PB(b)], w=[("xT", pos)])
        slots = [ws_next(("wmo", 0)), ws_next(("wmo", 1), hold=1)]
        for pos in range(npos):
            rb = pos % 2
            for hf in range(2):
                bk = nb()
                wv = ring[:, slots[hf], :].rearrange("p (k n) -> p k n", k=8)
                for k in range(8):
                    S.op("pe", lambda bk=bk, k=k, pos=pos, wv=wv: nc.tensor.matmul(pb[bk][:, :], xT[:, k, pos * 128:(pos + 1) * 128], wv[:, k, :],
                                                                                start=(k == 0), stop=(k == 7)),
                         r=[("ring", slots[hf]), ("xT", pos)], w=[PB(bk)])
                S.op("dve", lambda bk=bk, hf=hf, pos=pos, rb=rb: nc.vector.scalar_tensor_tensor(
                    rbuf[:, rb, hf * 512:(hf + 1) * 512], pb[bk][:, :], 1.0 / ALPHA, xres[:, pos, hf * 512:(hf + 1) * 512], ALU.mult, ALU.add),
                    r=[PB(bk), ("xres", pos)], w=[("rbuf", rb, hf)])
            layer_norm(rb, 2, xres[:, pos, :], [("xres", pos)])

    def load_cache():
        for s in range(2):
            S.dma("sp", lambda s=s: nc.sync.dma_start(out=ctmp[:], in_=cache_ckv[s].rearrange("(i p) d -> p i d", p=128)), w=YM, key="cache")
            for dup in range(2):
                S.dma("sp", lambda s=s, dup=dup: nc.sync.dma_start(out=ktmp[:, :, dup * 64:(dup + 1) * 64], in_=cache_kpe[s].rearrange("(i p) d -> p i d", p=128)), w=YM, key="cache", join=True)
            S.op("act", lambda s=s: nc.scalar.copy(aug[:, 1 + 8 * s:9 + 8 * s, 0:128], ctmp[:]), r=YM, w=[("aug", 1 + 8 * s + i) for i in range(8)])
            for hb in range(2):
                b = nb()
                for i in range(4):
                    S.op("pe", lambda i=i, hb=hb, b=b: nc.tensor.transpose(pb[b][:, i * 128:(i + 1) * 128], ctmp[:, hb * 4 + i, :], ident_f),
                         r=YM + ["cst"], w=[PB(b)])
                c0 = (1 + 8 * s + 4 * hb) * 128
                evac(ckvT[:, c0:c0 + 512], pb[b][:, :], r=[PB(b)], w=[("ckvT", 1 + 8 * s + 4 * hb + i) for i in range(4)])
                b = nb()
                for i in range(4):
                    S.op("pe", lambda i=i, hb=hb, b=b: nc.tensor.transpose(pb[b][:, i * 128:(i + 1) * 128], ktmp[:, hb * 4 + i, :], ident_f),
                         r=YM + ["cst"], w=[PB(b)])
                evac(kpeT[:, c0:c0 + 512], pb[b][:, :], r=[PB(b)], w=[("kpeT", 1 + 8 * s + 4 * hb + i) for i in range(4)])
            S.dma("sp", lambda s=s: nc.sync.dma_start(out=Ss[s][:], in_=state_in[s].rearrange("h p d -> p h d")), w=["Ss%d" % s], key="cache2", join=(s == 1))
        for s in range(2):
            S.op("act", lambda s=s: nc.scalar.copy(Ssbf[s][:], Ss[s][:]), r=["Ss%d" % s], w=["Ss%dbf" % s])

    ycnt = [0]
    try:
        for bi, (kind, tiles) in enumerate(blocks):
            npos = len(tiles)
            if STAGE < 3:
                raise _Stop()
            if kind == "ctx" and STAGE < 8:
                raise _Stop()
            if kind == "own" and STAGE < 9:
                raise _Stop()
            if STAGE < 99 and bi >= 6 + max(0, STAGE - 9):
                raise _Stop()
            for pos, t in enumerate(tiles):
                S.dma("sp", lambda pos=pos, t=t: nc.sync.dma_start(out=xres[:, pos, :], in_=x_all[t]), w=[("xres", pos)], key=("xl", pos))
            ffn(0, npos, [(xres[:, pos, :], [("xres", pos)], None) for pos in range(npos)])
            if kind == "b0":
                if DBG:
                    S.dma("pool", lambda: nc.gpsimd.dma_start(out=dbg[:, 0:1024], in_=xres[:, 1, :]), r=[("xres", 1)], key="dbg0")
                if STAGE < 4:
                    raise _Stop()
                load_cache()
            mix_in_block(kind, tiles)
            if kind == "b0":
                if STAGE < 5:
                    raise _Stop()
                state_update(0, 16, 1, Sst, Sbf, "S")
                S.op("dve", lambda: nc.vector.tensor_copy(Smeta[:], Sst[:]), r=["S"], w=["Smeta"])
                info = tile_info(1)
                for _ in retention_tile(1, info):
                    pass
                if STAGE < 6:
                    raise _Stop()
                keylist = [(0, 2, None)] + [(1 + i, None, 5) for i in range(8)] + [(9 + i, None, 6) for i in range(8)] + [(17, None, 4)]
                for _ in mla_pre(1, info, 0):
                    pass
                mla_attn(1, info, keylist, 0)
                if DBG:
                    S.dma("pool", lambda: nc.gpsimd.dma_start(out=dbg[:, 1024:2048], in_=ymix[:, 1, :]), r=[("ymix", 1)], key="dbg1")
                if STAGE < 7:
                    raise _Stop()
                S.op("pool", lambda: nc.gpsimd.memset(ymix[:, 0, :], 0.0), w=[("ymix", 0)])
                mix_out_block(tiles)
                if DBG:
                    S.dma("pool", lambda: nc.gpsimd.dma_start(out=dbg[:, 2048:3072], in_=xres[:, 1, :]), r=[("xres", 1)], key="dbg2")
                for s in range(2):
                    S.dma("pool", lambda s=s: nc.gpsimd.dma_start(out=state_s[s].rearrange("h p d -> p h d"), in_=Ss[s][:]), r=["Ss%d" % s], key="o_state", join=(s == 1))

                def after_s(i):
                    def f():
                        S.dma("pool", lambda: nc.gpsimd.dma_start(out=y_samp, in_=ystage[0:64, i, :]), r=[("ystage", i)], key=("o_y", i))
                    return f
                dsts = []
                for pos in range(npos):
                    i = 0
                    ycnt[0] += 1
                    dsts.append((ystage[:, i, :], [("ystage", i)], after_s(i) if pos == 1 else None))
                ffn(1, npos, dsts)
            elif kind == "ctx":
                for pos, t in enumerate(tiles):
                    state_update(pos, 128, 0, Sst, Sbf, "S")
                if tiles[-1] == 17:
                    S.op("dve", lambda: nc.vector.tensor_tensor(Tt[:], Sst[:], Smeta[:], ALU.subtract), r=["S", "Smeta"], w=["Tt"])
                    S.op("dve", lambda: nc.vector.scalar_tensor_tensor(Sst[:].rearrange("p a b -> p (a b)"), Tt[:].rearrange("p a b -> p (a b)"), flg[:, 0:1],
                                                                       Smeta[:].rearrange("p a b -> p (a b)"), ALU.mult, ALU.add), r=["Tt", "Smeta", "flg"], w=["S"])
                    S.op("act", lambda: nc.scalar.copy(Sbf[:], Sst[:]), r=["S"], w=["Sbf"])
            else:
                def pre_gen(pos, t, par):
                    info = tile_info(t)
                    yield from retention_tile(pos, info)
                    yield from mla_pre(pos, info, par)

                def drain(g):
                    if g is not None:
                        for _ in g:
                            pass
                drain(pre_gen(0, tiles[0], 0))
                for pos, t in enumerate(tiles):
                    info = tile_info(t)
                    j = info["j"]
                    keylist = [(0, 2, None)] + [(1 + i, 1, None) for i in range(16)] + [(17 + i, None, None) for i in range(j)] + \
                              [(17 + j, None, 2)]
                    side = pre_gen(pos + 1, tiles[pos + 1], (pos + 1) % 2) if pos + 1 < len(tiles) else None
                    mla_attn(pos, info, keylist, pos % 2, side)
                    drain(side)
                mix_out_block(tiles)

                def after_o(i, j):
                    def f():
                        S.dma("pool", lambda: nc.gpsimd.dma_start(out=y_own[j * 128:(j + 1) * 128, :], in_=ystage[:, i, :]), r=[("ystage", i)], key=("o_y", i))
                    return f
                dsts = []
                for pos, t in enumerate(tiles):
                    i = 0
                    ycnt[0] += 1
                    dsts.append((ystage[:, i, :], [("ystage", i)], after_o(i, tile_info(t)["j"])))
                ffn(1, npos, dsts)
        S.dma("pool", lambda: nc.gpsimd.dma_start(out=state_p.rearrange("h p d -> p h d"), in_=Sst[:]), r=["S"], key="o_state2")

    except _Stop:
        pass
    stats = S.emit(es)
    es.close()
    return nc, stats


def _rope_tab(pos, d):
    inv = 10000.0 ** (-np.arange(0, d, 2, dtype=np.float64) / d)
    ang = pos.astype(np.float64)[:, None] * inv[None, :]
    c, s = np.cos(ang), np.sin(ang)
    return np.concatenate([c, c], 1), np.concatenate([-s, s], 1)


def _host_constants(r):
    pos = np.zeros((NT, 128), np.float64)
    pos[0] = np.arange(128)
    pos[1] = 16 + 1024 + (np.arange(128) % 32)
    for j in range(16):
        pos[2 + j] = 16 + j * 128 + np.arange(128)
        pos[18 + j] = 16 + r * 2048 + j * 128 + np.arange(128)
    tab = np.zeros((NT, 128, 384), np.float32)
    for t in range(NT):
        ccr, ssr = _rope_tab(pos[t], 128)
        ccm, ssm = _rope_tab(pos[t], 64)
        tab[t, :, 0:128] = ccr; tab[t, :, 128:256] = ssr
        tab[t, :, 256:320] = ccm; tab[t, :, 320:384] = ssm
    k = np.arange(128)[:, None]; q = np.arange(128)[None, :]
    consts = np.zeros((128, 7, 128), np.float32)
    consts[:, 0] = np.eye(128)
    consts[:, 1] = (q >= k)
    consts[:, 2] = 1.0 - ((k >= 64) & (q < 64))
    same = (k // 32 == q // 32) & (k < 64) & (q < 64)
    consts[:, 3] = same & (q >= k)
    consts[:, 4] = same
    consts[:, 5] = np.broadcast_to(q < 32, (128, 128))
    consts[:, 6] = np.broadcast_to((q >= 32) & (q < 64), (128, 128))
    gam = 1.0 - 2.0 ** (-5.0 - np.arange(4, dtype=np.float64))
    dect = np.zeros((128, 3, 3, 4), np.float64)
    row = np.arange(128, dtype=np.float64)
    for typ, (idx, C) in enumerate(((row, 128), (row, 16), (row % 32, 32))):
        dect[:, typ, 0, :] = gam[None, :] ** (idx[:, None] + 1.0)
        dect[:, typ, 1, :] = (128.0 ** -0.5) * gam[None, :] ** (-idx[:, None] - 1.0)
        dect[:, typ, 2, :] = gam[None, :] ** C
    flg = np.zeros((128, 4), np.float32)
    flg[16:, 2] = -30000.0
    flg[:, 0] = float(r)
    flg[:, 1] = 0.0 if r == 1 else -30000.0
    return tab, consts, dect.reshape(128, 36).astype(np.float32), flg


_CACHE = {}


def kernel(x_prompt, x_sample, cache_mla_ckv, cache_mla_kpe, state_ret, meta_tokens,
           ffn1_w_in, ffn1_w_out, ln1_g, ln1_b, w_mix_in, ret_gn_g, mla_q_norm_g, mla_w_uq,
           mla_kv_norm_g, mla_w_ukv, w_mix_out, ln2_g, ln2_b, ffn2_w_in, ffn2_w_out, ln3_g, ln3_b):
    f32 = lambda a: np.ascontiguousarray(np.asarray(a, dtype=np.float32))
    x_prompt = f32(x_prompt); x_sample = f32(x_sample)
    if "nc" not in _CACHE:
        _CACHE["nc"] = build_program()
    nc, stats = _CACHE["nc"]
    lnvec = np.stack([f32(ln1_g), f32(ln1_b), f32(ln2_g), f32(ln2_b), f32(ln3_g), f32(ln3_b)], 0)
    in_maps = []
    for c in range(8):
        b, r = c // 2, c % 2
        x_all = np.zeros((NT, 128, D), np.float32)
        x_all[0, 0:16] = f32(meta_tokens)
        x_all[1, 0:32] = x_sample[2 * c]
        x_all[1, 32:64] = x_sample[2 * c + 1]
        x_all[2:18] = x_prompt[b, 0:2048].reshape(16, 128, D)
        x_all[18:34] = x_prompt[b, r * 2048:(r + 1) * 2048].reshape(16, 128, D)
        tab, consts, dect, flg = _host_constants(r)
        in_maps.append(dict(
            x_all=x_all, tab_all=tab,
            cache_ckv=f32(cache_mla_ckv[2 * c:2 * c + 2]), cache_kpe=f32(cache_mla_kpe[2 * c:2 * c + 2]),
            state_in=f32(state_ret[2 * c:2 * c + 2]),
            ffn1_w_in=f32(ffn1_w_in), ffn2_w_in=f32(ffn2_w_in), ffn1_w_out=f32(ffn1_w_out), ffn2_w_out=f32(ffn2_w_out),
            w_mix_in=f32(w_mix_in), w_mix_out=f32(w_mix_out), mla_w_uq=f32(mla_w_uq), mla_w_ukv=f32(mla_w_ukv),
            lnvec=lnvec, ret_gn_g=f32(ret_gn_g), mla_q_norm_g=f32(mla_q_norm_g), mla_kv_norm_g=f32(mla_kv_norm_g),
            consts=consts, dect=dect, flg=flg))
    res = run_bass_kernel_spmd(nc, in_maps, core_ids=list(range(8)))
    R = res.results
    _CACHE["last"] = R
    y_prompt = np.zeros((4, 4096, D), np.float32)
    y_sample = np.zeros((16, 32, D), np.float32)
    p_ckv = np.zeros((4, 4112, 128), np.float32)
    p_kpe = np.zeros((4, 4112, 64), np.float32)
    p_state = np.zeros((4, 4, 128, 128), np.float32)
    s_ckv = np.zeros((16, 32, 128), np.float32)
    s_kpe = np.zeros((16, 32, 64), np.float32)
    s_state = np.zeros((16, 4, 128, 128), np.float32)
    for c in range(8):
        b, r = c // 2, c % 2
        o = R[c]
        y_prompt[b, r * 2048:(r + 1) * 2048] = o["y_own"]
        p_ckv[b, 16 + r * 2048:16 + (r + 1) * 2048] = o["ckv_own"]
        p_kpe[b, 16 + r * 2048:16 + (r + 1) * 2048] = o["kpe_own"]
        if r == 0:
            p_ckv[b, 0:16] = o["ckv_meta"]
            p_kpe[b, 0:16] = o["kpe_meta"]
        else:
            p_state[b] = o["state_p"]
        y_sample[2 * c:2 * c + 2] = o["y_samp"].reshape(2, 32, D)
        s_ckv[2 * c:2 * c + 2] = o["ckv_samp"].reshape(2, 32, 128)
        s_kpe[2 * c:2 * c + 2] = o["kpe_samp"].reshape(2, 32, 64)
        s_state[2 * c:2 * c + 2] = o["state_s"]
    return (y_prompt, y_sample, p_ckv, p_kpe, p_state, s_ckv, s_kpe, s_state)
```

```python
import contextlib
import os
import numpy as np
import concourse.bass as bass
import concourse.mybir as mybir
from concourse.bass_utils import run_bass_kernel_spmd

F32 = mybir.dt.float32
BF16 = mybir.dt.bfloat16
AF = mybir.ActivationFunctionType
ALU = mybir.AluOpType

COMPUTE = ("pe", "act", "dve", "pool")

D = 1024
DFF = 2816
NCH = 22
ALPHA = 2.0 ** 0.25
LN_EPS = 1e-5
RMS_EPS = 1e-6
NT = 34
NSLOT = 33
AUGW = 132
SCALE = 192.0 ** -0.5
NS = 3
STAGE = int(os.environ.get("KSTAGE", "99"))
DBG = bool(int(os.environ.get("KDBG", "0")))


class _Stop(Exception):
    pass


class _Op:
    __slots__ = ("eng", "fn", "deps", "is_dma", "grp", "need_inc", "val")


class Sched:
    def __init__(self, nc):
        self.nc = nc
        self.ops = []
        self.bufs = {}
        self.eng = {"pe": nc.tensor, "act": nc.scalar, "dve": nc.vector,
                    "pool": nc.gpsimd, "sp": nc.sync}
        self.cur_grp = {}

    def _deps_for(self, op, r, w):
        deps = []
        for k in r:
            st = self.bufs.get(k)
            if st is None:
                st = self.bufs[k] = [None, {}]
            if st[0] is not None:
                deps.append(st[0])
            if isinstance(k, tuple) and k[0] == "pb":
                for ek, ro in st[1].items():
                    if ek != op.eng:
                        deps.append(ro)
        for k in w:
            st = self.bufs.get(k)
            if st is None:
                st = self.bufs[k] = [None, {}]
            if st[0] is not None:
                deps.append(st[0])
            deps.extend(st[1].values())
        for k in r:
            st = self.bufs[k]
            key = ("dma", id(op)) if op.is_dma else op.eng
            st[1][key] = op
        for k in w:
            st = self.bufs[k]
            st[0] = op
            st[1] = {}
        return deps

    def op(self, eng, fn, r=(), w=()):
        o = _Op()
        o.eng = eng; o.fn = fn; o.is_dma = False; o.grp = None
        o.need_inc = False; o.val = None
        o.deps = self._deps_for(o, r, w)
        self.ops.append(o)
        return o

    def dma(self, q, fn, r=(), w=(), key=None, join=False):
        o = _Op()
        o.eng = q; o.fn = fn; o.is_dma = True
        o.need_inc = True; o.val = None
        if join and key in self.cur_grp:
            self.cur_grp[key].append(o)
        else:
            self.cur_grp[key] = [o]
        o.grp = (key, self.cur_grp[key])
        o.deps = self._deps_for(o, r, w)
        self.ops.append(o)
        return o

    def emit(self, stack, final_wait_eng="sp"):
        nc = self.nc
        ops = self.ops
        for o in ops:
            for d in o.deps:
                if not d.is_dma:
                    if d.eng == "pe" and o.eng == "pe" and not o.is_dma:
                        continue
                    d.need_inc = True
        cnt = {e: 0 for e in COMPUTE}
        dcnt = {}
        seen = set()
        for o in ops:
            if o.is_dma:
                key, members = o.grp
                if id(members) not in seen:
                    seen.add(id(members))
                    final = dcnt.get(key, 0) + 16 * len(members)
                    dcnt[key] = final
                    for m in members:
                        m.val = final
            elif o.need_inc:
                cnt[o.eng] += 1
                o.val = cnt[o.eng]
        sems = {}
        for e in COMPUTE:
            sems[e] = stack.enter_context(nc.semaphore("s_" + e))
        for i, key in enumerate(dcnt):
            sems[key] = stack.enter_context(nc.semaphore("d%d" % i))
        waited = {e: {} for e in self.eng}
        nwait = 0
        for o in ops:
            E = self.eng[o.eng]
            need = {}
            for d in o.deps:
                if d.is_dma:
                    if o.is_dma and d.grp[1] is o.grp[1]:
                        continue
                    sk = d.grp[0]
                else:
                    if d.eng == "pe" and o.eng == "pe" and not o.is_dma:
                        continue
                    sk = d.eng
                if need.get(sk, 0) < d.val:
                    need[sk] = d.val
            for sk, v in need.items():
                if waited[o.eng].get(sk, 0) < v:
                    E.wait_ge(sems[sk], v)
                    waited[o.eng][sk] = v
                    nwait += 1
            ins = o.fn()
            if o.is_dma:
                ins.then_inc(sems[o.grp[0]], 16)
            elif o.need_inc:
                ins.then_inc(sems[o.eng], 1)
        E = self.eng[final_wait_eng]
        for key, v in dcnt.items():
            if waited[final_wait_eng].get(key, 0) < v:
                E.wait_ge(sems[key], v)
        for e in COMPUTE:
            if cnt[e] > 0:
                E.wait_ge(sems[e], cnt[e])
        return dict(n_ops=len(ops), n_wait=nwait, cnt=cnt, n_dsem=len(dcnt))


def build_program():
    nc = bass.Bass("TRN2", target_bir_lowering=False)
    es = contextlib.ExitStack()
    S = Sched(nc)

    def din(name, shape, dt=F32):
        return nc.dram_tensor(name, shape, dt, kind="ExternalInput").ap()

    def dout(name, shape):
        return nc.dram_tensor(name, shape, F32, kind="ExternalOutput").ap()

    def dscr(name, shape, dt=BF16):
        return nc.dram_tensor(name, shape, dt, kind="Internal").ap()

    def sb(name, shape, dt=F32):
        return es.enter_context(nc.sbuf_tensor("sb_" + name, shape, dt))

    x_all = din("x_all", [NT, 128, D])
    tab_all = din("tab_all", [NT, 128, 384])
    cache_ckv = din("cache_ckv", [2, 1024, 128])
    cache_kpe = din("cache_kpe", [2, 1024, 64])
    state_in = din("state_in", [2, 4, 128, 128])
    w_in = [din("ffn1_w_in", [D, 2 * DFF]), din("ffn2_w_in", [D, 2 * DFF])]
    w_out = [din("ffn1_w_out", [DFF, D]), din("ffn2_w_out", [DFF, D])]
    w_mix_in = din("w_mix_in", [D, 2496])
    w_mix_out = din("w_mix_out", [D, D])
    w_uq = din("mla_w_uq", [256, 768])
    w_ukv = din("mla_w_ukv", [128, 1024])
    lnvec = din("lnvec", [6, D])
    gn_g = din("ret_gn_g", [512])
    gq_g = din("mla_q_norm_g", [256])
    gkv_g = din("mla_kv_norm_g", [128])
    consts = din("consts", [128, 7, 128])
    dect_in = din("dect", [128, 36])
    flg_in = din("flg", [128, 4])

    y_own = dout("y_own", [2048, D])
    y_samp = dout("y_samp", [64, D])
    ckv_own = dout("ckv_own", [2048, 128])
    kpe_own = dout("kpe_own", [2048, 64])
    ckv_meta = dout("ckv_meta", [16, 128])
    kpe_meta = dout("kpe_meta", [16, 64])
    ckv_samp = dout("ckv_samp", [64, 128])
    kpe_samp = dout("kpe_samp", [64, 64])
    state_p = dout("state_p", [4, 128, 128])
    state_s = dout("state_s", [2, 4, 128, 128])
    dbg = dout("dbg", [128, 4096]) if DBG else None

    winb = [dscr("winb%d" % f, [11, 128, 8, 2, 256]) for f in range(2)]
    woutb = [dscr("woutb%d" % f, [6, 128, 4, D]) for f in range(2)]
    wmib = dscr("wmib", [5, 128, 8, 512])
    wmob = dscr("wmob", [2, 128, 8, 512])

    cst = sb("cst", [128, 7, 128])
    ident_f = cst[:, 0, :]
    identb = sb("identb", [128, 128], BF16)
    dect = sb("dect", [128, 3, 3, 4])
    flg = sb("flg", [128, 4])
    lnv = sb("lnv", [128, 6, D])
    gng = sb("gng", [128, 512])
    gqg = sb("gqg", [128, 256])
    gkvg = sb("gkvg", [128, 128])
    wuq = sb("wuq", [128, 2, 768], BF16)
    wukv = sb("wukv", [128, 1024], BF16)
    wukT = sb("wukT", [128, 4, 128], BF16)
    ckvT = sb("ckvT", [128, NSLOT * 128], BF16)
    kpeT = sb("kpeT", [128, NSLOT * 128], BF16)
    aug = sb("aug", [128, NSLOT, AUGW], BF16)
    Sst = sb("Sst", [128, 4, 128])
    Sbf = sb("Sbf", [128, 4, 128], BF16)
    Smeta = sb("Smeta", [128, 4, 128])
    Ss = [sb("Ss%d" % s, [128, 4, 128]) for s in range(2)]
    Ssbf = [sb("Ssbf%d" % s, [128, 4, 128], BF16) for s in range(2)]
    Tt = sb("Tt", [128, 4, 128])
    ring = sb("ring", [128, NS, 4096], BF16)
    xT = sb("xT", [128, 8, 512], BF16)
    xres = sb("xres", [128, 4, D])
    ovl = sb("ovl", [128, 11264], BF16)
    hid = ovl[:, :].rearrange("p (c t) -> p c t", c=NCH)
    qd = ovl[:, 0:2048].rearrange("p (a b) -> p a b", a=4)
    kd = ovl[:, 2048:4096].rearrange("p (a b) -> p a b", a=4)
    vv = ovl[:, 4096:6144].rearrange("p (a b) -> p a b", a=4)
    sgr = ovl[:, 6144:8192].rearrange("p (a b) -> p a b", a=4)
    lat = ovl[:, 8192:8192 + 2 * 4 * 320].bitcast(F32).rearrange("p (a b) -> p a b", a=4)
    cqb = sb("cqb", [128, 4, 256])
    ymix = sb("ymix", [128, 4, D], BF16)
    cqnT = sb("cqnT", [128, 4, 2, 128], BF16)
    tabs = sb("tabs", [128, 4, 384])
    rbuf = sb("rbuf", [128, 2, D])
    ystage = sb("ystage", [128, 1, D])
    sgt = sb("sgt", [128, 2, 512])
    t1 = sb("t1", [128, 512])
    t2 = sb("t2", [128, 512])
    lst = sb("lst", [128, 2, 2, 6])
    lmv = sb("lmv", [128, 2, 4])
    sq = sb("sq", [128, 256])
    rms = sb("rms", [128, 8])
    ckvn = sb("ckvn", [128, 2, 128])
    kper = sb("kper", [128, 2, 128])
    cqn = sb("cqn", [128, 256])
    qT = sb("qT", [128, 4, 128], BF16)
    kT = sb("kT", [128, 4, 128], BF16)
    scm = sb("scm", [128, 4, 128], BF16)
    gst = sb("gst", [128, 4, 6])
    gmv = sb("gmv", [128, 4, 2])
    gsm = sb("gsm", [128, 12])
    on = sb("on", [128, 512])
    qnT = sb("qnT", [128, 4, 128], BF16)
    qabsT2 = sb("qabsT", [128, 2, 4, 128], BF16)
    qrr = sb("qrr", [128, 4, 64])
    qrT2 = sb("qrT", [128, 2, 4, 128], BF16)
    onesb = sb("onesb", [128, 128], BF16)
    rsum = sb("rsum", [128, 512])
    qTm = rsum[:].bitcast(BF16).rearrange("p (s h d) -> p s h d", s=2, h=4)
    pT = sb("pT", [128, 2, 512], BF16)
    olatT = sb("olatT", [128, 4, 128], BF16)
    ymf = ymix[:, :, :].rearrange("p a b -> p (a b)").bitcast(F32)
    ctmp = ymf[:, 0:1024].rearrange("p (a b) -> p a b", a=8)
    ktmp = ymf[:, 1024:2048].rearrange("p (a b) -> p a b", a=8)
    pb = [es.enter_context(nc.psum_tensor("pb%d" % i, [128, 512], F32)) for i in range(8)]

    def PB(i):
        return ("pb", i)

    YM = [("ymix", p) for p in range(4)]

    def OV(c):
        return ("ov", c)

    def Kq(pos):
        return OV(pos)

    def Kk(pos):
        return OV(4 + pos)

    def Kv(pos):
        return OV(8 + pos)

    def Kg(pos):
        return OV(12 + pos)

    def Kl(pos):
        return [OV(16 + (640 * pos) // 512), OV(16 + (640 * pos + 639) // 512)]

    misc_banks = [0, 1, 6, 7]
    misc_ctr = [0]

    def nb():
        b = misc_banks[misc_ctr[0] % len(misc_banks)]
        misc_ctr[0] += 1
        return b

    evac_ctr = [0]

    def evac(out_ap, in_ap, r, w):
        evac_ctr[0] += 1
        if evac_ctr[0] % 2:
            S.op("act", lambda: nc.scalar.copy(out_ap, in_ap), r=r, w=w)
        else:
            S.op("dve", lambda: nc.vector.tensor_copy(out_ap, in_ap), r=r, w=w)

    S.dma("sp", lambda: nc.sync.dma_start(out=cst[:], in_=consts), w=["cst"], key="c0")
    S.dma("sp", lambda: nc.sync.dma_start(out=dect[:].rearrange("p a b c -> p (a b c)"), in_=dect_in), w=["dect"], key="c0", join=True)
    S.dma("sp", lambda: nc.sync.dma_start(out=flg[:], in_=flg_in), w=["flg"], key="c0", join=True)
    S.dma("sp", lambda: nc.sync.dma_start(out=gng[:], in_=gn_g.partition_broadcast(128)), w=["gng"], key="c0", join=True)
    S.dma("sp", lambda: nc.sync.dma_start(out=gqg[:], in_=gq_g.partition_broadcast(128)), w=["gqg"], key="c0", join=True)
    S.dma("sp", lambda: nc.sync.dma_start(out=gkvg[:], in_=gkv_g.partition_broadcast(128)), w=["gkvg"], key="c0", join=True)
    for i in range(6):
        S.dma("sp", lambda i=i: nc.sync.dma_start(out=lnv[:, i, :], in_=lnvec[i].partition_broadcast(128)), w=["lnv"], key="c0", join=True)

    if STAGE >= 1:
        for hq in range(11):
            for gu in range(2):
                src = w_in[0][:, gu * DFF + hq * 256: gu * DFF + (hq + 1) * 256].rearrange("(k p) n -> p k n", p=128)
                dst = winb[0][hq, :, :, gu, :]
                S.dma("pool", lambda src=src, dst=dst: nc.gpsimd.dma_start(out=dst, in_=src),
                      w=[("winb", 0, hq)], key=("pc_win", hq), join=(gu == 1))
        S.dma("pool", lambda: nc.gpsimd.dma_start(out=wuq[:], in_=w_uq.rearrange("(k p) n -> p k n", p=128)), w=["wuq"], key="pc_small")
        S.dma("pool", lambda: nc.gpsimd.dma_start(out=wukv[:], in_=w_ukv), w=["wukv"], key="pc_small", join=True)
        S.dma("sp", lambda: nc.sync.dma_start(out=ctmp[:].rearrange("p a b -> p (a b)"), in_=w_ukv), w=YM, key="c1")
        for pc in range(6):
            ncc = 4 if pc < 5 else 2
            src = w_out[0][pc * 512: pc * 512 + ncc * 128, :].rearrange("(c p) n -> p c n", p=128)
            dst = woutb[0][pc, :, 0:ncc, :]
            S.dma("pool", lambda src=src, dst=dst: nc.gpsimd.dma_start(out=dst, in_=src),
                  w=[("woutb", 0, pc)], key=("pc_wout", pc))
        for g in range(5):
            n = 512 if g < 4 else 448
            src = w_mix_in[:, g * 512: g * 512 + n].rearrange("(k p) n -> p k n", p=128)
            dst = wmib[g, :, :, 0:n]
            S.dma("pool", lambda src=src, dst=dst: nc.gpsimd.dma_start(out=dst, in_=src), w=[("wmib", g)], key=("pc_wmi", g))
        for hf in range(2):
            src = w_mix_out[:, hf * 512:(hf + 1) * 512].rearrange("(k p) n -> p k n", p=128)
            dst = wmob[hf]
            S.dma("pool", lambda src=src, dst=dst: nc.gpsimd.dma_start(out=dst, in_=src), w=[("wmob", hf)], key="pc_wmo", join=(hf == 1))
        first = True
        for hq in range(11):
            for gu in range(2):
                src = w_in[1][:, gu * DFF + hq * 256: gu * DFF + (hq + 1) * 256].rearrange("(k p) n -> p k n", p=128)
                dst = winb[1][hq, :, :, gu, :]
                S.dma("pool", lambda src=src, dst=dst: nc.gpsimd.dma_start(out=dst, in_=src),
                      w=[("winb", 1, hq)], key="pc2", join=not first)
                first = False
        for pc in range(6):
            ncc = 4 if pc < 5 else 2
            src = w_out[1][pc * 512: pc * 512 + ncc * 128, :].rearrange("(c p) n -> p c n", p=128)
            dst = woutb[1][pc, :, 0:ncc, :]
            S.dma("pool", lambda src=src, dst=dst: nc.gpsimd.dma_start(out=dst, in_=src),
                  w=[("woutb", 1, pc)], key="pc2", join=True)

    if STAGE >= 2:
        S.op("dve", lambda: nc.vector.tensor_copy(identb[:], cst[:, 0, :]), r=["cst"], w=["identb"])
        S.op("pool", lambda: nc.gpsimd.memset(aug[:, :, 128:129], 1.0), w=[("aug", s) for s in range(NSLOT)])
        S.op("pool", lambda: nc.gpsimd.memset(Sst[:], 0.0), w=["S"])
        S.op("pool", lambda: nc.gpsimd.memset(qrT2[:], 0.0), w=[("qrT", 0), ("qrT", 1)])
        S.op("pool", lambda: nc.gpsimd.memset(onesb[:], 1.0), w=["onesb"])
        b = nb()
        for h in range(4):
            S.op("pe", lambda h=h, b=b: nc.tensor.transpose(pb[b][:, h * 128:(h + 1) * 128], ctmp[:, 2 * h, :], ident_f),
                 r=YM + ["cst"], w=[PB(b)])
        evac(wukT[:], pb[b][:].rearrange("p (a b) -> p a b", a=4), r=[PB(b)], w=["wukT"])

    blocks = [("b0", [0, 1])] + [("ctx", list(range(2 + 4 * i, 6 + 4 * i))) for i in range(4)] + \
             [("own", list(range(18 + 4 * i, 22 + 4 * i))) for i in range(4)]
    plan = []
    for kind, tiles in blocks:
        nrep = 1 if len(tiles) <= 2 else 2
        plan += [("win", 0, hq) for hq in range(11)] + [("wout", 0, pc) for pc in range(6)] * nrep
        plan += [("wmi", g) for g in ([0, 1, 2, 3, 4] if kind != "ctx" else [1, 2, 4])]
        if kind != "ctx":
            plan += [("wmo", 0), ("wmo", 1)]
            plan += [("win", 1, hq) for hq in range(11)] + [("wout", 1, pc) for pc in range(6)] * nrep
    ws_state = dict(next_load=0, next_use=0)

    def ws_issue(i):
        p = plan[i]
        slot = i % NS
        dst = ring[:, slot, 0:4096]
        if p[0] == "win":
            src = winb[p[1]][p[2]].rearrange("p k g n -> p (k g n)")
            key = ("winb", p[1], p[2])
        elif p[0] == "wout":
            ncc = 4 if p[2] < 5 else 2
            src = woutb[p[1]][p[2]][:, 0:ncc, :].rearrange("p c n -> p (c n)")
            dst = ring[:, slot, 0:ncc * 1024]
            key = ("woutb", p[1], p[2])
        elif p[0] == "wmi":
            n = 512 if p[1] < 4 else 448
            src = wmib[p[1]][:, :, 0:n]
            dst = ring[:, slot, :].rearrange("p (k n) -> p k n", k=8)[:, :, 0:n]
            key = ("wmib", p[1])
        else:
            src = wmob[p[1]].rearrange("p k n -> p (k n)")
            key = ("wmob", p[1])
        S.dma("sp", lambda: nc.sync.dma_start(out=dst, in_=src), r=[key], w=[("ring", slot)], key=("ring", slot))

    def ws_next(expect, hold=0):
        i = ws_state["next_use"]
        assert plan[i] == expect, (plan[i], expect)
        while ws_state["next_load"] < min(len(plan), i + NS - hold):
            ws_issue(ws_state["next_load"])
            ws_state["next_load"] += 1
        ws_state["next_use"] += 1
        return i % NS

    def make_xT(npos, positions=None):
        for pos in (range(npos) if positions is None else positions):
            for hb in range(2):
                bk = hb
                for kk in range(4):
                    k = hb * 4 + kk
                    S.op("pe", lambda pos=pos, k=k, kk=kk, bk=bk: nc.tensor.transpose(
                        pb[bk][:, kk * 128:(kk + 1) * 128], xres[:, pos, k * 128:(k + 1) * 128], ident_f),
                        r=[("xres", pos), "cst"], w=[PB(bk)])
                evac(xT[:, hb * 4:hb * 4 + 4, pos * 128:(pos + 1) * 128], pb[bk][:].rearrange("p (a b) -> p a b", a=4),
                     r=[PB(bk)], w=[("xT", pos)])

    ln_ctr = [0]

    def layer_norm(rb, gi, dst_ap, dst_keys):
        i = ln_ctr[0] % 2
        ln_ctr[0] += 1
        R = ("rbuf", rb)
        for hf in range(2):
            S.op("dve", lambda hf=hf: nc.vector.bn_stats(lst[:, i, hf, :], rbuf[:, rb, hf * 512:(hf + 1) * 512]), r=[R], w=[("lst", i)])
        S.op("dve", lambda: nc.vector.bn_aggr(lmv[:, i, 0:2], lst[:, i, :, :].rearrange("p a b -> p (a b)")), r=[("lst", i)], w=[("lmv", i)])
        S.op("act", lambda: nc.scalar.activation(lmv[:, i, 2:3], lmv[:, i, 1:2], AF.Ln, bias=LN_EPS / (ALPHA * ALPHA), scale=1.0),
             r=[("lmv", i)], w=[("lmv", i)])
        S.op("act", lambda: nc.scalar.activation(lmv[:, i, 2:3], lmv[:, i, 2:3], AF.Exp, scale=-0.5), r=[("lmv", i)], w=[("lmv", i)])
        S.op("dve", lambda: nc.vector.tensor_scalar(lmv[:, i, 3:4], lmv[:, i, 0:1], lmv[:, i, 2:3], -1.0, ALU.mult, ALU.mult),
             r=[("lmv", i)], w=[("lmv", i)])
        S.op("act", lambda: nc.scalar.activation(rbuf[:, rb, :], rbuf[:, rb, :], AF.Identity, bias=lmv[:, i, 3:4], scale=lmv[:, i, 2:3]),
             r=[R, ("lmv", i)], w=[R])
        S.op("pool", lambda: nc.gpsimd.tensor_tensor(rbuf[:, rb, :], rbuf[:, rb, :], lnv[:, gi, :], ALU.mult), r=[R, "lnv"], w=[R])
        S.op("pool", lambda: nc.gpsimd.tensor_tensor(dst_ap, rbuf[:, rb, :], lnv[:, gi + 1, :], ALU.add), r=[R, "lnv"], w=dst_keys)

    def rms_norm(src_ap, n, g_ap, dst_ap, r, w, col):
        S.op("act", lambda: nc.scalar.activation(sq[:, 0:n], src_ap, AF.Square, accum_out=rms[:, col:col + 1]), r=r, w=["sq", ("rms", col)])
        S.op("act", lambda: nc.scalar.activation(rms[:, col:col + 1], rms[:, col:col + 1], AF.Ln, bias=RMS_EPS, scale=1.0 / n),
             r=[("rms", col)], w=[("rms", col)])
        S.op("act", lambda: nc.scalar.activation(rms[:, col:col + 1], rms[:, col:col + 1], AF.Exp, scale=-0.5), r=[("rms", col)], w=[("rms", col)])
        S.op("dve", lambda: nc.vector.scalar_tensor_tensor(dst_ap, src_ap, rms[:, col:col + 1], g_ap, ALU.mult, ALU.mult),
             r=list(r) + [("rms", col)], w=w)

    rope_ctr = [0]

    def rope_from(src3, H, Dh, cc_ap, ss_ap, out3, r, w, dec_ap=None):
        hh = Dh // 2
        n = H * Dh
        rope_ctr[0] += 1
        if rope_ctr[0] % 2:
            b1, b2, k1, k2a, k2b = t1, t2, "t1", "t2a", "t2b"
        else:
            b1, b2, k1, k2a, k2b = sgt[:, 0, :], sgt[:, 1, :], ("sgt", 0), ("sgt", 1), ("sgt", 1)
        a1 = b1[:, 0:n].rearrange("p (h d) -> p h d", h=H)
        a2 = b2[:, 0:n].rearrange("p (h d) -> p h d", h=H)
        ccb = cc_ap.unsqueeze(1).broadcast_to([128, H, Dh])
        S.op("dve", lambda: nc.vector.tensor_tensor(a1, src3, ccb, ALU.mult), r=r, w=[k1])
        S.op("dve", lambda: nc.vector.tensor_tensor(a2[:, :, 0:hh], src3[:, :, hh:Dh], ss_ap[:, 0:hh].unsqueeze(1).broadcast_to([128, H, hh]), ALU.mult),
             r=r, w=[k2a])
        S.op("dve", lambda: nc.vector.tensor_tensor(a2[:, :, hh:Dh], src3[:, :, 0:hh], ss_ap[:, hh:Dh].unsqueeze(1).broadcast_to([128, H, hh]), ALU.mult),
             r=r, w=[k2b])
        if dec_ap is None:
            S.op("pool", lambda: nc.gpsimd.tensor_tensor(out3, a1, a2, ALU.add), r=[k1, k2a, k2b], w=w)
        else:
            S.op("pool", lambda: nc.gpsimd.tensor_tensor(a1, a1, a2, ALU.add), r=[k2a, k2b], w=[k1])
            S.op("pool", lambda: nc.gpsimd.tensor_tensor(out3, a1, dec_ap.unsqueeze(2).broadcast_to([128, H, Dh]), ALU.mult), r=[k1, "dect"], w=w)

    def ffn(f, npos, dsts):
        T = npos * 128
        make_xT(npos)
        xkeys = [("xT", p) for p in range(npos)]
        hkeys = ["ovl"]
        for hq in range(11):
            slot = ws_next(("win", f, hq))
            wv = ring[:, slot, :].rearrange("p (k g n) -> p k g n", k=8, g=2)
            for cc in range(2):
                c = 2 * hq + cc
                idx = c % 2
                for gu in range(2):
                    bk = 2 * gu + idx
                    for k in range(8):
                        S.op("pe", lambda bk=bk, k=k, gu=gu, cc=cc, wv=wv: nc.tensor.matmul(
                            pb[bk][:, 0:T], wv[:, k, gu, cc * 128:(cc + 1) * 128], xT[:, k, 0:T], start=(k == 0), stop=(k == 7)),
                            r=[("ring", slot)] + xkeys, w=[PB(bk)])
                S.op("act", lambda idx=idx: nc.scalar.activation(sgt[:, idx, 0:T], pb[idx][:, 0:T], AF.Silu), r=[PB(idx)], w=[("sgt", idx)])
                S.op("dve", lambda idx=idx, c=c: nc.vector.tensor_tensor(hid[:, c, 0:T], pb[2 + idx][:, 0:T], sgt[:, idx, 0:T], ALU.mult),
                     r=[PB(2 + idx), ("sgt", idx)], w=[OV(c)])
        groups = [list(range(npos))] if npos <= 2 else [[0, 1], [2, 3]]
        for grp in groups:
            for pc in range(6):
                slot = ws_next(("wout", f, pc))
                ncc = 4 if pc < 5 else 2
                wv = ring[:, slot, :].rearrange("p (c n) -> p c n", c=4)
                for pos in grp:
                    for hf in range(2):
                        bk = pos * 2 + hf
                        for cc in range(ncc):
                            c = pc * 4 + cc
                            S.op("pe", lambda bk=bk, c=c, cc=cc, pos=pos, hf=hf, wv=wv: nc.tensor.matmul(
                                pb[bk][:, :], hid[:, c, pos * 128:(pos + 1) * 128], wv[:, cc, hf * 512:(hf + 1) * 512],
                                start=(c == 0), stop=(c == NCH - 1)),
                                r=[("ring", slot), OV(c)], w=[PB(bk)])
            for pos in grp:
                rb = pos % 2
                for hf in range(2):
                    bk = pos * 2 + hf
                    S.op("dve", lambda bk=bk, hf=hf, pos=pos, rb=rb: nc.vector.scalar_tensor_tensor(
                        rbuf[:, rb, hf * 512:(hf + 1) * 512], pb[bk][:, :], 0.5 / ALPHA, xres[:, pos, hf * 512:(hf + 1) * 512], ALU.mult, ALU.add),
                        r=[PB(bk), ("xres", pos)], w=[("rbuf", rb)])
                dst_ap, dst_keys, after = dsts[pos]
                layer_norm(rb, 0 if f == 0 else 4, dst_ap, dst_keys)
                if after is not None:
                    after()

    def tile_info(t):
        if t == 0:
            return dict(kind="meta", slot=0, typ=1, K=16)
        if t == 1:
            return dict(kind="samp", slot=17, typ=2, K=64)
        if t < 18:
            return dict(kind="ctx", slot=1 + (t - 2), typ=0, K=128, j=t - 2)
        return dict(kind="own", slot=17 + (t - 18), typ=0, K=128, j=t - 18)

    out_ctr = [0]

    def front_latents(t, pos, info):
        slot = info["slot"]
        i = out_ctr[0] % 2
        out_ctr[0] += 1
        L = Kl(pos)
        rms_norm(lat[:, pos, 0:128], 128, gkvg[:], ckvn[:, i, :], r=L + ["gkvg"], w=[("ckvn", i)], col=i)
        rope_from(lat[:, pos, 128:192].unsqueeze(1), 1, 64, tabs[:, pos, 256:320], tabs[:, pos, 320:384],
                  kper[:, i, 0:64].unsqueeze(1), r=L + [("tabs", pos)], w=[("kper", i)])
        S.op("pool", lambda: nc.gpsimd.tensor_copy(kper[:, i, 64:128], kper[:, i, 0:64]), r=[("kper", i)], w=[("kper", i)])
        kind = info["kind"]
        if kind == "own":
            j = info["j"]
            S.dma("pool", lambda: nc.gpsimd.dma_start(out=ckv_own[j * 128:(j + 1) * 128, :], in_=ckvn[:, i, :]), r=[("ckvn", i)], key=("o_lat", i))
            S.dma("pool", lambda: nc.gpsimd.dma_start(out=kpe_own[j * 128:(j + 1) * 128, :], in_=kper[:, i, 0:64]), r=[("kper", i)], key=("o_lat", i), join=True)
        elif kind == "meta":
            S.dma("pool", lambda: nc.gpsimd.dma_start(out=ckv_meta, in_=ckvn[0:16, i, :]), r=[("ckvn", i)], key=("o_lat", i))
            S.dma("pool", lambda: nc.gpsimd.dma_start(out=kpe_meta, in_=kper[0:16, i, 0:64]), r=[("kper", i)], key=("o_lat", i), join=True)
        elif kind == "samp":
            S.dma("pool", lambda: nc.gpsimd.dma_start(out=ckv_samp, in_=ckvn[0:64, i, :]), r=[("ckvn", i)], key=("o_lat", i))
            S.dma("pool", lambda: nc.gpsimd.dma_start(out=kpe_samp, in_=kper[0:64, i, 0:64]), r=[("kper", i)], key=("o_lat", i), join=True)
        S.op("act", lambda: nc.scalar.copy(aug[:, slot, 0:128], ckvn[:, i, :]), r=[("ckvn", i)], w=[("aug", slot)])
        b = nb()
        S.op("pe", lambda: nc.tensor.transpose(pb[b][:, 0:128], ckvn[:, i, :], ident_f), r=[("ckvn", i), "cst"], w=[PB(b)])
        S.op("pe", lambda: nc.tensor.transpose(pb[b][:, 128:256], kper[:, i, :], ident_f), r=[("kper", i), "cst"], w=[PB(b)])
        S.op("dve", lambda: nc.vector.tensor_copy(ckvT[:, slot * 128:(slot + 1) * 128], pb[b][:, 0:128]), r=[PB(b)], w=[("ckvT", slot)])
        S.op("act", lambda: nc.scalar.copy(kpeT[:, slot * 128:(slot + 1) * 128], pb[b][:, 128:256]), r=[PB(b)], w=[("kpeT", slot)])
        if kind in ("own", "samp"):
            rms_norm(cqb[:, pos, :], 256, gqg[:], cqn[:], r=[("cqb", pos), "gqg"], w=["cqn"], col=2 + i)
            b2 = nb()
            for kc in range(2):
                S.op("pe", lambda kc=kc: nc.tensor.transpose(pb[b2][:, kc * 128:(kc + 1) * 128], cqn[:, kc * 128:(kc + 1) * 128], ident_f),
                     r=["cqn", "cst"], w=[PB(b2)])
            evac(cqnT[:, pos, :, :], pb[b2][:, 0:256].rearrange("p (a b) -> p a b", a=2), r=[PB(b2)], w=[("cqnT", pos)])

    def mix_in_block(kind, tiles):
        npos = len(tiles)
        groups = [0, 1, 2, 3, 4] if kind != "ctx" else [1, 2, 4]
        for t_i, t in enumerate(tiles):
            S.dma("sp", lambda t=t, t_i=t_i: nc.sync.dma_start(out=tabs[:, t_i, :], in_=tab_all[t]), w=[("tabs", t_i)], key=("tabs", t_i))
        mb = [4, 5, 6, 7]
        ctr = 0
        slots = {}
        slots[groups[0]] = ws_next(("wmi", groups[0]))
        slots[groups[1]] = ws_next(("wmi", groups[1]), hold=1)
        if npos == 4:
            work = [("T", [0, 1]), ("M", groups[0], [0, 1]), ("M", groups[1], [0, 1]), ("T", [2, 3]), ("M", groups[0], [2, 3]), ("M", groups[1], [2, 3])]
        else:
            work = [("T", list(range(npos))), ("M", groups[0], list(range(npos))), ("M", groups[1], list(range(npos)))]
        for g in groups[2:]:
            work.append(("M", g, list(range(npos))))
        for item in work:
            if item[0] == "T":
                make_xT(npos, positions=item[1])
                continue
            g = item[1]
            if g not in slots:
                slots[g] = ws_next(("wmi", g))
            slot = slots[g]
            n = 512 if g < 4 else 448
            wv = ring[:, slot, :].rearrange("p (k n) -> p k n", k=8)
            for pos in item[2]:
                t = tiles[pos]
                info = tile_info(t)
                bk = mb[ctr % 4]
                ctr += 1
                for k in range(8):
                    S.op("pe", lambda bk=bk, k=k, pos=pos, n=n, wv=wv: nc.tensor.matmul(
                        pb[bk][:, 0:n], xT[:, k, pos * 128:(pos + 1) * 128], wv[:, k, 0:n], start=(k == 0), stop=(k == 7)),
                        r=[("ring", slot), ("xT", pos)], w=[PB(bk)])
                src3 = pb[bk][:, :].rearrange("p (h d) -> p h d", h=4)
                typ = info["typ"]
                if g == 0:
                    rope_from(src3, 4, 128, tabs[:, pos, 0:128], tabs[:, pos, 128:256],
                              qd[:, pos, :].rearrange("p (h d) -> p h d", h=4), r=[PB(bk), ("tabs", pos)], w=[Kq(pos)], dec_ap=dect[:, typ, 0, :])
                elif g == 1:
                    rope_from(src3, 4, 128, tabs[:, pos, 0:128], tabs[:, pos, 128:256],
                              kd[:, pos, :].rearrange("p (h d) -> p h d", h=4), r=[PB(bk), ("tabs", pos)], w=[Kk(pos)], dec_ap=dect[:, typ, 1, :])
                elif g == 2:
                    S.op("act", lambda bk=bk, pos=pos: nc.scalar.copy(vv[:, pos, :], pb[bk][:, :]), r=[PB(bk)], w=[Kv(pos)])
                elif g == 3:
                    S.op("act", lambda bk=bk, pos=pos: nc.scalar.activation(sgr[:, pos, :], pb[bk][:, :], AF.Silu), r=[PB(bk)], w=[Kg(pos)])
                else:
                    S.op("act", lambda bk=bk, pos=pos: nc.scalar.copy(cqb[:, pos, :], pb[bk][:, 0:256]), r=[PB(bk)], w=[("cqb", pos)])
                    S.op("dve", lambda bk=bk, pos=pos: nc.vector.tensor_copy(lat[:, pos, 0:192], pb[bk][:, 256:448]), r=[PB(bk)], w=Kl(pos))
        for pos, t in enumerate(tiles):
            front_latents(t, pos, tile_info(t))

    def state_update(pos, K, typ, S_t, Sbf_t, skey, base=0):
        b = nb()
        kv = pb[b][:, :].rearrange("p (h d) -> p h d", h=4)
        for h in range(4):
            S.op("pe", lambda h=h: nc.tensor.matmul(kv[:, h, :], kd[base:base + K, pos, h * 128:(h + 1) * 128],
                                                    vv[base:base + K, pos, h * 128:(h + 1) * 128], start=True, stop=True),
                 r=[Kk(pos), Kv(pos)], w=[PB(b)])
        S.op("dve", lambda: nc.vector.tensor_tensor(Tt[:], kv, S_t[:], ALU.add), r=[PB(b), skey], w=["Tt"])
        S.op("pool", lambda: nc.gpsimd.tensor_tensor(S_t[:], Tt[:], dect[:, typ, 2, :].unsqueeze(2).broadcast_to([128, 4, 128]), ALU.mult),
             r=["Tt", "dect"], w=[skey])
        S.op("act", lambda: nc.scalar.copy(Sbf_t[:], S_t[:]), r=[skey], w=[skey + "bf"])

    def retention_tile(pos, info):
        samp = info["kind"] == "samp"
        for src, dstT, nm, kf in ((qd, qT, "qd", Kq), (kd, kT, "kd", Kk)):
            b = nb()
            pbb = pb[b][:].bitcast(BF16)
            for h in range(4):
                S.op("pe", lambda h=h, src=src, pbb=pbb: nc.tensor.transpose(pbb[:, h * 128:(h + 1) * 128], src[:, pos, h * 128:(h + 1) * 128], identb[:]),
                     r=[kf(pos), "identb"], w=[PB(b)])
            evac(dstT[:], pbb[:, 0:512].rearrange("p (a b) -> p a b", a=4), r=[PB(b)], w=[nm + "T"])
            yield
        b = nb()
        sc = pb[b][:, :].rearrange("p (h d) -> p h d", h=4)
        for h in range(4):
            S.op("pe", lambda h=h: nc.tensor.matmul(sc[:, h, :], kT[:, h, :], qT[:, h, :], start=True, stop=True), r=["kdT", "qdT"], w=[PB(b)])
        mi = 3 if samp else 1
        S.op("dve", lambda: nc.vector.tensor_tensor(scm[:], sc, cst[:, mi, :].unsqueeze(1).broadcast_to([128, 4, 128]), ALU.mult),
             r=[PB(b), "cst"], w=["scm"])
        yield
        if samp:
            for s in range(2):
                S.op("pool", lambda s=s: nc.gpsimd.tensor_tensor(qTm[:, s, :, :], qT[:], cst[:, 5 + s, :].unsqueeze(1).broadcast_to([128, 4, 128]), ALU.mult),
                     r=["qdT", "cst"], w=["rsum"])
        bo = nb()
        o3 = pb[bo][:, :].rearrange("p (h d) -> p h d", h=4)
        for h in range(4):
            S.op("pe", lambda h=h: nc.tensor.matmul(o3[:, h, :], scm[:, h, :], vv[:, pos, h * 128:(h + 1) * 128], start=True, stop=False),
                 r=["scm", Kv(pos)], w=[PB(bo)])
            if samp:
                for s in range(2):
                    S.op("pe", lambda h=h, s=s: nc.tensor.matmul(o3[:, h, :], qTm[:, s, h, :], Ssbf[s][:, h, :], start=False, stop=(s == 1)),
                         r=["rsum", "Ss%dbf" % s], w=[PB(bo)])
            else:
                S.op("pe", lambda h=h: nc.tensor.matmul(o3[:, h, :], qT[:, h, :], Sbf[:, h, :], start=False, stop=True),
                     r=["qdT", "Sbf"], w=[PB(bo)])
        yield
        if samp:
            for s in range(2):
                state_update(pos, 32, 2, Ss[s], Ssbf[s], "Ss%d" % s, base=32 * s)
        else:
            state_update(pos, 128, 0, Sst, Sbf, "S")
        yield
        for h in range(4):
            S.op("dve", lambda h=h: nc.vector.bn_stats(gst[:, h, :], o3[:, h, :]), r=[PB(bo)], w=[("gst", h)])
            S.op("dve", lambda h=h: nc.vector.bn_aggr(gmv[:, h, :], gst[:, h, :]), r=[("gst", h)], w=["gmv"])
        S.op("act", lambda: nc.scalar.activation(gsm[:, 0:4], gmv[:, :, 1], AF.Ln, bias=LN_EPS, scale=1.0), r=["gmv"], w=["gsm"])
        S.op("act", lambda: nc.scalar.activation(gsm[:, 4:8], gsm[:, 0:4], AF.Exp, scale=-0.5), r=["gsm"], w=["gsm"])
        S.op("dve", lambda: nc.vector.scalar_tensor_tensor(gsm[:, 8:12], gmv[:, :, 0], -1.0, gsm[:, 4:8], ALU.mult, ALU.mult), r=["gsm", "gmv"], w=["gsm"])
        yield
        for h in range(4):
            S.op("act", lambda h=h: nc.scalar.activation(on[:, h * 128:(h + 1) * 128], o3[:, h, :], AF.Identity,
                                                         bias=gsm[:, 8 + h:9 + h], scale=gsm[:, 4 + h:5 + h]), r=[PB(bo), "gsm"], w=["on"])
        S.op("pool", lambda: nc.gpsimd.tensor_tensor(on[:], on[:], gng[:], ALU.mult), r=["on", "gng"], w=["on"])
        S.op("pool", lambda: nc.gpsimd.tensor_tensor(ymix[:, pos, 0:512], on[:], sgr[:, pos, :], ALU.mult), r=["on", Kg(pos)], w=[("ymix", pos)])
        yield

    def mla_pre(pos, info, par):
        qabsT = qabsT2[:, par]
        qrT = qrT2[:, par]
        b = nb()
        q3 = pb[b][:, :].rearrange("p (h d) -> p h d", h=4)
        for h in range(4):
            for kc in range(2):
                S.op("pe", lambda h=h, kc=kc: nc.tensor.matmul(q3[:, h, :], wuq[:, kc, h * 192:h * 192 + 128], cqnT[:, pos, kc, :],
                                                               start=(kc == 0), stop=(kc == 1)), r=["wuq", ("cqnT", pos)], w=[PB(b)])
        S.op("act", lambda: nc.scalar.copy(qnT[:], q3), r=[PB(b)], w=["qnT"])
        yield
        b2 = nb()
        qa3 = pb[b2][:, :].rearrange("p (h d) -> p h d", h=4)
        for h in range(4):
            S.op("pe", lambda h=h: nc.tensor.matmul(qa3[:, h, :], wukT[:, h, :], qnT[:, h, :], start=True, stop=True), r=["wukT", "qnT"], w=[PB(b2)])
        S.op("dve", lambda: nc.vector.tensor_copy(qabsT, qa3), r=[PB(b2)], w=[("qabsT", par)])
        yield
        b3 = nb()
        qr3 = pb[b3][:, 0:256].rearrange("p (h d) -> p h d", h=4)
        wr = wuq[:, :, :].rearrange("p k (h d) -> p k h d", h=4)
        for kc in range(2):
            S.op("pe", lambda kc=kc: nc.tensor.matmul(qr3, cqnT[:, pos, kc, :], wr[:, kc, :, 128:192], start=(kc == 0), stop=(kc == 1)),
                 r=["wuq", ("cqnT", pos)], w=[PB(b3)])
        rope_from(qr3, 4, 64, tabs[:, pos, 256:320], tabs[:, pos, 320:384], qrr[:], r=[PB(b3), ("tabs", pos)], w=["qrr"])
        yield
        yield
        yield
        yield
        b4 = nb()
        for pr in range(2):
            S.op("pe", lambda pr=pr: nc.tensor.transpose(pb[b4][:, pr * 128:(pr + 1) * 128], qrr[:, 2 * pr:2 * pr + 2, :].rearrange("p a b -> p (a b)"), ident_f),
                 r=["qrr", "cst"], w=[PB(b4)])
        qr4 = qrT.rearrange("p (a b) d -> p a b d", b=2)
        S.op("dve", lambda: nc.vector.tensor_copy(qr4[0:64, :, 0, :], pb[b4][0:64, 0:256].rearrange("p (a d) -> p a d", a=2)), r=[PB(b4)], w=[("qrT", par)])
        S.op("act", lambda: nc.scalar.copy(qr4[64:128, :, 1, :], pb[b4][64:128, 0:256].rearrange("p (a d) -> p a d", a=2)), r=[PB(b4)], w=[("qrT", par)])
        yield

    def mla_attn(pos, info, keylist, par, side=None):
        qabsT = qabsT2[:, par]
        qrT = qrT2[:, par]
        qa_flat = qabsT.rearrange("p a b -> p (a b)")
        qr_flat = qrT.rearrange("p a b -> p (a b)")
        nk_tiles = len(keylist)

        def emit_qk(ji):
            slot = keylist[ji][0]
            sb_i = 2 + ji % 2
            S.op("pe", lambda: nc.tensor.matmul(pb[sb_i][:, :], ckvT[:, slot * 128:(slot + 1) * 128], qa_flat, start=True, stop=False),
                 r=[("ckvT", slot), ("qabsT", par)], w=[PB(sb_i)])
            S.op("pe", lambda: nc.tensor.matmul(pb[sb_i][:, :], kpeT[:, slot * 128:(slot + 1) * 128], qr_flat, start=False, stop=True),
                 r=[("kpeT", slot), ("qrT", par)], w=[PB(sb_i)])

        emit_qk(0)
        for ji, (slot, biascol, mask) in enumerate(keylist):
            sb_i = 2 + ji % 2
            pi = ji % 2
            if ji + 1 < nk_tiles:
                emit_qk(ji + 1)
            if biascol is None:
                S.op("act", lambda sb_i=sb_i, pi=pi: nc.scalar.activation(pT[:, pi, :], pb[sb_i][:, :], AF.Exp, scale=SCALE),
                     r=[PB(sb_i)], w=[("pT", pi)])
            else:
                S.op("act", lambda sb_i=sb_i, pi=pi, biascol=biascol: nc.scalar.activation(pT[:, pi, :], pb[sb_i][:, :], AF.Exp, scale=SCALE,
                                                                                           bias=flg[:, biascol:biascol + 1]),
                     r=[PB(sb_i), "flg"], w=[("pT", pi)])
            if mask is not None:
                S.op("pool", lambda pi=pi, mask=mask: nc.gpsimd.tensor_tensor(
                    pT[:, pi, :].rearrange("p (h d) -> p h d", h=4), pT[:, pi, :].rearrange("p (h d) -> p h d", h=4),
                    cst[:, mask, :].unsqueeze(1).broadcast_to([128, 4, 128]), ALU.mult), r=[("pT", pi), "cst"], w=[("pT", pi)])
            S.op("pe", lambda pi=pi, slot=slot, ji=ji: nc.tensor.matmul(pb[4][:, :], aug[:, slot, 0:128], pT[:, pi, :],
                                                                       start=(ji == 0), stop=(ji == nk_tiles - 1)),
                 r=[("pT", pi), ("aug", slot)], w=[PB(4)])
            S.op("pe", lambda pi=pi, ji=ji: nc.tensor.matmul(pb[5][:, :], onesb[:], pT[:, pi, :],
                                                            start=(ji == 0), stop=(ji == nk_tiles - 1)),
                 r=[("pT", pi), "onesb"], w=[PB(5)])
            if side is not None:
                next(side, None)
        S.op("dve", lambda: nc.vector.reciprocal(rsum[:], pb[5][:, :]), r=[PB(5)], w=["rsum"])
        S.op("dve", lambda: nc.vector.tensor_tensor(olatT[:].rearrange("p a b -> p (a b)"), pb[4][:, :], rsum[:], ALU.mult), r=[PB(4), "rsum"], w=["olatT"])
        b6 = nb()
        my3 = pb[b6][:, :].rearrange("p (h d) -> p h d", h=4)
        for h in range(4):
            S.op("pe", lambda h=h: nc.tensor.matmul(my3[:, h, :], olatT[:, h, :], wukv[:, h * 256 + 128:h * 256 + 256], start=True, stop=True),
                 r=["olatT", "wukv"], w=[PB(b6)])
        S.op("act", lambda: nc.scalar.copy(ymix[:, pos, 512:1024], pb[b6][:, :]), r=[PB(b6)], w=[("ymix", pos)])

    def mix_out_block(tiles):
        npos = len(tiles)
        for pos in range(npos):
            b = nb()
            pbb = pb[b][:].bitcast(BF16)
            for k in range(8):
                S.op("pe", lambda k=k, pbb=pbb, pos=pos: nc.tensor.transpose(pbb[:, k * 128:(k + 1) * 128], ymix[:, pos, k * 128:(k + 1) * 128], identb[:]),
                     r=[("ymix", pos), "identb"], w=[PB(b)])
            evac(xT[:, :, pos * 128:(pos + 1) * 128], pbb[:, :].rearrange("p (a b) -> p a b", a=8), r=[PB(b)], w=[("xT", pos)])
        slots = [ws_next(("wmo", 0)), ws_next(("wmo", 1), hold=1)]
        for pos in range(npos):
            rb = pos % 2
            for hf in range(2):
                bk = nb()
                wv = ring[:, slots[hf], :].rearrange("p (k n) -> p k n", k=8)
                for k in range(8):
                    S.op("pe", lambda bk=bk, k=k, pos=pos, wv=wv: nc.tensor.matmul(pb[bk][:, :], xT[:, k, pos * 128:(pos + 1) * 128], wv[:, k, :],
                                                                                start=(k == 0), stop=(k == 7)),
                         r=[("ring", slots[hf]), ("xT", pos)], w=[PB(bk)])
                S.op("dve", lambda bk=bk, hf=hf, pos=pos, rb=rb: nc.vector.scalar_tensor_tensor(
                    rbuf[:, rb, hf * 512:(hf + 1) * 512], pb[bk][:, :], 1.0 / ALPHA, xres[:, pos, hf * 512:(hf + 1) * 512], ALU.mult, ALU.add),
                    r=[PB(bk), ("xres", pos)], w=[("rbuf", rb)])
            layer_norm(rb, 2, xres[:, pos, :], [("xres", pos)])

    def load_cache():
        for s in range(2):
            S.dma("sp", lambda s=s: nc.sync.dma_start(out=ctmp[:], in_=cache_ckv[s].rearrange("(i p) d -> p i d", p=128)), w=YM, key="cache")
            for dup in range(2):
                S.dma("sp", lambda s=s, dup=dup: nc.sync.dma_start(out=ktmp[:, :, dup * 64:(dup + 1) * 64], in_=cache_kpe[s].rearrange("(i p) d -> p i d", p=128)), w=YM, key="cache", join=True)
            S.op("act", lambda s=s: nc.scalar.copy(aug[:, 1 + 8 * s:9 + 8 * s, 0:128], ctmp[:]), r=YM, w=[("aug", 1 + 8 * s + i) for i in range(8)])
            for hb in range(2):
                b = nb()
                for i in range(4):
                    S.op("pe", lambda i=i, hb=hb, b=b: nc.tensor.transpose(pb[b][:, i * 128:(i + 1) * 128], ctmp[:, hb * 4 + i, :], ident_f),
                         r=YM + ["cst"], w=[PB(b)])
                c0 = (1 + 8 * s + 4 * hb) * 128
                evac(ckvT[:, c0:c0 + 512], pb[b][:, :], r=[PB(b)], w=[("ckvT", 1 + 8 * s + 4 * hb + i) for i in range(4)])
                b = nb()
                for i in range(4):
                    S.op("pe", lambda i=i, hb=hb, b=b: nc.tensor.transpose(pb[b][:, i * 128:(i + 1) * 128], ktmp[:, hb * 4 + i, :], ident_f),
                         r=YM + ["cst"], w=[PB(b)])
                evac(kpeT[:, c0:c0 + 512], pb[b][:, :], r=[PB(b)], w=[("kpeT", 1 + 8 * s + 4 * hb + i) for i in range(4)])
            S.dma("sp", lambda s=s: nc.sync.dma_start(out=Ss[s][:], in_=state_in[s].rearrange("h p d -> p h d")), w=["Ss%d" % s], key="cache2", join=(s == 1))
        for s in range(2):
            S.op("act", lambda s=s: nc.scalar.copy(Ssbf[s][:], Ss[s][:]), r=["Ss%d" % s], w=["Ss%dbf" % s])

    ycnt = [0]
    try:
        for bi, (kind, tiles) in enumerate(blocks):
            npos = len(tiles)
            if STAGE < 3:
                raise _Stop()
            if kind == "ctx" and STAGE < 8:
                raise _Stop()
            if kind == "own" and STAGE < 9:
                raise _Stop()
            if STAGE < 99 and bi >= 6 + max(0, STAGE - 9):
                raise _Stop()
            for pos, t in enumerate(tiles):
                S.dma("sp", lambda pos=pos, t=t: nc.sync.dma_start(out=xres[:, pos, :], in_=x_all[t]), w=[("xres", pos)], key=("xl", pos))
            ffn(0, npos, [(xres[:, pos, :], [("xres", pos)], None) for pos in range(npos)])
            if kind == "b0":
                if DBG:
                    S.dma("pool", lambda: nc.gpsimd.dma_start(out=dbg[:, 0:1024], in_=xres[:, 1, :]), r=[("xres", 1)], key="dbg0")
                if STAGE < 4:
                    raise _Stop()
                load_cache()
            mix_in_block(kind, tiles)
            if kind == "b0":
                if STAGE < 5:
                    raise _Stop()
                state_update(0, 16, 1, Sst, Sbf, "S")
                S.op("dve", lambda: nc.vector.tensor_copy(Smeta[:], Sst[:]), r=["S"], w=["Smeta"])
                info = tile_info(1)
                for _ in retention_tile(1, info):
                    pass
                if STAGE < 6:
                    raise _Stop()
                keylist = [(0, 2, None)] + [(1 + i, None, 5) for i in range(8)] + [(9 + i, None, 6) for i in range(8)] + [(17, None, 4)]
                for _ in mla_pre(1, info, 0):
                    pass
                mla_attn(1, info, keylist, 0)
                if DBG:
                    S.dma("pool", lambda: nc.gpsimd.dma_start(out=dbg[:, 1024:2048], in_=ymix[:, 1, :]), r=[("ymix", 1)], key="dbg1")
                if STAGE < 7:
                    raise _Stop()
                S.op("pool", lambda: nc.gpsimd.memset(ymix[:, 0, :], 0.0), w=[("ymix", 0)])
                mix_out_block(tiles)
                if DBG:
                    S.dma("pool", lambda: nc.gpsimd.dma_start(out=dbg[:, 2048:3072], in_=xres[:, 1, :]), r=[("xres", 1)], key="dbg2")
                for s in range(2):
                    S.dma("pool", lambda s=s: nc.gpsimd.dma_start(out=state_s[s].rearrange("h p d -> p h d"), in_=Ss[s][:]), r=["Ss%d" % s], key="o_state", join=(s == 1))

                def after_s(i):
                    def f():
                        S.dma("pool", lambda: nc.gpsimd.dma_start(out=y_samp, in_=ystage[0:64, i, :]), r=[("ystage", i)], key=("o_y", i))
                    return f
                dsts = []
                for pos in range(npos):
                    i = 0
                    ycnt[0] += 1
                    dsts.append((ystage[:, i, :], [("ystage", i)], after_s(i) if pos == 1 else None))
                ffn(1, npos, dsts)
            elif kind == "ctx":
                for pos, t in enumerate(tiles):
                    state_update(pos, 128, 0, Sst, Sbf, "S")
                if tiles[-1] == 17:
                    S.op("dve", lambda: nc.vector.tensor_tensor(Tt[:], Sst[:], Smeta[:], ALU.subtract), r=["S", "Smeta"], w=["Tt"])
                    S.op("dve", lambda: nc.vector.scalar_tensor_tensor(Sst[:].rearrange("p a b -> p (a b)"), Tt[:].rearrange("p a b -> p (a b)"), flg[:, 0:1],
                                                                       Smeta[:].rearrange("p a b -> p (a b)"), ALU.mult, ALU.add), r=["Tt", "Smeta", "flg"], w=["S"])
                    S.op("act", lambda: nc.scalar.copy(Sbf[:], Sst[:]), r=["S"], w=["Sbf"])
            else:
                def pre_gen(pos, t, par):
                    info = tile_info(t)
                    yield from retention_tile(pos, info)
                    yield from mla_pre(pos, info, par)

                def drain(g):
                    if g is not None:
                        for _ in g:
                            pass
                drain(pre_gen(0, tiles[0], 0))
                for pos, t in enumerate(tiles):
                    info = tile_info(t)
                    j = info["j"]
                    keylist = [(0, 2, None)] + [(1 + i, 1, None) for i in range(16)] + [(17 + i, None, None) for i in range(j)] + \
                              [(17 + j, None, 2)]
                    side = pre_gen(pos + 1, tiles[pos + 1], (pos + 1) % 2) if pos + 1 < len(tiles) else None
                    mla_attn(pos, info, keylist, pos % 2, side)
                    drain(side)
                mix_out_block(tiles)

                def after_o(i, j):
                    def f():
                        S.dma("pool", lambda: nc.gpsimd.dma_start(out=y_own[j * 128:(j + 1) * 128, :], in_=ystage[:, i, :]), r=[("ystage", i)], key=("o_y", i))
                    return f
                dsts = []
                for pos, t in enumerate(tiles):
                    i = 0
                    ycnt[0] += 1
                    dsts.append((ystage[:, i, :], [("ystage", i)], after_o(i, tile_info(t)["j"])))
                ffn(1, npos, dsts)
        S.dma("pool", lambda: nc.gpsimd.dma_start(out=state_p.rearrange("h p d -> p h d"), in_=Sst[:]), r=["S"], key="o_state2")

    except _Stop:
        pass
    stats = S.emit(es)
    es.close()
    return nc, stats


def _rope_tab(pos, d):
    inv = 10000.0 ** (-np.arange(0, d, 2, dtype=np.float64) / d)
    ang = pos.astype(np.float64)[:, None] * inv[None, :]
    c, s = np.cos(ang), np.sin(ang)
    return np.concatenate([c, c], 1), np.concatenate([-s, s], 1)


def _host_constants(r):
    pos = np.zeros((NT, 128), np.float64)
    pos[0] = np.arange(128)
    pos[1] = 16 + 1024 + (np.arange(128) % 32)
    for j in range(16):
        pos[2 + j] = 16 + j * 128 + np.arange(128)
        pos[18 + j] = 16 + r * 2048 + j * 128 + np.arange(128)
    tab = np.zeros((NT, 128, 384), np.float32)
    for t in range(NT):
        ccr, ssr = _rope_tab(pos[t], 128)
        ccm, ssm = _rope_tab(pos[t], 64)
        tab[t, :, 0:128] = ccr; tab[t, :, 128:256] = ssr
        tab[t, :, 256:320] = ccm; tab[t, :, 320:384] = ssm
    k = np.arange(128)[:, None]; q = np.arange(128)[None, :]
    consts = np.zeros((128, 7, 128), np.float32)
    consts[:, 0] = np.eye(128)
    consts[:, 1] = (q >= k)
    consts[:, 2] = 1.0 - ((k >= 64) & (q < 64))
    same = (k // 32 == q // 32) & (k < 64) & (q < 64)
    consts[:, 3] = same & (q >= k)
    consts[:, 4] = same
    consts[:, 5] = np.broadcast_to(q < 32, (128, 128))
    consts[:, 6] = np.broadcast_to((q >= 32) & (q < 64), (128, 128))
    gam = 1.0 - 2.0 ** (-5.0 - np.arange(4, dtype=np.float64))
    dect = np.zeros((128, 3, 3, 4), np.float64)
    row = np.arange(128, dtype=np.float64)
    for typ, (idx, C) in enumerate(((row, 128), (row, 16), (row % 32, 32))):
        dect[:, typ, 0, :] = gam[None, :] ** (idx[:, None] + 1.0)
        dect[:, typ, 1, :] = (128.0 ** -0.5) * gam[None, :] ** (-idx[:, None] - 1.0)
        dect[:, typ, 2, :] = gam[None, :] ** C
    flg = np.zeros((128, 4), np.float32)
    flg[16:, 2] = -30000.0
    flg[:, 0] = float(r)
    flg[:, 1] = 0.0 if r == 1 else -30000.0
    return tab, consts, dect.reshape(128, 36).astype(np.float32), flg


_CACHE = {}


def kernel(x_prompt, x_sample, cache_mla_ckv, cache_mla_kpe, state_ret, meta_tokens,
           ffn1_w_in, ffn1_w_out, ln1_g, ln1_b, w_mix_in, ret_gn_g, mla_q_norm_g, mla_w_uq,
           mla_kv_norm_g, mla_w_ukv, w_mix_out, ln2_g, ln2_b, ffn2_w_in, ffn2_w_out, ln3_g, ln3_b):
    f32 = lambda a: np.ascontiguousarray(np.asarray(a, dtype=np.float32))
    x_prompt = f32(x_prompt); x_sample = f32(x_sample)
    if "nc" not in _CACHE:
        _CACHE["nc"] = build_program()
    nc, stats = _CACHE["nc"]
    lnvec = np.stack([f32(ln1_g), f32(ln1_b), f32(ln2_g), f32(ln2_b), f32(ln3_g), f32(ln3_b)], 0)
    in_maps = []
    for c in range(8):
        b, r = c // 2, c % 2
        x_all = np.zeros((NT, 128, D), np.float32)
        x_all[0, 0:16] = f32(meta_tokens)
        x_all[1, 0:32] = x_sample[2 * c]
        x_all[1, 32:64] = x_sample[2 * c + 1]
        x_all[2:18] = x_prompt[b, 0:2048].reshape(16, 128, D)
        x_all[18:34] = x_prompt[b, r * 2048:(r + 1) * 2048].reshape(16, 128, D)
        tab, consts, dect, flg = _host_constants(r)
        in_maps.append(dict(
            x_all=x_all, tab_all=tab,
            cache_ckv=f32(cache_mla_ckv[2 * c:2 * c + 2]), cache_kpe=f32(cache_mla_kpe[2 * c:2 * c + 2]),
            state_in=f32(state_ret[2 * c:2 * c + 2]),
            ffn1_w_in=f32(ffn1_w_in), ffn2_w_in=f32(ffn2_w_in), ffn1_w_out=f32(ffn1_w_out), ffn2_w_out=f32(ffn2_w_out),
            w_mix_in=f32(w_mix_in), w_mix_out=f32(w_mix_out), mla_w_uq=f32(mla_w_uq), mla_w_ukv=f32(mla_w_ukv),
            lnvec=lnvec, ret_gn_g=f32(ret_gn_g), mla_q_norm_g=f32(mla_q_norm_g), mla_kv_norm_g=f32(mla_kv_norm_g),
            consts=consts, dect=dect, flg=flg))
    res = run_bass_kernel_spmd(nc, in_maps, core_ids=list(range(8)))
    R = res.results
    _CACHE["last"] = R
    y_prompt = np.zeros((4, 4096, D), np.float32)
    y_sample = np.zeros((16, 32, D), np.float32)
    p_ckv = np.zeros((4, 4112, 128), np.float32)
    p_kpe = np.zeros((4, 4112, 64), np.float32)
    p_state = np.zeros((4, 4, 128, 128), np.float32)
    s_ckv = np.zeros((16, 32, 128), np.float32)
    s_kpe = np.zeros((16, 32, 64), np.float32)
    s_state = np.zeros((16, 4, 128, 128), np.float32)
    for c in range(8):
        b, r = c // 2, c % 2
        o = R[c]
        y_prompt[b, r * 2048:(r + 1) * 2048] = o["y_own"]
        p_ckv[b, 16 + r * 2048:16 + (r + 1) * 2048] = o["ckv_own"]
        p_kpe[b, 16 + r * 2048:16 + (r + 1) * 2048] = o["kpe_own"]
        if r == 0:
            p_ckv[b, 0:16] = o["ckv_meta"]
            p_kpe[b, 0:16] = o["kpe_meta"]
        else:
            p_state[b] = o["state_p"]
        y_sample[2 * c:2 * c + 2] = o["y_samp"].reshape(2, 32, D)
        s_ckv[2 * c:2 * c + 2] = o["ckv_samp"].reshape(2, 32, 128)
        s_kpe[2 * c:2 * c + 2] = o["kpe_samp"].reshape(2, 32, 64)
        s_state[2 * c:2 * c + 2] = o["state_s"]
    return (y_prompt, y_sample, p_ckv, p_kpe, p_state, s_ckv, s_kpe, s_state)
```

```python
import contextlib
import os
import numpy as np
import concourse.bass as bass
import concourse.mybir as mybir
from concourse.bass_utils import run_bass_kernel_spmd

F32 = mybir.dt.float32
BF16 = mybir.dt.bfloat16
AF = mybir.ActivationFunctionType
ALU = mybir.AluOpType

COMPUTE = ("pe", "act", "dve", "pool")

D = 1024
DFF = 2816
NCH = 22
ALPHA = 2.0 ** 0.25
LN_EPS = 1e-5
RMS_EPS = 1e-6
NT = 34
NSLOT = 33
AUGW = 132
SCALE = 192.0 ** -0.5
NS = 3
STAGE = int(os.environ.get("KSTAGE", "99"))
DBG = bool(int(os.environ.get("KDBG", "0")))


class _Stop(Exception):
    pass


class _Op:
    __slots__ = ("eng", "fn", "deps", "is_dma", "grp", "need_inc", "val")


class Sched:
    def __init__(self, nc):
        self.nc = nc
        self.ops = []
        self.bufs = {}
        self.eng = {"pe": nc.tensor, "act": nc.scalar, "dve": nc.vector,
                    "pool": nc.gpsimd, "sp": nc.sync}
        self.cur_grp = {}

    def _deps_for(self, op, r, w):
        deps = []
        for k in r:
            st = self.bufs.get(k)
            if st is None:
                st = self.bufs[k] = [None, {}]
            if st[0] is not None:
                deps.append(st[0])
            if isinstance(k, tuple) and k[0] == "pb":
                for ek, ro in st[1].items():
                    if ek != op.eng:
                        deps.append(ro)
        for k in w:
            st = self.bufs.get(k)
            if st is None:
                st = self.bufs[k] = [None, {}]
            if st[0] is not None:
                deps.append(st[0])
            deps.extend(st[1].values())
        for k in r:
            st = self.bufs[k]
            key = ("dma", id(op)) if op.is_dma else op.eng
            st[1][key] = op
        for k in w:
            st = self.bufs[k]
            st[0] = op
            st[1] = {}
        return deps

    def op(self, eng, fn, r=(), w=()):
        o = _Op()
        o.eng = eng; o.fn = fn; o.is_dma = False; o.grp = None
        o.need_inc = False; o.val = None
        o.deps = self._deps_for(o, r, w)
        self.ops.append(o)
        return o

    def dma(self, q, fn, r=(), w=(), key=None, join=False):
        o = _Op()
        o.eng = q; o.fn = fn; o.is_dma = True
        o.need_inc = True; o.val = None
        if join and key in self.cur_grp:
            self.cur_grp[key].append(o)
        else:
            self.cur_grp[key] = [o]
        o.grp = (key, self.cur_grp[key])
        o.deps = self._deps_for(o, r, w)
        self.ops.append(o)
        return o

    def emit(self, stack, final_wait_eng="sp"):
        nc = self.nc
        ops = self.ops
        for o in ops:
            for d in o.deps:
                if not d.is_dma:
                    if d.eng == "pe" and o.eng == "pe" and not o.is_dma:
                        continue
                    d.need_inc = True
        cnt = {e: 0 for e in COMPUTE}
        dcnt = {}
        seen = set()
        for o in ops:
            if o.is_dma:
                key, members = o.grp
                if id(members) not in seen:
                    seen.add(id(members))
                    final = dcnt.get(key, 0) + 16 * len(members)
                    dcnt[key] = final
                    for m in members:
                        m.val = final
            elif o.need_inc:
                cnt[o.eng] += 1
                o.val = cnt[o.eng]
        sems = {}
        for e in COMPUTE:
            sems[e] = stack.enter_context(nc.semaphore("s_" + e))
        for i, key in enumerate(dcnt):
            sems[key] = stack.enter_context(nc.semaphore("d%d" % i))
        waited = {e: {} for e in self.eng}
        nwait = 0
        for o in ops:
            E = self.eng[o.eng]
            need = {}
            for d in o.deps:
                if d.is_dma:
                    if o.is_dma and d.grp[1] is o.grp[1]:
                        continue
                    sk = d.grp[0]
                else:
                    if d.eng == "pe" and o.eng == "pe" and not o.is_dma:
                        continue
                    sk = d.eng
                if need.get(sk, 0) < d.val:
                    need[sk] = d.val
            for sk, v in need.items():
                if waited[o.eng].get(sk, 0) < v:
                    E.wait_ge(sems[sk], v)
                    waited[o.eng][sk] = v
                    nwait += 1
            ins = o.fn()
            if o.is_dma:
                ins.then_inc(sems[o.grp[0]], 16)
            elif o.need_inc:
                ins.then_inc(sems[o.eng], 1)
        E = self.eng[final_wait_eng]
        for key, v in dcnt.items():
            if waited[final_wait_eng].get(key, 0) < v:
                E.wait_ge(sems[key], v)
        for e in COMPUTE:
            if cnt[e] > 0:
                E.wait_ge(sems[e], cnt[e])
        return dict(n_ops=len(ops), n_wait=nwait, cnt=cnt, n_dsem=len(dcnt))


def build_program():
    nc = bass.Bass("TRN2", target_bir_lowering=False)
    es = contextlib.ExitStack()
    S = Sched(nc)

    def din(name, shape, dt=F32):
        return nc.dram_tensor(name, shape, dt, kind="ExternalInput").ap()

    def dout(name, shape):
        return nc.dram_tensor(name, shape, F32, kind="ExternalOutput").ap()

    def dscr(name, shape, dt=BF16):
        return nc.dram_tensor(name, shape, dt, kind="Internal").ap()

    def sb(name, shape, dt=F32):
        return es.enter_context(nc.sbuf_tensor("sb_" + name, shape, dt))

    x_all = din("x_all", [NT, 128, D])
    tab_all = din("tab_all", [NT, 128, 384])
    cache_ckv = din("cache_ckv", [2, 1024, 128])
    cache_kpe = din("cache_kpe", [2, 1024, 64])
    state_in = din("state_in", [2, 4, 128, 128])
    w_in = [din("ffn1_w_in", [D, 2 * DFF]), din("ffn2_w_in", [D, 2 * DFF])]
    w_out = [din("ffn1_w_out", [DFF, D]), din("ffn2_w_out", [DFF, D])]
    w_mix_in = din("w_mix_in", [D, 2496])
    w_mix_out = din("w_mix_out", [D, D])
    w_uq = din("mla_w_uq", [256, 768])
    w_ukv = din("mla_w_ukv", [128, 1024])
    lnvec = din("lnvec", [6, D])
    gn_g = din("ret_gn_g", [512])
    gq_g = din("mla_q_norm_g", [256])
    gkv_g = din("mla_kv_norm_g", [128])
    consts = din("consts", [128, 7, 128])
    dect_in = din("dect", [128, 36])
    flg_in = din("flg", [128, 4])

    y_own = dout("y_own", [2048, D])
    y_samp = dout("y_samp", [64, D])
    ckv_own = dout("ckv_own", [2048, 128])
    kpe_own = dout("kpe_own", [2048, 64])
    ckv_meta = dout("ckv_meta", [16, 128])
    kpe_meta = dout("kpe_meta", [16, 64])
    ckv_samp = dout("ckv_samp", [64, 128])
    kpe_samp = dout("kpe_samp", [64, 64])
    state_p = dout("state_p", [4, 128, 128])
    state_s = dout("state_s", [2, 4, 128, 128])
    dbg = dout("dbg", [128, 4096]) if DBG else None

    winb = [dscr("winb%d" % f, [11, 128, 8, 2, 256]) for f in range(2)]
    woutb = [dscr("woutb%d" % f, [6, 128, 4, D]) for f in range(2)]
    wmib = dscr("wmib", [5, 128, 8, 512])
    wmob = dscr("wmob", [2, 128, 8, 512])

    cst = sb("cst", [128, 7, 128])
    ident_f = cst[:, 0, :]
    identb = sb("identb", [128, 128], BF16)
    dect = sb("dect", [128, 3, 3, 4])
    flg = sb("flg", [128, 4])
    lnv = sb("lnv", [128, 6, D])
    gng = sb("gng", [128, 512])
    gqg = sb("gqg", [128, 256])
    gkvg = sb("gkvg", [128, 128])
    wuq = sb("wuq", [128, 2, 768], BF16)
    wukv = sb("wukv", [128, 1024], BF16)
    wukT = sb("wukT", [128, 4, 128], BF16)
    ckvT = sb("ckvT", [128, NSLOT * 128], BF16)
    kpeT = sb("kpeT", [128, NSLOT * 128], BF16)
    aug = sb("aug", [128, NSLOT, AUGW], BF16)
    Sst = sb("Sst", [128, 4, 128])
    Sbf = sb("Sbf", [128, 4, 128], BF16)
    Smeta = sb("Smeta", [128, 4, 128])
    Ss = [sb("Ss%d" % s, [128, 4, 128]) for s in range(2)]
    Ssbf = [sb("Ssbf%d" % s, [128, 4, 128], BF16) for s in range(2)]
    Tt = sb("Tt", [128, 4, 128])
    ring = sb("ring", [128, NS, 4096], BF16)
    xT = sb("xT", [128, 8, 512], BF16)
    xres = sb("xres", [128, 4, D])
    ovl = sb("ovl", [128, 11264], BF16)
    hid = ovl[:, :].rearrange("p (c t) -> p c t", c=NCH)
    qd = ovl[:, 0:2048].rearrange("p (a b) -> p a b", a=4)
    kd = ovl[:, 2048:4096].rearrange("p (a b) -> p a b", a=4)
    vv = ovl[:, 4096:6144].rearrange("p (a b) -> p a b", a=4)
    sgr = ovl[:, 6144:8192].rearrange("p (a b) -> p a b", a=4)
    lat = ovl[:, 8192:8192 + 2 * 4 * 320].bitcast(F32).rearrange("p (a b) -> p a b", a=4)
    cqb = sb("cqb", [128, 4, 256])
    ymix = sb("ymix", [128, 4, D], BF16)
    cqnT = sb("cqnT", [128, 4, 2, 128], BF16)
    tabs = sb("tabs", [128, 4, 384])
    rbuf = sb("rbuf", [128, 2, D])
    ystage = sb("ystage", [128, 1, D])
    sgt = sb("sgt", [128, 2, 512])
    t1 = sb("t1", [128, 512])
    t2 = sb("t2", [128, 512])
    lst = sb("lst", [128, 2, 2, 6])
    lmv = sb("lmv", [128, 2, 4])
    sq = sb("sq", [128, 256])
    rms = sb("rms", [128, 8])
    ckvn = sb("ckvn", [128, 2, 128])
    kper = sb("kper", [128, 2, 128])
    cqn = sb("cqn", [128, 256])
    qT = sb("qT", [128, 4, 128], BF16)
    kT = sb("kT", [128, 4, 128], BF16)
    scm = sb("scm", [128, 4, 128], BF16)
    gst = sb("gst", [128, 4, 6])
    gmv = sb("gmv", [128, 4, 2])
    gsm = sb("gsm", [128, 12])
    on = sb("on", [128, 512])
    qnT = sb("qnT", [128, 4, 128], BF16)
    qabsT2 = sb("qabsT", [128, 2, 4, 128], BF16)
    qrr = sb("qrr", [128, 4, 64])
    qrT2 = sb("qrT", [128, 2, 4, 128], BF16)
    onesb = sb("onesb", [128, 128], BF16)
    rsum = sb("rsum", [128, 512])
    qTm = rsum[:].bitcast(BF16).rearrange("p (s h d) -> p s h d", s=2, h=4)
    pT = sb("pT", [128, 2, 512], BF16)
    olatT = sb("olatT", [128, 4, 128], BF16)
    ymf = ymix[:, :, :].rearrange("p a b -> p (a b)").bitcast(F32)
    ctmp = ymf[:, 0:1024].rearrange("p (a b) -> p a b", a=8)
    ktmp = ymf[:, 1024:2048].rearrange("p (a b) -> p a b", a=8)
    pb = [es.enter_context(nc.psum_tensor("pb%d" % i, [128, 512], F32)) for i in range(8)]

    def PB(i):
        return ("pb", i)

    YM = [("ymix", p) for p in range(4)]

    def OV(c):
        return ("ov", c)

    def Kq(pos):
        return OV(pos)

    def Kk(pos):
        return OV(4 + pos)

    def Kv(pos):
        return OV(8 + pos)

    def Kg(pos):
        return OV(12 + pos)

    def Kl(pos):
        return [OV(16 + (640 * pos) // 512), OV(16 + (640 * pos + 639) // 512)]

    misc_banks = [0, 1, 6, 7]
    misc_ctr = [0]

    def nb():
        b = misc_banks[misc_ctr[0] % len(misc_banks)]
        misc_ctr[0] += 1
        return b

    evac_ctr = [0]

    def evac(out_ap, in_ap, r, w):
        evac_ctr[0] += 1
        if evac_ctr[0] % 2:
            S.op("act", lambda: nc.scalar.copy(out_ap, in_ap), r=r, w=w)
        else:
            S.op("dve", lambda: nc.vector.tensor_copy(out_ap, in_ap), r=r, w=w)

    S.dma("sp", lambda: nc.sync.dma_start(out=cst[:], in_=consts), w=["cst"], key="c0")
    S.dma("sp", lambda: nc.sync.dma_start(out=dect[:].rearrange("p a b c -> p (a b c)"), in_=dect_in), w=["dect"], key="c0", join=True)
    S.dma("sp", lambda: nc.sync.dma_start(out=flg[:], in_=flg_in), w=["flg"], key="c0", join=True)
    S.dma("sp", lambda: nc.sync.dma_start(out=gng[:], in_=gn_g.partition_broadcast(128)), w=["gng"], key="c0", join=True)
    S.dma("sp", lambda: nc.sync.dma_start(out=gqg[:], in_=gq_g.partition_broadcast(128)), w=["gqg"], key="c0", join=True)
    S.dma("sp", lambda: nc.sync.dma_start(out=gkvg[:], in_=gkv_g.partition_broadcast(128)), w=["gkvg"], key="c0", join=True)
    for i in range(6):
        S.dma("sp", lambda i=i: nc.sync.dma_start(out=lnv[:, i, :], in_=lnvec[i].partition_broadcast(128)), w=["lnv"], key="c0", join=True)

    if STAGE >= 1:
        for hq in range(11):
            for gu in range(2):
                src = w_in[0][:, gu * DFF + hq * 256: gu * DFF + (hq + 1) * 256].rearrange("(k p) n -> p k n", p=128)
                dst = winb[0][hq, :, :, gu, :]
                S.dma("pool", lambda src=src, dst=dst: nc.gpsimd.dma_start(out=dst, in_=src),
                      w=[("winb", 0, hq)], key=("pc_win", hq), join=(gu == 1))
        S.dma("pool", lambda: nc.gpsimd.dma_start(out=wuq[:], in_=w_uq.rearrange("(k p) n -> p k n", p=128)), w=["wuq"], key="pc_small")
        S.dma("pool", lambda: nc.gpsimd.dma_start(out=wukv[:], in_=w_ukv), w=["wukv"], key="pc_small", join=True)
        S.dma("sp", lambda: nc.sync.dma_start(out=ctmp[:].rearrange("p a b -> p (a b)"), in_=w_ukv), w=YM, key="c1")
        for pc in range(6):
            ncc = 4 if pc < 5 else 2
            src = w_out[0][pc * 512: pc * 512 + ncc * 128, :].rearrange("(c p) n -> p c n", p=128)
            dst = woutb[0][pc, :, 0:ncc, :]
            S.dma("pool", lambda src=src, dst=dst: nc.gpsimd.dma_start(out=dst, in_=src),
                  w=[("woutb", 0, pc)], key=("pc_wout", pc))
        for g in range(5):
            n = 512 if g < 4 else 448
            src = w_mix_in[:, g * 512: g * 512 + n].rearrange("(k p) n -> p k n", p=128)
            dst = wmib[g, :, :, 0:n]
            S.dma("pool", lambda src=src, dst=dst: nc.gpsimd.dma_start(out=dst, in_=src), w=[("wmib", g)], key=("pc_wmi", g))
        for hf in range(2):
            src = w_mix_out[:, hf * 512:(hf + 1) * 512].rearrange("(k p) n -> p k n", p=128)
            dst = wmob[hf]
            S.dma("pool", lambda src=src, dst=dst: nc.gpsimd.dma_start(out=dst, in_=src), w=[("wmob", hf)], key="pc_wmo", join=(hf == 1))
        first = True
        for hq in range(11):
            for gu in range(2):
                src = w_in[1][:, gu * DFF + hq * 256: gu * DFF + (hq + 1) * 256].rearrange("(k p) n -> p k n", p=128)
                dst = winb[1][hq, :, :, gu, :]
                S.dma("pool", lambda src=src, dst=dst: nc.gpsimd.dma_start(out=dst, in_=src),
                      w=[("winb", 1, hq)], key="pc2", join=not first)
                first = False
        for pc in range(6):
            ncc = 4 if pc < 5 else 2
            src = w_out[1][pc * 512: pc * 512 + ncc * 128, :].rearrange("(c p) n -> p c n", p=128)
            dst = woutb[1][pc, :, 0:ncc, :]
            S.dma("pool", lambda src=src, dst=dst: nc.gpsimd.dma_start(out=dst, in_=src),
                  w=[("woutb", 1, pc)], key="pc2", join=True)

    if STAGE >= 2:
        S.op("dve", lambda: nc.vector.tensor_copy(identb[:], cst[:, 0, :]), r=["cst"], w=["identb"])
        S.op("pool", lambda: nc.gpsimd.memset(aug[:, :, 128:129], 1.0), w=[("aug", s) for s in range(NSLOT)])
        S.op("pool", lambda: nc.gpsimd.memset(Sst[:], 0.0), w=["S"])
        S.op("pool", lambda: nc.gpsimd.memset(qrT2[:], 0.0), w=[("qrT", 0), ("qrT", 1)])
        S.op("pool", lambda: nc.gpsimd.memset(onesb[:], 1.0), w=["onesb"])
        b = nb()
        for h in range(4):
            S.op("pe", lambda h=h, b=b: nc.tensor.transpose(pb[b][:, h * 128:(h + 1) * 128], ctmp[:, 2 * h, :], ident_f),
                 r=YM + ["cst"], w=[PB(b)])
        evac(wukT[:], pb[b][:].rearrange("p (a b) -> p a b", a=4), r=[PB(b)], w=["wukT"])

    blocks = [("b0", [0, 1])] + [("ctx", list(range(2 + 4 * i, 6 + 4 * i))) for i in range(4)] + \
             [("own", list(range(18 + 4 * i, 22 + 4 * i))) for i in range(4)]
    plan = []
    for kind, tiles in blocks:
        nrep = 1 if len(tiles) <= 2 else 2
        plan += [("win", 0, hq) for hq in range(11)] + [("wout", 0, pc) for pc in range(6)] * nrep
        plan += [("wmi", g) for g in ([0, 1, 2, 3, 4] if kind != "ctx" else [1, 2, 4])]
        if kind != "ctx":
            plan += [("wmo", 0), ("wmo", 1)]
            plan += [("win", 1, hq) for hq in range(11)] + [("wout", 1, pc) for pc in range(6)] * nrep
    ws_state = dict(next_load=0, next_use=0)

    def ws_issue(i):
        p = plan[i]
        slot = i % NS
        dst = ring[:, slot, 0:4096]
        if p[0] == "win":
            src = winb[p[1]][p[2]].rearrange("p k g n -> p (k g n)")
            key = ("winb", p[1], p[2])
        elif p[0] == "wout":
            ncc = 4 if p[2] < 5 else 2
            src = woutb[p[1]][p[2]][:, 0:ncc, :].rearrange("p c n -> p (c n)")
            dst = ring[:, slot, 0:ncc * 1024]
            key = ("woutb", p[1], p[2])
        elif p[0] == "wmi":
            n = 512 if p[1] < 4 else 448
            src = wmib[p[1]][:, :, 0:n]
            dst = ring[:, slot, :].rearrange("p (k n) -> p k n", k=8)[:, :, 0:n]
            key = ("wmib", p[1])
        else:
            src = wmob[p[1]].rearrange("p k n -> p (k n)")
            key = ("wmob", p[1])
        S.dma("sp", lambda: nc.sync.dma_start(out=dst, in_=src), r=[key], w=[("ring", slot)], key=("ring", slot))

    def ws_next(expect, hold=0):
        i = ws_state["next_use"]
        assert plan[i] == expect, (plan[i], expect)
        while ws_state["next_load"] < min(len(plan), i + NS - hold):
            ws_issue(ws_state["next_load"])
            ws_state["next_load"] += 1
        ws_state["next_use"] += 1
        return i % NS

    def make_xT(npos, positions=None):
        for pos in (range(npos) if positions is None else positions):
            for hb in range(2):
                bk = hb
                for kk in range(4):
                    k = hb * 4 + kk
                    S.op("pe", lambda pos=pos, k=k, kk=kk, bk=bk: nc.tensor.transpose(
                        pb[bk][:, kk * 128:(kk + 1) * 128], xres[:, pos, k * 128:(k + 1) * 128], ident_f),
                        r=[("xres", pos), "cst"], w=[PB(bk)])
                evac(xT[:, hb * 4:hb * 4 + 4, pos * 128:(pos + 1) * 128], pb[bk][:].rearrange("p (a b) -> p a b", a=4),
                     r=[PB(bk)], w=[("xT", pos)])

    ln_ctr = [0]

    def layer_norm(rb, gi, dst_ap, dst_keys):
        i = ln_ctr[0] % 2
        ln_ctr[0] += 1
        R = ("rbuf", rb)
        for hf in range(2):
            S.op("dve", lambda hf=hf: nc.vector.bn_stats(lst[:, i, hf, :], rbuf[:, rb, hf * 512:(hf + 1) * 512]), r=[R], w=[("lst", i)])
        S.op("dve", lambda: nc.vector.bn_aggr(lmv[:, i, 0:2], lst[:, i, :, :].rearrange("p a b -> p (a b)")), r=[("lst", i)], w=[("lmv", i)])
        S.op("act", lambda: nc.scalar.activation(lmv[:, i, 2:3], lmv[:, i, 1:2], AF.Ln, bias=LN_EPS / (ALPHA * ALPHA), scale=1.0),
             r=[("lmv", i)], w=[("lmv", i)])
        S.op("act", lambda: nc.scalar.activation(lmv[:, i, 2:3], lmv[:, i, 2:3], AF.Exp, scale=-0.5), r=[("lmv", i)], w=[("lmv", i)])
        S.op("dve", lambda: nc.vector.tensor_scalar(lmv[:, i, 3:4], lmv[:, i, 0:1], lmv[:, i, 2:3], -1.0, ALU.mult, ALU.mult),
             r=[("lmv", i)], w=[("lmv", i)])
        S.op("act", lambda: nc.scalar.activation(rbuf[:, rb, :], rbuf[:, rb, :], AF.Identity, bias=lmv[:, i, 3:4], scale=lmv[:, i, 2:3]),
             r=[R, ("lmv", i)], w=[R])
        S.op("pool", lambda: nc.gpsimd.tensor_tensor(rbuf[:, rb, :], rbuf[:, rb, :], lnv[:, gi, :], ALU.mult), r=[R, "lnv"], w=[R])
        S.op("pool", lambda: nc.gpsimd.tensor_tensor(dst_ap, rbuf[:, rb, :], lnv[:, gi + 1, :], ALU.add), r=[R, "lnv"], w=dst_keys)

    def rms_norm(src_ap, n, g_ap, dst_ap, r, w, col):
        S.op("act", lambda: nc.scalar.activation(sq[:, 0:n], src_ap, AF.Square, accum_out=rms[:, col:col + 1]), r=r, w=["sq", ("rms", col)])
        S.op("act", lambda: nc.scalar.activation(rms[:, col:col + 1], rms[:, col:col + 1], AF.Ln, bias=RMS_EPS, scale=1.0 / n),
             r=[("rms", col)], w=[("rms", col)])
        S.op("act", lambda: nc.scalar.activation(rms[:, col:col + 1], rms[:, col:col + 1], AF.Exp, scale=-0.5), r=[("rms", col)], w=[("rms", col)])
        S.op("dve", lambda: nc.vector.scalar_tensor_tensor(dst_ap, src_ap, rms[:, col:col + 1], g_ap, ALU.mult, ALU.mult),
             r=list(r) + [("rms", col)], w=w)

    rope_ctr = [0]

    def rope_from(src3, H, Dh, cc_ap, ss_ap, out3, r, w, dec_ap=None):
        hh = Dh // 2
        n = H * Dh
        rope_ctr[0] += 1
        if rope_ctr[0] % 2:
            b1, b2, k1, k2a, k2b = t1, t2, "t1", "t2a", "t2b"
        else:
            b1, b2, k1, k2a, k2b = sgt[:, 0, :], sgt[:, 1, :], ("sgt", 0), ("sgt", 1), ("sgt", 1)
        a1 = b1[:, 0:n].rearrange("p (h d) -> p h d", h=H)
        a2 = b2[:, 0:n].rearrange("p (h d) -> p h d", h=H)
        ccb = cc_ap.unsqueeze(1).broadcast_to([128, H, Dh])
        S.op("dve", lambda: nc.vector.tensor_tensor(a1, src3, ccb, ALU.mult), r=r, w=[k1])
        S.op("dve", lambda: nc.vector.tensor_tensor(a2[:, :, 0:hh], src3[:, :, hh:Dh], ss_ap[:, 0:hh].unsqueeze(1).broadcast_to([128, H, hh]), ALU.mult),
             r=r, w=[k2a])
        S.op("dve", lambda: nc.vector.tensor_tensor(a2[:, :, hh:Dh], src3[:, :, 0:hh], ss_ap[:, hh:Dh].unsqueeze(1).broadcast_to([128, H, hh]), ALU.mult),
             r=r, w=[k2b])
        if dec_ap is None:
            S.op("pool", lambda: nc.gpsimd.tensor_tensor(out3, a1, a2, ALU.add), r=[k1, k2a, k2b], w=w)
        else:
            S.op("pool", lambda: nc.gpsimd.tensor_tensor(a1, a1, a2, ALU.add), r=[k2a, k2b], w=[k1])
            S.op("pool", lambda: nc.gpsimd.tensor_tensor(out3, a1, dec_ap.unsqueeze(2).broadcast_to([128, H, Dh]), ALU.mult), r=[k1, "dect"], w=w)

    def ffn(f, npos, dsts):
        T = npos * 128
        make_xT(npos)
        xkeys = [("xT", p) for p in range(npos)]
        hkeys = ["ovl"]
        for hq in range(11):
            slot = ws_next(("win", f, hq))
            wv = ring[:, slot, :].rearrange("p (k g n) -> p k g n", k=8, g=2)
            for cc in range(2):
                c = 2 * hq + cc
                idx = c % 2
                for gu in range(2):
                    bk = 2 * gu + idx
                    for k in range(8):
                        S.op("pe", lambda bk=bk, k=k, gu=gu, cc=cc, wv=wv: nc.tensor.matmul(
                            pb[bk][:, 0:T], wv[:, k, gu, cc * 128:(cc + 1) * 128], xT[:, k, 0:T], start=(k == 0), stop=(k == 7)),
                            r=[("ring", slot)] + xkeys, w=[PB(bk)])
                S.op("act", lambda idx=idx: nc.scalar.activation(sgt[:, idx, 0:T], pb[idx][:, 0:T], AF.Silu), r=[PB(idx)], w=[("sgt", idx)])
                S.op("dve", lambda idx=idx, c=c: nc.vector.tensor_tensor(hid[:, c, 0:T], pb[2 + idx][:, 0:T], sgt[:, idx, 0:T], ALU.mult),
                     r=[PB(2 + idx), ("sgt", idx)], w=[OV(c)])
        groups = [list(range(npos))] if npos <= 2 else [[0, 1], [2, 3]]
        for grp in groups:
            for pc in range(6):
                slot = ws_next(("wout", f, pc))
                ncc = 4 if pc < 5 else 2
                wv = ring[:, slot, :].rearrange("p (c n) -> p c n", c=4)
                for pos in grp:
                    for hf in range(2):
                        bk = pos * 2 + hf
                        for cc in range(ncc):
                            c = pc * 4 + cc
                            S.op("pe", lambda bk=bk, c=c, cc=cc, pos=pos, hf=hf, wv=wv: nc.tensor.matmul(
                                pb[bk][:, :], hid[:, c, pos * 128:(pos + 1) * 128], wv[:, cc, hf * 512:(hf + 1) * 512],
                                start=(c == 0), stop=(c == NCH - 1)),
                                r=[("ring", slot), OV(c)], w=[PB(bk)])
            for pos in grp:
                rb = pos % 2
                for hf in range(2):
                    bk = pos * 2 + hf
                    S.op("dve", lambda bk=bk, hf=hf, pos=pos, rb=rb: nc.vector.scalar_tensor_tensor(
                        rbuf[:, rb, hf * 512:(hf + 1) * 512], pb[bk][:, :], 0.5 / ALPHA, xres[:, pos, hf * 512:(hf + 1) * 512], ALU.mult, ALU.add),
                        r=[PB(bk), ("xres", pos)], w=[("rbuf", rb)])
                dst_ap, dst_keys, after = dsts[pos]
                layer_norm(rb, 0 if f == 0 else 4, dst_ap, dst_keys)
                if after is not None:
                    after()

    def tile_info(t):
        if t == 0:
            return dict(kind="meta", slot=0, typ=1, K=16)
        if t == 1:
            return dict(kind="samp", slot=17, typ=2, K=64)
        if t < 18:
            return dict(kind="ctx", slot=1 + (t - 2), typ=0, K=128, j=t - 2)
        return dict(kind="own", slot=17 + (t - 18), typ=0, K=128, j=t - 18)

    out_ctr = [0]

    def front_latents(t, pos, info):
        slot = info["slot"]
        i = out_ctr[0] % 2
        out_ctr[0] += 1
        L = Kl(pos)
        rms_norm(lat[:, pos, 0:128], 128, gkvg[:], ckvn[:, i, :], r=L + ["gkvg"], w=[("ckvn", i)], col=i)
        rope_from(lat[:, pos, 128:192].unsqueeze(1), 1, 64, tabs[:, pos, 256:320], tabs[:, pos, 320:384],
                  kper[:, i, 0:64].unsqueeze(1), r=L + [("tabs", pos)], w=[("kper", i)])
        S.op("pool", lambda: nc.gpsimd.tensor_copy(kper[:, i, 64:128], kper[:, i, 0:64]), r=[("kper", i)], w=[("kper", i)])
        kind = info["kind"]
        if kind == "own":
            j = info["j"]
            S.dma("pool", lambda: nc.gpsimd.dma_start(out=ckv_own[j * 128:(j + 1) * 128, :], in_=ckvn[:, i, :]), r=[("ckvn", i)], key=("o_lat", i))
            S.dma("pool", lambda: nc.gpsimd.dma_start(out=kpe_own[j * 128:(j + 1) * 128, :], in_=kper[:, i, 0:64]), r=[("kper", i)], key=("o_lat", i), join=True)
        elif kind == "meta":
            S.dma("pool", lambda: nc.gpsimd.dma_start(out=ckv_meta, in_=ckvn[0:16, i, :]), r=[("ckvn", i)], key=("o_lat", i))
            S.dma("pool", lambda: nc.gpsimd.dma_start(out=kpe_meta, in_=kper[0:16, i, 0:64]), r=[("kper", i)], key=("o_lat", i), join=True)
        elif kind == "samp":
            S.dma("pool", lambda: nc.gpsimd.dma_start(out=ckv_samp, in_=ckvn[0:64, i, :]), r=[("ckvn", i)], key=("o_lat", i))
            S.dma("pool", lambda: nc.gpsimd.dma_start(out=kpe_samp, in_=kper[0:64, i, 0:64]), r=[("kper", i)], key=("o_lat", i), join=True)
        S.op("act", lambda: nc.scalar.copy(aug[:, slot, 0:128], ckvn[:, i, :]), r=[("ckvn", i)], w=[("aug", slot)])
        b = nb()
        S.op("pe", lambda: nc.tensor.transpose(pb[b][:, 0:128], ckvn[:, i, :], ident_f), r=[("ckvn", i), "cst"], w=[PB(b)])
        S.op("pe", lambda: nc.tensor.transpose(pb[b][:, 128:256], kper[:, i, :], ident_f), r=[("kper", i), "cst"], w=[PB(b)])
        S.op("dve", lambda: nc.vector.tensor_copy(ckvT[:, slot * 128:(slot + 1) * 128], pb[b][:, 0:128]), r=[PB(b)], w=[("ckvT", slot)])
        S.op("act", lambda: nc.scalar.copy(kpeT[:, slot * 128:(slot + 1) * 128], pb[b][:, 128:256]), r=[PB(b)], w=[("kpeT", slot)])
        if kind in ("own", "samp"):
            rms_norm(cqb[:, pos, :], 256, gqg[:], cqn[:], r=[("cqb", pos), "gqg"], w=["cqn"], col=2 + i)
            b2 = nb()
            for kc in range(2):
                S.op("pe", lambda kc=kc: nc.tensor.transpose(pb[b2][:, kc * 128:(kc + 1) * 128], cqn[:, kc * 128:(kc + 1) * 128], ident_f),
                     r=["cqn", "cst"], w=[PB(b2)])
            evac(cqnT[:, pos, :, :], pb[b2][:, 0:256].rearrange("p (a b) -> p a b", a=2), r=[PB(b2)], w=[("cqnT", pos)])

    def mix_in_block(kind, tiles):
        npos = len(tiles)
        groups = [0, 1, 2, 3, 4] if kind != "ctx" else [1, 2, 4]
        for t_i, t in enumerate(tiles):
            S.dma("sp", lambda t=t, t_i=t_i: nc.sync.dma_start(out=tabs[:, t_i, :], in_=tab_all[t]), w=[("tabs", t_i)], key=("tabs", t_i))
        mb = [4, 5, 6, 7]
        ctr = 0
        slots = {}
        slots[groups[0]] = ws_next(("wmi", groups[0]))
        slots[groups[1]] = ws_next(("wmi", groups[1]), hold=1)
        if npos == 4:
            work = [("T", [0, 1]), ("M", groups[0], [0, 1]), ("M", groups[1], [0, 1]), ("T", [2, 3]), ("M", groups[0], [2, 3]), ("M", groups[1], [2, 3])]
        else:
            work = [("T", list(range(npos))), ("M", groups[0], list(range(npos))), ("M", groups[1], list(range(npos)))]
        for g in groups[2:]:
            work.append(("M", g, list(range(npos))))
        for item in work:
            if item[0] == "T":
                make_xT(npos, positions=item[1])
                continue
            g = item[1]
            if g not in slots:
                slots[g] = ws_next(("wmi", g))
            slot = slots[g]
            n = 512 if g < 4 else 448
            wv = ring[:, slot, :].rearrange("p (k n) -> p k n", k=8)
            for pos in item[2]:
                t = tiles[pos]
                info = tile_info(t)
                bk = mb[ctr % 4]
                ctr += 1
                for k in range(8):
                    S.op("pe", lambda bk=bk, k=k, pos=pos, n=n, wv=wv: nc.tensor.matmul(
                        pb[bk][:, 0:n], xT[:, k, pos * 128:(pos + 1) * 128], wv[:, k, 0:n], start=(k == 0), stop=(k == 7)),
                        r=[("ring", slot), ("xT", pos)], w=[PB(bk)])
                src3 = pb[bk][:, :].rearrange("p (h d) -> p h d", h=4)
                typ = info["typ"]
                if g == 0:
                    rope_from(src3, 4, 128, tabs[:, pos, 0:128], tabs[:, pos, 128:256],
                              qd[:, pos, :].rearrange("p (h d) -> p h d", h=4), r=[PB(bk), ("tabs", pos)], w=[Kq(pos)], dec_ap=dect[:, typ, 0, :])
                elif g == 1:
                    rope_from(src3, 4, 128, tabs[:, pos, 0:128], tabs[:, pos, 128:256],
                              kd[:, pos, :].rearrange("p (h d) -> p h d", h=4), r=[PB(bk), ("tabs", pos)], w=[Kk(pos)], dec_ap=dect[:, typ, 1, :])
                elif g == 2:
                    S.op("act", lambda bk=bk, pos=pos: nc.scalar.copy(vv[:, pos, :], pb[bk][:, :]), r=[PB(bk)], w=[Kv(pos)])
                elif g == 3:
                    S.op("act", lambda bk=bk, pos=pos: nc.scalar.activation(sgr[:, pos, :], pb[bk][:, :], AF.Silu), r=[PB(bk)], w=[Kg(pos)])
                else:
                    S.op("act", lambda bk=bk, pos=pos: nc.scalar.copy(cqb[:, pos, :], pb[bk][:, 0:256]), r=[PB(bk)], w=[("cqb", pos)])
                    S.op("dve", lambda bk=bk, pos=pos: nc.vector.tensor_copy(lat[:, pos, 0:192], pb[bk][:, 256:448]), r=[PB(bk)], w=Kl(pos))
        for pos, t in enumerate(tiles):
            front_latents(t, pos, tile_info(t))

    def state_update(pos, K, typ, S_t, Sbf_t, skey, base=0):
        b = nb()
        kv = pb[b][:, :].rearrange("p (h d) -> p h d", h=4)
        for h in range(4):
            S.op("pe", lambda h=h: nc.tensor.matmul(kv[:, h, :], kd[base:base + K, pos, h * 128:(h + 1) * 128],
                                                    vv[base:base + K, pos, h * 128:(h + 1) * 128], start=True, stop=True),
                 r=[Kk(pos), Kv(pos)], w=[PB(b)])
        S.op("dve", lambda: nc.vector.tensor_tensor(Tt[:], kv, S_t[:], ALU.add), r=[PB(b), skey], w=["Tt"])
        S.op("pool", lambda: nc.gpsimd.tensor_tensor(S_t[:], Tt[:], dect[:, typ, 2, :].unsqueeze(2).broadcast_to([128, 4, 128]), ALU.mult),
             r=["Tt", "dect"], w=[skey])
        S.op("act", lambda: nc.scalar.copy(Sbf_t[:], S_t[:]), r=[skey], w=[skey + "bf"])

    def retention_tile(pos, info):
        samp = info["kind"] == "samp"
        for src, dstT, nm, kf in ((qd, qT, "qd", Kq), (kd, kT, "kd", Kk)):
            b = nb()
            pbb = pb[b][:].bitcast(BF16)
            for h in range(4):
                S.op("pe", lambda h=h, src=src, pbb=pbb: nc.tensor.transpose(pbb[:, h * 128:(h + 1) * 128], src[:, pos, h * 128:(h + 1) * 128], identb[:]),
                     r=[kf(pos), "identb"], w=[PB(b)])
            evac(dstT[:], pbb[:, 0:512].rearrange("p (a b) -> p a b", a=4), r=[PB(b)], w=[nm + "T"])
            yield
        b = nb()
        sc = pb[b][:, :].rearrange("p (h d) -> p h d", h=4)
        for h in range(4):
            S.op("pe", lambda h=h: nc.tensor.matmul(sc[:, h, :], kT[:, h, :], qT[:, h, :], start=True, stop=True), r=["kdT", "qdT"], w=[PB(b)])
        mi = 3 if samp else 1
        S.op("dve", lambda: nc.vector.tensor_tensor(scm[:], sc, cst[:, mi, :].unsqueeze(1).broadcast_to([128, 4, 128]), ALU.mult),
             r=[PB(b), "cst"], w=["scm"])
        yield
        if samp:
            for s in range(2):
                S.op("pool", lambda s=s: nc.gpsimd.tensor_tensor(qTm[:, s, :, :], qT[:], cst[:, 5 + s, :].unsqueeze(1).broadcast_to([128, 4, 128]), ALU.mult),
                     r=["qdT", "cst"], w=["rsum"])
        bo = nb()
        o3 = pb[bo][:, :].rearrange("p (h d) -> p h d", h=4)
        for h in range(4):
            S.op("pe", lambda h=h: nc.tensor.matmul(o3[:, h, :], scm[:, h, :], vv[:, pos, h * 128:(h + 1) * 128], start=True, stop=False),
                 r=["scm", Kv(pos)], w=[PB(bo)])
            if samp:
                for s in range(2):
                    S.op("pe", lambda h=h, s=s: nc.tensor.matmul(o3[:, h, :], qTm[:, s, h, :], Ssbf[s][:, h, :], start=False, stop=(s == 1)),
                         r=["rsum", "Ss%dbf" % s], w=[PB(bo)])
            else:
                S.op("pe", lambda h=h: nc.tensor.matmul(o3[:, h, :], qT[:, h, :], Sbf[:, h, :], start=False, stop=True),
                     r=["qdT", "Sbf"], w=[PB(bo)])
        yield
        if samp:
            for s in range(2):
                state_update(pos, 32, 2, Ss[s], Ssbf[s], "Ss%d" % s, base=32 * s)
        else:
            state_update(pos, 128, 0, Sst, Sbf, "S")
        yield
        for h in range(4):
            S.op("dve", lambda h=h: nc.vector.bn_stats(gst[:, h, :], o3[:, h, :]), r=[PB(bo)], w=[("gst", h)])
            S.op("dve", lambda h=h: nc.vector.bn_aggr(gmv[:, h, :], gst[:, h, :]), r=[("gst", h)], w=["gmv"])
        S.op("act", lambda: nc.scalar.activation(gsm[:, 0:4], gmv[:, :, 1], AF.Ln, bias=LN_EPS, scale=1.0), r=["gmv"], w=["gsm"])
        S.op("act", lambda: nc.scalar.activation(gsm[:, 4:8], gsm[:, 0:4], AF.Exp, scale=-0.5), r=["gsm"], w=["gsm"])
        yield
        for h in range(4):
            S.op("dve", lambda h=h: nc.vector.tensor_scalar(on[:, h * 128:(h + 1) * 128], o3[:, h, :], gmv[:, h, 0:1], gsm[:, 4 + h:5 + h],
                                                            ALU.subtract, ALU.mult), r=[PB(bo), "gsm", "gmv"], w=["on"])
        S.op("pool", lambda: nc.gpsimd.tensor_tensor(on[:], on[:], gng[:], ALU.mult), r=["on", "gng"], w=["on"])
        S.op("pool", lambda: nc.gpsimd.tensor_tensor(ymix[:, pos, 0:512], on[:], sgr[:, pos, :], ALU.mult), r=["on", Kg(pos)], w=[("ymix", pos)])
        yield

    def mla_pre(pos, info, par):
        qabsT = qabsT2[:, par]
        qrT = qrT2[:, par]
        b = nb()
        q3 = pb[b][:, :].rearrange("p (h d) -> p h d", h=4)
        for h in range(4):
            for kc in range(2):
                S.op("pe", lambda h=h, kc=kc: nc.tensor.matmul(q3[:, h, :], wuq[:, kc, h * 192:h * 192 + 128], cqnT[:, pos, kc, :],
                                                               start=(kc == 0), stop=(kc == 1)), r=["wuq", ("cqnT", pos)], w=[PB(b)])
        S.op("act", lambda: nc.scalar.copy(qnT[:], q3), r=[PB(b)], w=["qnT"])
        yield
        b2 = nb()
        qa3 = pb[b2][:, :].rearrange("p (h d) -> p h d", h=4)
        for h in range(4):
            S.op("pe", lambda h=h: nc.tensor.matmul(qa3[:, h, :], wukT[:, h, :], qnT[:, h, :], start=True, stop=True), r=["wukT", "qnT"], w=[PB(b2)])
        S.op("dve", lambda: nc.vector.tensor_copy(qabsT, qa3), r=[PB(b2)], w=[("qabsT", par)])
        yield
        b3 = nb()
        qr3 = pb[b3][:, 0:256].rearrange("p (h d) -> p h d", h=4)
        wr = wuq[:, :, :].rearrange("p k (h d) -> p k h d", h=4)
        for kc in range(2):
            S.op("pe", lambda kc=kc: nc.tensor.matmul(qr3, cqnT[:, pos, kc, :], wr[:, kc, :, 128:192], start=(kc == 0), stop=(kc == 1)),
                 r=["wuq", ("cqnT", pos)], w=[PB(b3)])
        rope_from(qr3, 4, 64, tabs[:, pos, 256:320], tabs[:, pos, 320:384], qrr[:], r=[PB(b3), ("tabs", pos)], w=["qrr"])
        yield
        yield
        yield
        yield
        b4 = nb()
        for pr in range(2):
            S.op("pe", lambda pr=pr: nc.tensor.transpose(pb[b4][:, pr * 128:(pr + 1) * 128], qrr[:, 2 * pr:2 * pr + 2, :].rearrange("p a b -> p (a b)"), ident_f),
                 r=["qrr", "cst"], w=[PB(b4)])
        qr4 = qrT.rearrange("p (a b) d -> p a b d", b=2)
        S.op("dve", lambda: nc.vector.tensor_copy(qr4[0:64, :, 0, :], pb[b4][0:64, 0:256].rearrange("p (a d) -> p a d", a=2)), r=[PB(b4)], w=[("qrT", par)])
        S.op("act", lambda: nc.scalar.copy(qr4[64:128, :, 1, :], pb[b4][64:128, 0:256].rearrange("p (a d) -> p a d", a=2)), r=[PB(b4)], w=[("qrT", par)])
        yield

    def mla_attn(pos, info, keylist, par, side=None):
        qabsT = qabsT2[:, par]
        qrT = qrT2[:, par]
        qa_flat = qabsT.rearrange("p a b -> p (a b)")
        qr_flat = qrT.rearrange("p a b -> p (a b)")
        nk_tiles = len(keylist)

        def emit_qk(ji):
            slot = keylist[ji][0]
            sb_i = 2 + ji % 2
            S.op("pe", lambda: nc.tensor.matmul(pb[sb_i][:, :], ckvT[:, slot * 128:(slot + 1) * 128], qa_flat, start=True, stop=False),
                 r=[("ckvT", slot), ("qabsT", par)], w=[PB(sb_i)])
            S.op("pe", lambda: nc.tensor.matmul(pb[sb_i][:, :], kpeT[:, slot * 128:(slot + 1) * 128], qr_flat, start=False, stop=True),
                 r=[("kpeT", slot), ("qrT", par)], w=[PB(sb_i)])

        emit_qk(0)
        for ji, (slot, biascol, mask) in enumerate(keylist):
            sb_i = 2 + ji % 2
            pi = ji % 2
            if ji + 1 < nk_tiles:
                emit_qk(ji + 1)
            if biascol is None:
                S.op("act", lambda sb_i=sb_i, pi=pi: nc.scalar.activation(pT[:, pi, :], pb[sb_i][:, :], AF.Exp, scale=SCALE),
                     r=[PB(sb_i)], w=[("pT", pi)])
            else:
                S.op("act", lambda sb_i=sb_i, pi=pi, biascol=biascol: nc.scalar.activation(pT[:, pi, :], pb[sb_i][:, :], AF.Exp, scale=SCALE,
                                                                                           bias=flg[:, biascol:biascol + 1]),
                     r=[PB(sb_i), "flg"], w=[("pT", pi)])
            if mask is not None:
                S.op("pool", lambda pi=pi, mask=mask: nc.gpsimd.tensor_tensor(
                    pT[:, pi, :].rearrange("p (h d) -> p h d", h=4), pT[:, pi, :].rearrange("p (h d) -> p h d", h=4),
                    cst[:, mask, :].unsqueeze(1).broadcast_to([128, 4, 128]), ALU.mult), r=[("pT", pi), "cst"], w=[("pT", pi)])
            S.op("pe", lambda pi=pi, slot=slot, ji=ji: nc.tensor.matmul(pb[4][:, :], aug[:, slot, 0:128], pT[:, pi, :],
                                                                       start=(ji == 0), stop=(ji == nk_tiles - 1)),
                 r=[("pT", pi), ("aug", slot)], w=[PB(4)])
            S.op("pe", lambda pi=pi, ji=ji: nc.tensor.matmul(pb[5][:, :], onesb[:], pT[:, pi, :],
                                                            start=(ji == 0), stop=(ji == nk_tiles - 1)),
                 r=[("pT", pi), "onesb"], w=[PB(5)])
            if side is not None:
                next(side, None)
        S.op("dve", lambda: nc.vector.reciprocal(rsum[:], pb[5][:, :]), r=[PB(5)], w=["rsum"])
        S.op("dve", lambda: nc.vector.tensor_tensor(olatT[:].rearrange("p a b -> p (a b)"), pb[4][:, :], rsum[:], ALU.mult), r=[PB(4), "rsum"], w=["olatT"])
        b6 = nb()
        my3 = pb[b6][:, :].rearrange("p (h d) -> p h d", h=4)
        for h in range(4):
            S.op("pe", lambda h=h: nc.tensor.matmul(my3[:, h, :], olatT[:, h, :], wukv[:, h * 256 + 128:h * 256 + 256], start=True, stop=True),
                 r=["olatT", "wukv"], w=[PB(b6)])
        S.op("act", lambda: nc.scalar.copy(ymix[:, pos, 512:1024], pb[b6][:, :]), r=[PB(b6)], w=[("ymix", pos)])

    def mix_out_block(tiles):
        npos = len(tiles)
        for pos in range(npos):
            b = nb()
            pbb = pb[b][:].bitcast(BF16)
            for k in range(8):
                S.op("pe", lambda k=k, pbb=pbb, pos=pos: nc.tensor.transpose(pbb[:, k * 128:(k + 1) * 128], ymix[:, pos, k * 128:(k + 1) * 128], identb[:]),
                     r=[("ymix", pos), "identb"], w=[PB(b)])
            evac(xT[:, :, pos * 128:(pos + 1) * 128], pbb[:, :].rearrange("p (a b) -> p a b", a=8), r=[PB(b)], w=[("xT", pos)])
        slots = [ws_next(("wmo", 0)), ws_next(("wmo", 1), hold=1)]
        for pos in range(npos):
            rb = pos % 2
            for hf in range(2):
                bk = nb()
                wv = ring[:, slots[hf], :].rearrange("p (k n) -> p k n", k=8)
                for k in range(8):
                    S.op("pe", lambda bk=bk, k=k, pos=pos, wv=wv: nc.tensor.matmul(pb[bk][:, :], xT[:, k, pos * 128:(pos + 1) * 128], wv[:, k, :],
                                                                                start=(k == 0), stop=(k == 7)),
                         r=[("ring", slots[hf]), ("xT", pos)], w=[PB(bk)])
                S.op("dve", lambda bk=bk, hf=hf, pos=pos, rb=rb: nc.vector.scalar_tensor_tensor(
                    rbuf[:, rb, hf * 512:(hf + 1) * 512], pb[bk][:, :], 1.0 / ALPHA, xres[:, pos, hf * 512:(hf + 1) * 512], ALU.mult, ALU.add),
                    r=[PB(bk), ("xres", pos)], w=[("rbuf", rb)])
            layer_norm(rb, 2, xres[:, pos, :], [("xres", pos)])

    def load_cache():
        for s in range(2):
            S.dma("sp", lambda s=s: nc.sync.dma_start(out=ctmp[:], in_=cache_ckv[s].rearrange("(i p) d -> p i d", p=128)), w=YM, key="cache")
            for dup in range(2):
                S.dma("sp", lambda s=s, dup=dup: nc.sync.dma_start(out=ktmp[:, :, dup * 64:(dup + 1) * 64], in_=cache_kpe[s].rearrange("(i p) d -> p i d", p=128)), w=YM, key="cache", join=True)
            S.op("act", lambda s=s: nc.scalar.copy(aug[:, 1 + 8 * s:9 + 8 * s, 0:128], ctmp[:]), r=YM, w=[("aug", 1 + 8 * s + i) for i in range(8)])
            for hb in range(2):
                b = nb()
                for i in range(4):
                    S.op("pe", lambda i=i, hb=hb, b=b: nc.tensor.transpose(pb[b][:, i * 128:(i + 1) * 128], ctmp[:, hb * 4 + i, :], ident_f),
                         r=YM + ["cst"], w=[PB(b)])
                c0 = (1 + 8 * s + 4 * hb) * 128
                evac(ckvT[:, c0:c0 + 512], pb[b][:, :], r=[PB(b)], w=[("ckvT", 1 + 8 * s + 4 * hb + i) for i in range(4)])
                b = nb()
                for i in range(4):
                    S.op("pe", lambda i=i, hb=hb, b=b: nc.tensor.transpose(pb[b][:, i * 128:(i + 1) * 128], ktmp[:, hb * 4 + i, :], ident_f),
                         r=YM + ["cst"], w=[PB(b)])
                evac(kpeT[:, c0:c0 + 512], pb[b][:, :], r=[PB(b)], w=[("kpeT", 1 + 8 * s + 4 * hb + i) for i in range(4)])
            S.dma("sp", lambda s=s: nc.sync.dma_start(out=Ss[s][:], in_=state_in[s].rearrange("h p d -> p h d")), w=["Ss%d" % s], key="cache2", join=(s == 1))
        for s in range(2):
            S.op("act", lambda s=s: nc.scalar.copy(Ssbf[s][:], Ss[s][:]), r=["Ss%d" % s], w=["Ss%dbf" % s])

    ycnt = [0]
    try:
        for bi, (kind, tiles) in enumerate(blocks):
            npos = len(tiles)
            if STAGE < 3:
                raise _Stop()
            if kind == "ctx" and STAGE < 8:
                raise _Stop()
            if kind == "own" and STAGE < 9:
                raise _Stop()
            if STAGE < 99 and bi >= 6 + max(0, STAGE - 9):
                raise _Stop()
            for pos, t in enumerate(tiles):
                S.dma("sp", lambda pos=pos, t=t: nc.sync.dma_start(out=xres[:, pos, :], in_=x_all[t]), w=[("xres", pos)], key=("xl", pos))
            ffn(0, npos, [(xres[:, pos, :], [("xres", pos)], None) for pos in range(npos)])
            if kind == "b0":
                if DBG:
                    S.dma("pool", lambda: nc.gpsimd.dma_start(out=dbg[:, 0:1024], in_=xres[:, 1, :]), r=[("xres", 1)], key="dbg0")
                if STAGE < 4:
                    raise _Stop()
                load_cache()
            mix_in_block(kind, tiles)
            if kind == "b0":
                if STAGE < 5:
                    raise _Stop()
                state_update(0, 16, 1, Sst, Sbf, "S")
                S.op("dve", lambda: nc.vector.tensor_copy(Smeta[:], Sst[:]), r=["S"], w=["Smeta"])
                info = tile_info(1)
                for _ in retention_tile(1, info):
                    pass
                if STAGE < 6:
                    raise _Stop()
                keylist = [(0, 2, None)] + [(1 + i, None, 5) for i in range(8)] + [(9 + i, None, 6) for i in range(8)] + [(17, None, 4)]
                for _ in mla_pre(1, info, 0):
                    pass
                mla_attn(1, info, keylist, 0)
                if DBG:
                    S.dma("pool", lambda: nc.gpsimd.dma_start(out=dbg[:, 1024:2048], in_=ymix[:, 1, :]), r=[("ymix", 1)], key="dbg1")
                if STAGE < 7:
                    raise _Stop()
                S.op("pool", lambda: nc.gpsimd.memset(ymix[:, 0, :], 0.0), w=[("ymix", 0)])
                mix_out_block(tiles)
                if DBG:
                    S.dma("pool", lambda: nc.gpsimd.dma_start(out=dbg[:, 2048:3072], in_=xres[:, 1, :]), r=[("xres", 1)], key="dbg2")
                for s in range(2):
                    S.dma("pool", lambda s=s: nc.gpsimd.dma_start(out=state_s[s].rearrange("h p d -> p h d"), in_=Ss[s][:]), r=["Ss%d" % s], key="o_state", join=(s == 1))

                def after_s(i):
                    def f():
                        S.dma("pool", lambda: nc.gpsimd.dma_start(out=y_samp, in_=ystage[0:64, i, :]), r=[("ystage", i)], key=("o_y", i))
                    return f
                dsts = []
                for pos in range(npos):
                    i = 0
                    ycnt[0] += 1
                    dsts.append((ystage[:, i, :], [("ystage", i)], after_s(i) if pos == 1 else None))
                ffn(1, npos, dsts)
            elif kind == "ctx":
                for pos, t in enumerate(tiles):
                    state_update(pos, 128, 0, Sst, Sbf, "S")
                if tiles[-1] == 17:
                    S.op("dve", lambda: nc.vector.tensor_tensor(Tt[:], Sst[:], Smeta[:], ALU.subtract), r=["S", "Smeta"], w=["Tt"])
                    S.op("dve", lambda: nc.vector.scalar_tensor_tensor(Sst[:].rearrange("p a b -> p (a b)"), Tt[:].rearrange("p a b -> p (a b)"), flg[:, 0:1],
                                                                       Smeta[:].rearrange("p a b -> p (a b)"), ALU.mult, ALU.add), r=["Tt", "Smeta", "flg"], w=["S"])
                    S.op("act", lambda: nc.scalar.copy(Sbf[:], Sst[:]), r=["S"], w=["Sbf"])
            else:
                def pre_gen(pos, t, par):
                    info = tile_info(t)
                    yield from retention_tile(pos, info)
                    yield from mla_pre(pos, info, par)

                def drain(g):
                    if g is not None:
                        for _ in g:
                            pass
                drain(pre_gen(0, tiles[0], 0))
                for pos, t in enumerate(tiles):
                    info = tile_info(t)
                    j = info["j"]
                    keylist = [(0, 2, None)] + [(1 + i, 1, None) for i in range(16)] + [(17 + i, None, None) for i in range(j)] + \
                              [(17 + j, None, 2)]
                    side = pre_gen(pos + 1, tiles[pos + 1], (pos + 1) % 2) if pos + 1 < len(tiles) else None
                    mla_attn(pos, info, keylist, pos % 2, side)
                    drain(side)
                mix_out_block(tiles)

                def after_o(i, j):
                    def f():
                        S.dma("pool", lambda: nc.gpsimd.dma_start(out=y_own[j * 128:(j + 1) * 128, :], in_=ystage[:, i, :]), r=[("ystage", i)], key=("o_y", i))
                    return f
                dsts = []
                for pos, t in enumerate(tiles):
                    i = 0
                    ycnt[0] += 1
                    dsts.append((ystage[:, i, :], [("ystage", i)], after_o(i, tile_info(t)["j"])))
                ffn(1, npos, dsts)
        S.dma("pool", lambda: nc.gpsimd.dma_start(out=state_p.rearrange("h p d -> p h d"), in_=Sst[:]), r=["S"], key="o_state2")

    except _Stop:
        pass
    stats = S.emit(es)
    es.close()
    return nc, stats


def _rope_tab(pos, d):
    inv = 10000.0 ** (-np.arange(0, d, 2, dtype=np.float64) / d)
    ang = pos.astype(np.float64)[:, None] * inv[None, :]
    c, s = np.cos(ang), np.sin(ang)
    return np.concatenate([c, c], 1), np.concatenate([-s, s], 1)


def _host_constants(r):
    pos = np.zeros((NT, 128), np.float64)
    pos[0] = np.arange(128)
    pos[1] = 16 + 1024 + (np.arange(128) % 32)
    for j in range(16):
        pos[2 + j] = 16 + j * 128 + np.arange(128)
        pos[18 + j] = 16 + r * 2048 + j * 128 + np.arange(128)
    tab = np.zeros((NT, 128, 384), np.float32)
    for t in range(NT):
        ccr, ssr = _rope_tab(pos[t], 128)
        ccm, ssm = _rope_tab(pos[t], 64)
        tab[t, :, 0:128] = ccr; tab[t, :, 128:256] = ssr
        tab[t, :, 256:320] = ccm; tab[t, :, 320:384] = ssm
    k = np.arange(128)[:, None]; q = np.arange(128)[None, :]
    consts = np.zeros((128, 7, 128), np.float32)
    consts[:, 0] = np.eye(128)
    consts[:, 1] = (q >= k)
    consts[:, 2] = 1.0 - ((k >= 64) & (q < 64))
    same = (k // 32 == q // 32) & (k < 64) & (q < 64)
    consts[:, 3] = same & (q >= k)
    consts[:, 4] = same
    consts[:, 5] = np.broadcast_to(q < 32, (128, 128))
    consts[:, 6] = np.broadcast_to((q >= 32) & (q < 64), (128, 128))
    gam = 1.0 - 2.0 ** (-5.0 - np.arange(4, dtype=np.float64))
    dect = np.zeros((128, 3, 3, 4), np.float64)
    row = np.arange(128, dtype=np.float64)
    for typ, (idx, C) in enumerate(((row, 128), (row, 16), (row % 32, 32))):
        dect[:, typ, 0, :] = gam[None, :] ** (idx[:, None] + 1.0)
        dect[:, typ, 1, :] = (128.0 ** -0.5) * gam[None, :] ** (-idx[:, None] - 1.0)
        dect[:, typ, 2, :] = gam[None, :] ** C
    flg = np.zeros((128, 4), np.float32)
    flg[16:, 2] = -30000.0
    flg[:, 0] = float(r)
    flg[:, 1] = 0.0 if r == 1 else -30000.0
    return tab, consts, dect.reshape(128, 36).astype(np.float32), flg


_CACHE = {}


def kernel(x_prompt, x_sample, cache_mla_ckv, cache_mla_kpe, state_ret, meta_tokens,
           ffn1_w_in, ffn1_w_out, ln1_g, ln1_b, w_mix_in, ret_gn_g, mla_q_norm_g, mla_w_uq,
           mla_kv_norm_g, mla_w_ukv, w_mix_out, ln2_g, ln2_b, ffn2_w_in, ffn2_w_out, ln3_g, ln3_b):
    f32 = lambda a: np.ascontiguousarray(np.asarray(a, dtype=np.float32))
    x_prompt = f32(x_prompt); x_sample = f32(x_sample)
    if "nc" not in _CACHE:
        _CACHE["nc"] = build_program()
    nc, stats = _CACHE["nc"]
    lnvec = np.stack([f32(ln1_g), f32(ln1_b), f32(ln2_g), f32(ln2_b), f32(ln3_g), f32(ln3_b)], 0)
    in_maps = []
    for c in range(8):
        b, r = c // 2, c % 2
        x_all = np.zeros((NT, 128, D), np.float32)
        x_all[0, 0:16] = f32(meta_tokens)
        x_all[1, 0:32] = x_sample[2 * c]
        x_all[1, 32:64] = x_sample[2 * c + 1]
        x_all[2:18] = x_prompt[b, 0:2048].reshape(16, 128, D)
        x_all[18:34] = x_prompt[b, r * 2048:(r + 1) * 2048].reshape(16, 128, D)
        tab, consts, dect, flg = _host_constants(r)
        in_maps.append(dict(
            x_all=x_all, tab_all=tab,
            cache_ckv=f32(cache_mla_ckv[2 * c:2 * c + 2]), cache_kpe=f32(cache_mla_kpe[2 * c:2 * c + 2]),
            state_in=f32(state_ret[2 * c:2 * c + 2]),
            ffn1_w_in=f32(ffn1_w_in), ffn2_w_in=f32(ffn2_w_in), ffn1_w_out=f32(ffn1_w_out), ffn2_w_out=f32(ffn2_w_out),
            w_mix_in=f32(w_mix_in), w_mix_out=f32(w_mix_out), mla_w_uq=f32(mla_w_uq), mla_w_ukv=f32(mla_w_ukv),
            lnvec=lnvec, ret_gn_g=f32(ret_gn_g), mla_q_norm_g=f32(mla_q_norm_g), mla_kv_norm_g=f32(mla_kv_norm_g),
            consts=consts, dect=dect, flg=flg))
    res = run_bass_kernel_spmd(nc, in_maps, core_ids=list(range(8)))
    R = res.results
    _CACHE["last"] = R
    y_prompt = np.zeros((4, 4096, D), np.float32)
    y_sample = np.zeros((16, 32, D), np.float32)
    p_ckv = np.zeros((4, 4112, 128), np.float32)
    p_kpe = np.zeros((4, 4112, 64), np.float32)
    p_state = np.zeros((4, 4, 128, 128), np.float32)
    s_ckv = np.zeros((16, 32, 128), np.float32)
    s_kpe = np.zeros((16, 32, 64), np.float32)
    s_state = np.zeros((16, 4, 128, 128), np.float32)
    for c in range(8):
        b, r = c // 2, c % 2
        o = R[c]
        y_prompt[b, r * 2048:(r + 1) * 2048] = o["y_own"]
        p_ckv[b, 16 + r * 2048:16 + (r + 1) * 2048] = o["ckv_own"]
        p_kpe[b, 16 + r * 2048:16 + (r + 1) * 2048] = o["kpe_own"]
        if r == 0:
            p_ckv[b, 0:16] = o["ckv_meta"]
            p_kpe[b, 0:16] = o["kpe_meta"]
        else:
            p_state[b] = o["state_p"]
        y_sample[2 * c:2 * c + 2] = o["y_samp"].reshape(2, 32, D)
        s_ckv[2 * c:2 * c + 2] = o["ckv_samp"].reshape(2, 32, 128)
        s_kpe[2 * c:2 * c + 2] = o["kpe_samp"].reshape(2, 32, 64)
        s_state[2 * c:2 * c + 2] = o["state_s"]
    return (y_prompt, y_sample, p_ckv, p_kpe, p_state, s_ckv, s_kpe, s_state)
```
